# Optimizing a Trainium2 kernel written in Bass

```python
import jax, jax.numpy as jnp
from jax import lax
import numpy as np

D_MODEL = 2048
BATCH = 8
SEQ = 2048
DEPTH = 1

MLA_HEADS = 8
MLA_Q_RANK = 512
MLA_KV_RANK = 512
MLA_NOPE_DIM = 128
MLA_ROPE_DIM = 64
MLA_V_DIM = 128
MLA_WIDTH = MLA_HEADS * MLA_V_DIM
MLA_IN = MLA_Q_RANK + MLA_KV_RANK + MLA_ROPE_DIM
ROPE_THETA = 10000.0
Q_BLOCK = 128

RWKV_HEAD = 64
RWKV_HEADS = 16
RWKV_WIDTH = RWKV_HEADS * RWKV_HEAD
DECAY_RANK = 96
ICLR_RANK = 96
RWKV_SIZES = [RWKV_WIDTH, RWKV_WIDTH, RWKV_WIDTH, DECAY_RANK, DECAY_RANK, ICLR_RANK, ICLR_RANK]
RWKV_IN = sum(RWKV_SIZES)
GN_EPS = 64e-5
NORM_EPS = 1e-6

IN_SIZES = [MLA_IN, RWKV_IN, MLA_WIDTH, RWKV_WIDTH, D_MODEL, D_MODEL]
D_IN = sum(IN_SIZES)

kernel_name = "hybrid_mla_rwkv7_gated_encoder_block"


def _offsets(sizes):
    return [int(o) for o in np.cumsum(sizes)[:-1]]


def rms_norm(x, g, eps=NORM_EPS):
    xf = x.astype(jnp.float32)
    y = xf * lax.rsqrt(jnp.mean(xf * xf, axis=-1, keepdims=True) + eps)
    return (y * g.astype(jnp.float32)).astype(x.dtype)


def apply_rotary(t, cos, sin):
    tf = t.astype(jnp.float32)
    t1, t2 = jnp.split(tf, 2, axis=-1)
    return jnp.concatenate([t1 * cos - t2 * sin, t1 * sin + t2 * cos], axis=-1).astype(t.dtype)


def centred_shift(p):
    pad = jnp.pad(p, ((0, 0), (1, 1), (0, 0)))
    return 0.5 * (pad[:, :-2] + pad[:, 2:])


def mla_attention(q_nope, q_rope, k_nope, k_rope, v):
    B, S, H, _ = q_nope.shape
    nb = S // Q_BLOCK
    scale = (MLA_NOPE_DIM + MLA_ROPE_DIM) ** -0.5
    qn = q_nope.reshape(B, nb, Q_BLOCK, H, MLA_NOPE_DIM).transpose(1, 0, 2, 3, 4)
    qr = q_rope.reshape(B, nb, Q_BLOCK, H, MLA_ROPE_DIM).transpose(1, 0, 2, 3, 4)

    def block(args):
        qn_b, qr_b = args
        s = (jnp.einsum('bqhd,bkhd->bhqk', qn_b, k_nope, preferred_element_type=jnp.float32)
             + jnp.einsum('bqhr,bkr->bhqk', qr_b, k_rope, preferred_element_type=jnp.float32))
        p = jax.nn.softmax(s * scale, axis=-1)
        return jnp.einsum('bhqk,bkhd->bqhd', p.astype(v.dtype), v)

    o = lax.map(block, (qn, qr))
    return o.transpose(1, 0, 2, 3, 4).reshape(B, S, H * MLA_V_DIM)


def rwkv7_bidir_scan(r, w_f, w_b, k_f, k_b, v, kk, a_f, a_b):
    B, S, H, N = r.shape

    def tm(fwd, bwd):
        return jnp.stack([fwd, bwd[:, ::-1]], axis=0).transpose(2, 0, 1, 3, 4)

    xs = (tm(r, r), tm(w_f, w_b), tm(k_f, k_b), tm(v, v), tm(-kk, -kk), tm(kk * a_f, kk * a_b))

    def step(st, inp):
        r_t, w_t, k_t, v_t, a_t, b_t = inp
        sa = jnp.einsum('dbhij,dbhj->dbhi', st, a_t)
        st = st * w_t[..., None, :] + sa[..., None] * b_t[..., None, :] + v_t[..., None] * k_t[..., None, :]
        y = jnp.einsum('dbhij,dbhj->dbhi', st, r_t)
        return st, y

    s0 = jnp.zeros((2, B, H, N, N), jnp.float32)
    _, ys = lax.scan(step, s0, xs)
    y = ys[:, 0] + ys[::-1, 1]
    return y.transpose(1, 0, 2, 3)


def setup_inputs(seed: int = 0) -> dict:
    key = jax.random.key(seed)
    ks = jax.random.split(key, 32)
    f32 = jnp.float32
    nrm = lambda k, shape, s: jax.random.normal(k, shape, f32) * s
    gain = lambda k, n: 1.0 + nrm(k, (n,), 0.02)
    return {
        "x": nrm(ks[0], (BATCH, SEQ, D_MODEL), 1.0),
        "g_pre": gain(ks[1], D_MODEL),
        "w_in": nrm(ks[2], (D_MODEL, D_IN), D_MODEL ** -0.5),
        "mla_q_norm": gain(ks[3], MLA_Q_RANK),
        "mla_wq_b": nrm(ks[4], (MLA_Q_RANK, MLA_HEADS * (MLA_NOPE_DIM + MLA_ROPE_DIM)), MLA_Q_RANK ** -0.5),
        "mla_kv_norm": gain(ks[5], MLA_KV_RANK),
        "mla_wkv_b": nrm(ks[6], (MLA_KV_RANK, MLA_HEADS * (MLA_NOPE_DIM + MLA_V_DIM)), MLA_KV_RANK ** -0.5),
        "rwkv_mu": jax.random.uniform(ks[7], (RWKV_IN,), f32),
        "rwkv_w0_f": nrm(ks[8], (RWKV_WIDTH,), 0.5),
        "rwkv_w2_f": nrm(ks[9], (DECAY_RANK, RWKV_WIDTH), 0.5 * DECAY_RANK ** -0.5),
        "rwkv_w0_b": nrm(ks[10], (RWKV_WIDTH,), 0.5),
        "rwkv_w2_b": nrm(ks[11], (DECAY_RANK, RWKV_WIDTH), 0.5 * DECAY_RANK ** -0.5),
        "rwkv_a0_f": nrm(ks[12], (RWKV_WIDTH,), 0.1),
        "rwkv_a2_f": nrm(ks[13], (ICLR_RANK, RWKV_WIDTH), 0.5 * ICLR_RANK ** -0.5),
        "rwkv_a0_b": nrm(ks[14], (RWKV_WIDTH,), 0.1),
        "rwkv_a2_b": nrm(ks[15], (ICLR_RANK, RWKV_WIDTH), 0.5 * ICLR_RANK ** -0.5),
        "rwkv_k_k": 0.85 + nrm(ks[16], (RWKV_WIDTH,), 0.02),
        "rwkv_k_a": gain(ks[17], RWKV_WIDTH),
        "rwkv_r_k": nrm(ks[18], (RWKV_HEADS, RWKV_HEAD), 0.1),
        "rwkv_gn_g": gain(ks[19], RWKV_WIDTH),
        "rwkv_gn_b": nrm(ks[20], (RWKV_WIDTH,), 0.01),
        "w_br_mla": nrm(ks[21], (MLA_WIDTH, D_MODEL), MLA_WIDTH ** -0.5),
        "w_br_rwkv": nrm(ks[22], (RWKV_WIDTH, D_MODEL), RWKV_WIDTH ** -0.5),
        "w_out": nrm(ks[23], (D_MODEL, D_MODEL), D_MODEL ** -0.5),
        "g_post": gain(ks[24], D_MODEL),
    }


def reference(x, g_pre, w_in, mla_q_norm, mla_wq_b, mla_kv_norm, mla_wkv_b, rwkv_mu,
              rwkv_w0_f, rwkv_w2_f, rwkv_w0_b, rwkv_w2_b, rwkv_a0_f, rwkv_a2_f, rwkv_a0_b,
              rwkv_a2_b, rwkv_k_k, rwkv_k_a, rwkv_r_k, rwkv_gn_g, rwkv_gn_b, w_br_mla,
              w_br_rwkv, w_out, g_post):
    f32 = jnp.float32
    B, S, _ = x.shape
    pos = jnp.arange(S, dtype=f32)
    inv_freq = jnp.power(ROPE_THETA, -jnp.arange(0, MLA_ROPE_DIM, 2, dtype=f32) / MLA_ROPE_DIM)
    ang = pos[:, None] * inv_freq[None, :]
    cos, sin = jnp.cos(ang), jnp.sin(ang)

    for _layer in range(DEPTH):
        h = rms_norm(x, g_pre)
        proj = h @ w_in
        mla_in, rwkv_in, z_mla, z_rwkv, gate_mla, gate_rwkv = jnp.split(proj, _offsets(IN_SIZES), axis=-1)

        q_a, kv_a, k_rope = jnp.split(mla_in, _offsets([MLA_Q_RANK, MLA_KV_RANK, MLA_ROPE_DIM]), axis=-1)
        q = (rms_norm(q_a, mla_q_norm) @ mla_wq_b).reshape(B, S, MLA_HEADS, MLA_NOPE_DIM + MLA_ROPE_DIM)
        kv = (rms_norm(kv_a, mla_kv_norm) @ mla_wkv_b).reshape(B, S, MLA_HEADS, MLA_NOPE_DIM + MLA_V_DIM)
        q_nope, q_rope = q[..., :MLA_NOPE_DIM], q[..., MLA_NOPE_DIM:]
        k_nope, v_mla = kv[..., :MLA_NOPE_DIM], kv[..., MLA_NOPE_DIM:]
        q_rope = apply_rotary(q_rope, cos[:, None, :], sin[:, None, :])
        k_rope = apply_rotary(k_rope, cos, sin)
        y_mla = mla_attention(q_nope, q_rope, k_nope, k_rope, v_mla)

        rin = rwkv_in.astype(f32)
        rin = rin + rwkv_mu * (centred_shift(rin) - rin)
        r, k, v, wd_f, wd_b, ad_f, ad_b = jnp.split(rin, _offsets(RWKV_SIZES), axis=-1)

        def decay(w0, wd, w2):
            z = w0 + jnp.tanh(wd) @ w2
            return jnp.exp(-jnp.exp(-jax.nn.softplus(-z) - 0.5))

        w_f = decay(rwkv_w0_f.astype(f32), wd_f, rwkv_w2_f.astype(f32))
        w_b = decay(rwkv_w0_b.astype(f32), wd_b, rwkv_w2_b.astype(f32))
        a_f = jax.nn.sigmoid(rwkv_a0_f.astype(f32) + ad_f @ rwkv_a2_f.astype(f32))
        a_b = jax.nn.sigmoid(rwkv_a0_b.astype(f32) + ad_b @ rwkv_a2_b.astype(f32))

        hd = lambda t: t.reshape(B, S, RWKV_HEADS, RWKV_HEAD)
        kk = hd(k * rwkv_k_k.astype(f32))
        kk = kk / jnp.maximum(jnp.linalg.norm(kk, axis=-1, keepdims=True), 1e-12)
        k_a = rwkv_k_a.astype(f32)
        k_f = k * (1.0 + (a_f - 1.0) * k_a)
        k_b = k * (1.0 + (a_b - 1.0) * k_a)
        r_h, v_h, k_fh, k_bh = hd(r), hd(v), hd(k_f), hd(k_b)
        y = rwkv7_bidir_scan(r_h, hd(w_f), hd(w_b), k_fh, k_bh, v_h, kk, hd(a_f), hd(a_b))
        mu = jnp.mean(y, axis=-1, keepdims=True)
        var = jnp.mean(jnp.square(y - mu), axis=-1, keepdims=True)
        yn = ((y - mu) * lax.rsqrt(var + GN_EPS)).reshape(B, S, RWKV_WIDTH)
        yn = yn * rwkv_gn_g.astype(f32) + rwkv_gn_b.astype(f32)
        bonus = jnp.sum(r_h * (k_fh + k_bh) * rwkv_r_k.astype(f32), axis=-1, keepdims=True) * v_h
        y_rwkv = (yn + bonus.reshape(B, S, RWKV_WIDTH)).astype(x.dtype)

        u_mla = (y_mla * jax.nn.silu(z_mla)) @ w_br_mla
        u_rwkv = (y_rwkv * jax.nn.silu(z_rwkv)) @ w_br_rwkv
        merged = jax.nn.sigmoid(gate_mla) * u_mla + jax.nn.sigmoid(gate_rwkv) * u_rwkv
        out = merged @ w_out
        x = (x + rms_norm(out, g_post)).astype(x.dtype)
    return x
```

```python
import contextlib
import math
import numpy as np
import concourse.bass as bass
import concourse.mybir as mybir
from concourse.bass_utils import run_bass_kernel_spmd

F32 = mybir.dt.float32
BF16 = mybir.dt.bfloat16
AF = mybir.ActivationFunctionType
ALU = mybir.AluOpType
AX = mybir.AxisListType

ENGS = ["tensor", "vector", "scalar", "gpsimd", "sync"]
EPOCH = 16000
NEPOCH = 6
NDMA_SEM = 24


def I(method, *a, **k):
    return lambda e: getattr(e, method)(*a, **k)


class Op:
    __slots__ = ("eng", "fn", "reads", "writes", "dma", "idx", "waits", "signal",
                 "seq", "dsem", "dval", "final", "bar")


class Sched:
    def __init__(self, nc, es):
        self.nc = nc
        self.ops = []
        self.esem = {(e, i): es.enter_context(nc.semaphore(f"s_{e}_{i}")) for e in ENGS for i in range(NEPOCH)}
        self.dsem = [es.enter_context(nc.semaphore(f"d_{i}")) for i in range(NDMA_SEM)]
        self.bsem = {e: es.enter_context(nc.semaphore(f"b_{e}")) for e in ENGS}
        self.cnt = {e: 0 for e in ENGS}
        self.duse = [0] * NDMA_SEM
        self.dn = 0
        self.nphase = 0
        self.barrier_fns = None
        self.final_ops = []
        self.alias = {}
        self.limit = None
        self.suffix = None
        self.shared = set()

    def kmap(self, k):
        k = self.alias.get(k, k)
        if self.suffix is None:
            return k
        b_ = k[0] if isinstance(k, tuple) else k
        if b_ in self.shared:
            return k
        return ("S", self.suffix, k)

    def add(self, eng, fn, reads=(), writes=(), dma=False, final=False):
        op = Op()
        if self.limit is not None and len(self.ops) >= self.limit:
            op.writes = ()
            return op
        op.eng = eng; op.fn = fn
        op.reads = tuple(self.kmap(k) for k in reads)
        op.writes = tuple(self.kmap(k) for k in writes)
        op.dma = dma; op.idx = len(self.ops); op.waits = []
        op.signal = False; op.seq = None; op.dsem = None; op.dval = None
        op.final = final
        op.bar = False
        self.ops.append(op)
        return op

    def pe(self, fn, reads=(), writes=()): return self.add("tensor", fn, reads, writes)
    def dve(self, fn, reads=(), writes=()): return self.add("vector", fn, reads, writes)
    def act(self, fn, reads=(), writes=()): return self.add("scalar", fn, reads, writes)
    def pool(self, fn, reads=(), writes=()): return self.add("gpsimd", fn, reads, writes)
    def dma(self, fn, reads=(), writes=(), final=False):
        return self.add("sync", fn, reads, writes, dma=True, final=final)

    def analyze(self):
        lw = {}; rd = {}
        ops = self.ops
        for op in ops:
            deps = set()
            for k in op.reads:
                w = lw.get(k)
                if w is not None:
                    deps.add(w)
            for k in op.writes:
                w = lw.get(k)
                if w is not None:
                    deps.add(w)
                for r in rd.get(k, ()):
                    deps.add(r)
            for k in op.reads:
                rd.setdefault(k, []).append(op.idx)
            for k in op.writes:
                lw[k] = op.idx
                rd[k] = []
            deps.discard(op.idx)
            best = {}; dmas = []
            for d in deps:
                p = ops[d]
                if p.dma:
                    dmas.append(d)
                    continue
                if p.eng == op.eng and not op.dma:
                    if op.eng == "tensor":
                        continue
                if p.eng not in best or best[p.eng] < d:
                    best[p.eng] = d
            if op.bar and op.eng == "tensor":
                for e2 in ("vector", "scalar", "gpsimd"):
                    lastop = [o2.idx for o2 in ops if o2.eng == e2 and not o2.bar and not o2.dma]
                    if lastop and (e2 not in best or best[e2] < lastop[-1]):
                        best[e2] = lastop[-1]
            op.waits = sorted(best.values()) + sorted(dmas)
            for d in op.waits:
                ops[d].signal = True

    def flush(self, last=False):
        nc = self.nc
        if self.limit is not None:
            print("phase", self.nphase, "nops", len(self.ops), flush=True)
        self.limit = None
        for e_ in ENGS:
            bop = self.add(e_, self.barrier_fns[e_], reads=["dmy" + e_],
                           writes=["bar" + e_] + (["ppv", "prk", "pYr", "pHr"] if e_ == "tensor" else []))
            bop.bar = True
        self.analyze()
        ops = self.ops
        for op in ops:
            if op.bar:
                continue
            if op.dma:
                s = self.dn % NDMA_SEM
                self.dn += 1
                self.duse[s] += 1
                op.dsem = s
                op.dval = 16 * self.duse[s]
            elif op.signal:
                self.cnt[op.eng] += 1
                op.seq = self.cnt[op.eng]
                assert op.seq <= EPOCH * NEPOCH
        phase = self.nphase
        self.nphase += 1
        dsem_end = list(self.duse)
        with nc.Block() as block:
            def make(engname):
                def body(e):
                    waited = {}
                    if phase > 0:
                        for pe_ in ENGS:
                            e.wait_ge(self.bsem[pe_], phase * (16 if pe_ == "sync" else 1))
                    for op in ops:
                        if op.eng != engname:
                            continue
                        for d in op.waits:
                            p = ops[d]
                            if p.dma:
                                key = ("d", p.dsem); val = p.dval; sem = self.dsem[p.dsem]
                            else:
                                ep = (p.seq - 1) // EPOCH
                                key = ("e", p.eng, ep); val = p.seq - ep * EPOCH
                                sem = self.esem[(p.eng, ep)]
                            if waited.get(key, 0) >= val:
                                continue
                            waited[key] = val
                            e.wait_ge(sem, val)
                        if op.bar:
                            if engname == "sync":
                                for s in range(NDMA_SEM):
                                    if dsem_end[s] > 0 and waited.get(("d", s), 0) < 16 * dsem_end[s]:
                                        e.wait_ge(self.dsem[s], 16 * dsem_end[s])
                                op.fn(e).then_inc(self.bsem["sync"], 16)
                                if last:
                                    e.wait_ge(self.bsem["sync"], 16 * (phase + 1))
                            else:
                                op.fn(e).then_inc(self.bsem[engname], 1)
                        elif op.dma:
                            sem = self.dsem[op.dsem]
                            key = ("d", op.dsem)
                            if op.dval > 16 and waited.get(key, 0) < op.dval - 16:
                                e.wait_ge(sem, op.dval - 16)
                                waited[key] = op.dval - 16
                            op.fn(e).then_inc(sem, 16)
                        else:
                            ins = op.fn(e)
                            if op.signal:
                                ep = (op.seq - 1) // EPOCH
                                ins.then_inc(self.esem[(op.eng, ep)], 1)
                return body
            for engname in ENGS:
                getattr(block, engname)(make(engname))
        self.ops = []
        self.alias = {}
        self.shared = set()


class Cfg:
    def __init__(s, S=2048, D=2048, HM=8, QR=512, KVR=512, HR=16):
        s.S = S; s.D = D; s.HM = HM; s.QR = QR; s.KVR = KVR; s.HR = HR
        s.MW = HM * 128; s.RW = HR * 64; s.NP = s.RW // 128
        s.MLA_IN = QR + KVR + 64; s.RWKV_IN = 3 * s.RW + 4 * 96
        s.D_IN = s.MLA_IN + s.RWKV_IN + s.MW + s.RW + 2 * D
        s.TG = min(512, S); s.NTG = S // s.TG; s.TT = S // 128; s.DC = D // 128
        s.QRC = QR // 128; s.KVRC = KVR // 128; s.MWC = s.MW // 128
        s.c_qa = 0; s.c_kva = QR; s.c_kr = QR + KVR
        s.c_r = s.MLA_IN; s.c_k = s.c_r + s.RW; s.c_v = s.c_k + s.RW; s.c_lora = s.c_v + s.RW
        s.c_zm = s.MLA_IN + s.RWKV_IN; s.c_zr = s.c_zm + s.MW
        s.c_gm = s.c_zr + s.RW; s.c_gr = s.c_gm + D
        s.CK = s.TG // 128


VEC_NAMES = ["g_pre", "mla_q_norm", "mla_kv_norm", "rwkv_mu", "rwkv_w0_f", "rwkv_w0_b", "rwkv_a0_f",
             "rwkv_a0_b", "rwkv_k_k", "rwkv_k_a", "rwkv_r_k", "rwkv_gn_g", "rwkv_gn_b", "g_post"]
MAT_SHAPES = lambda c: {
    "w_in": [c.D, c.D_IN], "mla_wq_b": [c.QR, c.HM * 192], "mla_wkv_b": [c.KVR, c.HM * 256],
    "rwkv_w2_f": [96, c.RW], "rwkv_w2_b": [96, c.RW], "rwkv_a2_f": [96, c.RW], "rwkv_a2_b": [96, c.RW],
    "w_br_mla": [c.MW, c.D], "w_br_rwkv": [c.RW, c.D], "w_out": [c.D, c.D]}
VEC_LENS = lambda c: {
    "g_pre": c.D, "mla_q_norm": c.QR, "mla_kv_norm": c.KVR, "rwkv_mu": c.RWKV_IN, "rwkv_w0_f": c.RW,
    "rwkv_w0_b": c.RW, "rwkv_a0_f": c.RW, "rwkv_a0_b": c.RW, "rwkv_k_k": c.RW, "rwkv_k_a": c.RW,
    "rwkv_r_k": c.RW, "rwkv_gn_g": c.RW, "rwkv_gn_b": c.RW, "g_post": c.D}

C0 = math.exp(-0.5)
DEBUG_LIMIT = None
NORM_EPS = 1e-6
GN_EPS = 64e-5


def bc_last(ap3, n):
    pat = [list(x) for x in ap3.ap]
    pat[-1] = [0, n]
    return bass.AP(ap3.tensor, ap3.offset, pat)


def build(c: Cfg, stop=None):
    nc = bass.Bass("TRN2", target_bir_lowering=False)
    S_, D, TG, NTG, TT, DC = c.S, c.D, c.TG, c.NTG, c.TT, c.DC
    dr = {}
    dr["x"] = nc.dram_tensor("x", [S_, D], F32, kind="ExternalInput").ap()
    for n, sh in MAT_SHAPES(c).items():
        dr[n] = nc.dram_tensor(n, sh, F32, kind="ExternalInput").ap()
    for n, ln in VEC_LENS(c).items():
        dr[n] = nc.dram_tensor(n, [ln], F32, kind="ExternalInput").ap()
    dr["cosT"] = nc.dram_tensor("cosT", [64, S_], F32, kind="ExternalInput").ap()
    dr["sinT"] = nc.dram_tensor("sinT", [64, S_], F32, kind="ExternalInput").ap()
    out = nc.dram_tensor("out", [S_, D], F32, kind="ExternalOutput").ap()

    def scr(name, shape, dt):
        return nc.dram_tensor(name, shape, dt, kind="Internal").ap()
    qTn = scr("qTn", [c.HM, 128, S_], BF16); qTr = scr("qTr", [c.HM, 64, S_], BF16)
    kTn = scr("kTn", [c.HM, 128, S_], BF16); krTd = scr("krTd", [64, S_], BF16)
    Vtm = scr("Vtm", [c.HM, S_, 128], BF16)
    ymz = scr("ymz", [c.MWC, 128, S_], BF16); yrz = scr("yrz", [c.NP, 128, S_], BF16)
    mrg = scr("mrg", [DC, 128, S_], BF16)
    rinT = scr("rinT", [3 * c.NP + 1, 128, S_], F32)
    kknD = scr("kknD", [c.NP, 128, S_], F32)
    lwT = scr("lwT", [4, 96, S_], BF16)
    szrD = scr("szrD", [c.NP, 128, S_], BF16)
    dummyD = scr("dummyD", [1, 16], F32)

    with contextlib.ExitStack() as es:
        S = Sched(nc, es)

        uniq = [0]

        def sb(name, shape, dt, stack=es):
            uniq[0] += 1
            return stack.enter_context(nc.sbuf_tensor(f"{name}_u{uniq[0]}", shape, dt))

        def ps(name, shape, dt, stack=es):
            return stack.enter_context(nc.psum_tensor(name, shape, dt))

        ident = sb("ident", [128, 128], BF16)
        identf = sb("identf", [128, 128], F32)
        ones128 = sb("ones128", [128, 128], BF16)
        ones32 = sb("ones32", [128, 128], F32)
        blk1 = sb("blk1", [128, 128], BF16)
        bdmask = sb("bdmask", [128, 128], F32)
        mU = sb("mU", [128, 128], BF16); mSU = sb("mSU", [128, 128], BF16)
        mL = sb("mL", [128, 128], BF16); mSL = sb("mSL", [128, 128], BF16)
        Mf = sb("Mf", [128, 2, 256], BF16); Mb = sb("Mb", [128, 2, 256], BF16)
        Nf = sb("Nf", [128, 2, 128], BF16); Nb = sb("Nb", [128, 2, 128], BF16)
        scanmask = sb("scanmask", [128, TG], F32)
        RT = sb("RT", [64, 64], F32); RTa = sb("RTa", [64, 64], F32)
        pvst = sb("pvst", [128, 128], F32)
        pv = sb("pv", [128, 128], F32); pvo = sb("pvo", [128, 128], F32); pvh = sb("pvh", [128, 128], F32)
        dmy = {e: sb(f"dmy_{e}", [128, 8], F32) for e in ENGS}
        BK = [ps(f"bank{i}", [128, 512], F32) for i in range(8)]
        pdm = BK[7][:, 504:512]
        S.barrier_fns = {
            "vector": I("memset", dmy["vector"][:, 0:1], 0.0),
            "gpsimd": I("memset", dmy["gpsimd"][:, 0:1], 0.0),
            "scalar": I("activation", out=dmy["scalar"][:, 0:1], in_=dmy["scalar"][:, 1:2], func=AF.Copy),
            "tensor": I("matmul", pdm[0:1, 0:1], lhsT=identf[0:1, 0:1], rhs=identf[0:1, 0:1], start=True, stop=True),
            "sync": I("dma_start", out=dummyD[0:1, 0:8], in_=dmy["sync"][0:1, 0:8]),
        }

        def tri(t, pat, cm, op, key):
            S.pool(I("memset", t[:], 1.0), writes=[key])
            S.pool(I("affine_select", out=t[:], in_=t[:], pattern=pat, compare_op=op, fill=0.0, base=0,
                     channel_multiplier=cm), reads=[key], writes=[key])
        for e_ in ENGS:
            S.pool(I("memset", dmy[e_][:], 0.0), writes=["dmy" + e_])
        tri(ident, [[-1, 128]], 1, ALU.is_equal, "ident")
        tri(identf, [[-1, 128]], 1, ALU.is_equal, "identf")
        tri(mU, [[1, 128]], -1, ALU.is_ge, "mU"); tri(mSU, [[1, 128]], -1, ALU.is_gt, "mSU")
        tri(mL, [[-1, 128]], 1, ALU.is_ge, "mL"); tri(mSL, [[-1, 128]], 1, ALU.is_gt, "mSL")
        S.pool(I("memset", ones128[:], 1.0), writes=["ones128"])
        S.pool(I("memset", ones32[:], 1.0), writes=["ones32"])
        for t, key in ((blk1, "blk1"), (bdmask, "bdmask")):
            S.pool(I("memset", t[:], 0.0), writes=[key])
            S.pool(I("memset", t[0:64, 0:64], 1.0), writes=[key])
            S.pool(I("memset", t[64:128, 64:128], 1.0), writes=[key])
        for h in range(2):
            S.pool(I("tensor_copy", out=Mf[:, h, 0:128], in_=mSU[:]), reads=["mSU"], writes=["Mf"])
            S.pool(I("tensor_copy", out=Mf[:, h, 128:256], in_=mU[:]), reads=["mU"], writes=["Mf"])
            S.pool(I("tensor_copy", out=Mb[:, h, 0:128], in_=mSL[:]), reads=["mSL"], writes=["Mb"])
            S.pool(I("tensor_copy", out=Mb[:, h, 128:256], in_=mL[:]), reads=["mL"], writes=["Mb"])
            S.pool(I("tensor_copy", out=Nf[:, h, :], in_=mSL[:]), reads=["mSL"], writes=["Nf"])
            S.pool(I("tensor_copy", out=Nb[:, h, :], in_=mSU[:]), reads=["mSU"], writes=["Nb"])
        S.pool(I("memset", scanmask[:], 1.0), writes=["scanmask"])
        for ck in range(c.CK):
            S.pool(I("memset", scanmask[:, ck * 128:ck * 128 + 1], 0.0), writes=["scanmask"])
        S.pool(I("memset", RTa[:], 1.0), writes=["RTa"])
        S.pool(I("affine_select", out=RTa[:], in_=RTa[:], pattern=[[-1, 64]], compare_op=ALU.is_equal, fill=0.0,
                 base=-32, channel_multiplier=1), reads=["RTa"], writes=["RTa"])
        S.pool(I("memset", RT[:], 1.0), writes=["RT"])
        S.pool(I("affine_select", out=RT[:], in_=RT[:], pattern=[[1, 64]], compare_op=ALU.is_equal, fill=0.0,
                 base=-32, channel_multiplier=-1), reads=["RT"], writes=["RT"])
        S.pool(I("tensor_tensor", out=RT[:], in0=RT[:], in1=RTa[:], op=ALU.subtract), reads=["RT", "RTa"], writes=["RT"])

        S.pool(I("memset", pvst[:], 0.0), writes=["pvst"])
        col = {}
        nrow = [0]

        def vec_rows(name, ap, n, L=128):
            col[name] = nrow[0]
            S.dma(I("dma_start", out=pvst[nrow[0]:nrow[0] + n, 0:L], in_=ap.rearrange("(c p) -> c p", p=L)),
                  reads=["pvst0"], writes=["pvst"])
            nrow[0] += n
        S.ops[-1].writes = ("pvst", "pvst0")
        vec_rows("g_pre", dr["g_pre"], DC)
        vec_rows("gq", dr["mla_q_norm"], c.QRC)
        vec_rows("gkv", dr["mla_kv_norm"], c.KVRC)
        vec_rows("mu", dr["rwkv_mu"][0:3 * c.RW], 3 * c.NP)
        vec_rows("mul", dr["rwkv_mu"][3 * c.RW:3 * c.RW + 384], 4, L=96)
        for nm in ["w0_f", "w0_b", "a0_f", "a0_b", "k_k", "k_a", "r_k", "gn_g", "gn_b"]:
            vec_rows(nm, dr["rwkv_" + nm], c.NP)
        assert nrow[0] <= 128
        ppv = BK[7][:, 0:128]
        S.pe(I("matmul", ppv[:], lhsT=pvst[:], rhs=identf[:], start=True, stop=True), reads=["pvst", "identf"], writes=["ppv"])
        S.dve(I("tensor_copy", out=pv[:], in_=ppv[:]), reads=["ppv"], writes=["pv"])
        S.dve(I("tensor_scalar", out=pvo[:], in0=pv[:], scalar1=-1.0, scalar2=1.0, op0=ALU.mult, op1=ALU.add), reads=["pv"], writes=["pvo"])
        S.dve(I("tensor_scalar", out=pvh[:], in0=pv[:], scalar1=0.5, scalar2=None, op0=ALU.mult), reads=["pv"], writes=["pvh"])
        S.flush()
        if stop is not None and S.nphase >= stop:
            return nc

        def pcol(name, i=0):
            return pv[:, col[name] + i:col[name] + i + 1]

        hstack = contextlib.ExitStack()
        es.enter_context(hstack)
        hT = sb("hT", [128, DC, S_], BF16, hstack)
        hts = {"hT": hT}
        hTd = nc.dram_tensor("hTd", [DC, 128, S_], BF16, kind="Internal").ap()
        stg = {}

        def alloc_stg():
            stg["stack"] = contextlib.ExitStack()
            es.enter_context(stg["stack"])
            stg["wst"] = [sb(f"wst{i}", [128, DC, 128], F32, stg["stack"]) for i in range(2)]
            stg["wbf"] = [sb(f"wbf{i}", [128, DC, 128], BF16, stg["stack"]) for i in range(2)]
        alloc_stg()
        pin = [BK[i][:, 0:TG] for i in range(2)]
        st = {"w": 0, "p": 0, "alt": 0}

        def alt_evac():
            st["alt"] ^= 1
            return st["alt"]

        def wload(col0, ncols):
            wst = stg["wst"]; wbf = stg["wbf"]
            sl = st["w"]; st["w"] ^= 1
            S.dma(I("dma_start", out=wst[sl][:, :, 0:ncols],
                    in_=dr["w_in"][:, col0:col0 + ncols].rearrange("(dc p) n -> p dc n", p=128)),
                  writes=[("wst", sl)])
            S.pool(I("tensor_copy", out=wbf[sl][:, :, 0:ncols], in_=wst[sl][:, :, 0:ncols]),
                   reads=[("wst", sl)], writes=[("wbf", sl)])
            st["pref"] = (col0, ncols, sl)

        def inproj(col0, ncols, consume, nxt=None):
            wbf = stg["wbf"]
            pf = st.get("pref")
            if pf is None or pf[0] != col0 or pf[1] != ncols:
                wload(col0, ncols)
                pf = st["pref"]
            sl = pf[2]
            st["pref"] = None
            if nxt is not None:
                wload(nxt[0], nxt[1])
            for tg in range(NTG):
                b = st["p"]; st["p"] ^= 1
                for dc in range(DC):
                    S.pe(I("matmul", pin[b][0:ncols, :], lhsT=wbf[sl][:, dc, 0:ncols],
                           rhs=hts["hT"][:, dc, tg * TG:(tg + 1) * TG], start=(dc == 0), stop=(dc == DC - 1)),
                         reads=[("wbf", sl), "hT"], writes=[("pin", b)])
                consume(tg, pin[b][0:ncols, :], ("pin", b))

        with contextlib.ExitStack() as ph:
            xt = [sb(f"xt{i}", [128, D], F32, ph) for i in range(2)]
            xs = [sb(f"xs{i}", [128, D], BF16, ph) for i in range(2)]
            junk = sb("junkA", [128, D], BF16, ph)
            sta = sb("sta", [128, TT, 4], F32, ph)
            pT = [BK[2 + i][:].bitcast(BF16)[:, 0:512].rearrange("p (j t) -> p j t", t=128) for i in range(2)]
            r_ = 0
            for tt in range(TT):
                sl = tt % 2
                S.dma(I("dma_start", out=xt[sl][:], in_=dr["x"][tt * 128:(tt + 1) * 128, :]), writes=[("xt", sl)])
                S.act(I("activation", out=junk[:], in_=xt[sl][:], func=AF.Square, accum_out=sta[:, tt, 0:1]),
                      reads=[("xt", sl)], writes=["junk", ("sta", tt)])
                S.dve(I("tensor_scalar", out=sta[:, tt, 1:2], in0=sta[:, tt, 0:1], scalar1=1.0 / D, scalar2=NORM_EPS,
                        op0=ALU.mult, op1=ALU.add), reads=[("sta", tt)], writes=[("sta", tt)])
                S.act(I("activation", out=sta[:, tt, 2:3], in_=sta[:, tt, 1:2], func=AF.Sqrt), reads=[("sta", tt)], writes=[("sta", tt)])
                S.dve(I("reciprocal", out=sta[:, tt, 3:4], in_=sta[:, tt, 2:3]), reads=[("sta", tt)], writes=[("sta", tt)])
                S.dve(I("tensor_scalar", out=xs[sl][:], in0=xt[sl][:], scalar1=sta[:, tt, 3:4], scalar2=None, op0=ALU.mult),
                      reads=[("xt", sl), ("sta", tt)], writes=[("xs", sl)])
                for g in range(0, DC, 4):
                    n = min(4, DC - g)
                    pb = r_ % 2; r_ += 1
                    for j in range(n):
                        S.pe(I("transpose", out=pT[pb][:, j, :], in_=xs[sl][:, (g + j) * 128:(g + j + 1) * 128], identity=ident[:]),
                             reads=[("xs", sl), "ident"], writes=[("pTA", pb)])
                    for j in range(n):
                        fn = I("tensor_scalar", out=hT[:, g + j, tt * 128:(tt + 1) * 128], in0=pT[pb][:, j, :],
                               scalar1=pcol("g_pre", g + j), scalar2=None, op0=ALU.mult)
                        (S.dve if (j % 2 == 0) else S.pool if False else S.dve)(fn, reads=[("pTA", pb), "pv"], writes=[("hT", tt, g + j)])
            S.dma(I("dma_start", out=hTd.rearrange("c p s -> p c s"), in_=hT[:]),
                  reads=[("hT", t_, g_) for t_ in range(TT) for g_ in range(DC)], writes=["hTd"])
            S.flush()
            if stop is not None and S.nphase >= stop:
                return nc

        def rms_rstd(ph, srcT, nchunk, rdim, rbc, tagp, skey, rkey):
            sq = [sb(f"sq{tagp}{i}", [128, S_], BF16, ph) for i in range(2)]
            pss = [BK[2 + i][:, 0:TG] for i in range(NTG)]
            for cc in range(nchunk):
                S.act(I("activation", out=sq[cc % 2][:], in_=srcT[:, cc, :], func=AF.Square),
                      reads=[(skey, cc)], writes=[("sq" + tagp, cc % 2)])
                for tg in range(NTG):
                    S.pe(I("matmul", pss[tg][:], lhsT=ones128[:], rhs=sq[cc % 2][:, tg * TG:(tg + 1) * TG],
                           start=(cc == 0), stop=(cc == nchunk - 1)), reads=[("sq" + tagp, cc % 2)], writes=[("pss" + tagp, tg)])
            for tg in range(NTG):
                sl_ = rbc[:, tg * TG:(tg + 1) * TG]
                S.dve(I("tensor_scalar", out=sl_, in0=pss[tg][:], scalar1=1.0 / rdim, scalar2=NORM_EPS, op0=ALU.mult, op1=ALU.add),
                      reads=[("pss" + tagp, tg)], writes=[(rkey, tg)])
                S.act(I("activation", out=sl_, in_=sl_, func=AF.Sqrt), reads=[(rkey, tg)], writes=[(rkey, tg)])
                S.dve(I("reciprocal", out=sl_, in_=sl_), reads=[(rkey, tg)], writes=[(rkey, tg)])

        def rope(ph_tiles, src32, tg, outbf, okey, skey):
            prot, t1, t2, cosS, sinS = ph_tiles
            S.pe(I("matmul", prot[0:64, :], lhsT=RT[:], rhs=src32, start=True, stop=True), reads=[skey, "RT"], writes=["prot"])
            S.pool(I("tensor_tensor", out=t1[0:64, :], in0=src32, in1=cosS[:, tg * TG:(tg + 1) * TG], op=ALU.mult),
                   reads=[skey, "cosS"], writes=["ropet1"])
            S.dve(I("tensor_tensor", out=t2[0:64, :], in0=prot[0:64, :], in1=sinS[:, tg * TG:(tg + 1) * TG], op=ALU.mult),
                  reads=["prot", "sinS"], writes=["ropet2"])
            S.dve(I("tensor_tensor", out=outbf, in0=t1[0:64, :], in1=t2[0:64, :], op=ALU.add),
                  reads=["ropet1", "ropet2"], writes=[okey])

        scale = (128 + 64) ** -0.5
        with contextlib.ExitStack() as ph:
            S.alias = {"pq0": ("bank", 2), "pq1": ("bank", 3), "prot": ("bank", 4)}
            S.alias.update({("pssq", i): ("bank", 2 + i) for i in range(NTG)})
            qaT = sb("qaT", [128, c.QRC, S_], BF16, ph)
            rq = sb("rq", [128, S_], F32, ph)
            cosS = sb("cosS", [64, S_], F32, ph); sinS = sb("sinS", [64, S_], F32, ph)
            S.dma(I("dma_start", out=cosS[:], in_=dr["cosT"]), writes=["cosS"])
            S.dma(I("dma_start", out=sinS[:], in_=dr["sinT"]), writes=["sinS"])
            for cc in range(c.QRC):
                def cons(tg, p_, pk, cc=cc):
                    S.act(I("activation", out=qaT[:, cc, tg * TG:(tg + 1) * TG], in_=p_, func=AF.Copy), reads=[pk], writes=[("qaT", cc)])
                inproj(c.c_qa + cc * 128, 128, cons, nxt=((c.c_qa + (cc + 1) * 128, 128) if cc + 1 < c.QRC else None))
            rms_rstd(ph, qaT, c.QRC, c.QR, rq, "q", "qaT", "rq")
            wqst = sb("wqst", [128, c.QRC, 192], F32, ph); wqbf = sb("wqbf", [128, c.QRC, 192], BF16, ph)
            pq = [BK[2 + i][:, 0:TG] for i in range(2)]
            prot = BK[4][:, 0:TG]
            t1 = sb("ropet1", [128, TG], F32, ph); t2 = sb("ropet2", [128, TG], F32, ph)
            q32 = sb("q32", [64, TG], F32, ph)
            qo = [sb(f"qo{i}", [128, TG], BF16, ph) for i in range(2)]
            qro = [sb(f"qro{i}", [64, TG], BF16, ph) for i in range(2)]
            k_ = 0
            for h in range(c.HM):
                S.dma(I("dma_start", out=wqst[:], in_=dr["mla_wq_b"][:, h * 192:(h + 1) * 192].rearrange("(c p) n -> p c n", p=128)), writes=["wqst"])
                for cc in range(c.QRC):
                    S.pool(I("tensor_scalar", out=wqbf[:, cc, :], in0=wqst[:, cc, :], scalar1=pcol("gq", cc), scalar2=None, op0=ALU.mult),
                           reads=["wqst", "pv"], writes=["wqbf"])
                for tg in range(NTG):
                    b = k_ % 2; k_ += 1
                    tsl = slice(tg * TG, (tg + 1) * TG)
                    for cc in range(c.QRC):
                        S.pe(I("matmul", pq[0][:], lhsT=wqbf[:, cc, 0:128], rhs=qaT[:, cc, tsl], start=(cc == 0), stop=(cc == c.QRC - 1)),
                             reads=["wqbf", ("qaT", cc)], writes=["pq0"])
                    S.dve(I("scalar_tensor_tensor", out=qo[b][:], in0=pq[0][:], scalar=scale, in1=rq[:, tsl], op0=ALU.mult, op1=ALU.mult),
                          reads=["pq0", ("rq", tg)], writes=[("qo", b)])
                    S.dma(I("dma_start", out=qTn[h, :, tsl], in_=qo[b][:]), reads=[("qo", b)], writes=[("qTn", h)])
                    for cc in range(c.QRC):
                        S.pe(I("matmul", pq[1][0:64, :], lhsT=wqbf[:, cc, 128:192], rhs=qaT[:, cc, tsl], start=(cc == 0), stop=(cc == c.QRC - 1)),
                             reads=["wqbf", ("qaT", cc)], writes=["pq1"])
                    S.dve(I("scalar_tensor_tensor", out=q32[:], in0=pq[1][0:64, :], scalar=scale, in1=rq[0:64, tsl], op0=ALU.mult, op1=ALU.mult),
                          reads=["pq1", ("rq", tg)], writes=["q32"])
                    rope((prot, t1, t2, cosS, sinS), q32[:], tg, qro[b][:], ("qro", b), "q32")
                    S.dma(I("dma_start", out=qTr[h, :, tsl], in_=qro[b][:]), reads=[("qro", b)], writes=[("qTr", h)])
            S.flush()
            if stop is not None and S.nphase >= stop:
                return nc

        with contextlib.ExitStack() as ph:
            S.alias = {"pk": ("bank", 2), "pvv": ("bank", 3)}
            S.alias.update({("psskv", i): ("bank", 2 + i) for i in range(NTG)})
            kvaT = sb("kvaT", [128, c.KVRC, S_], BF16, ph)
            rkv = sb("rkv", [128, S_], F32, ph)
            rkt = sb("rkt", [128, TT], F32, ph)
            cosS = sb("cosS2", [64, S_], F32, ph); sinS = sb("sinS2", [64, S_], F32, ph)
            S.dma(I("dma_start", out=cosS[:], in_=dr["cosT"]), writes=["cosS"])
            S.dma(I("dma_start", out=sinS[:], in_=dr["sinT"]), writes=["sinS"])
            for cc in range(c.KVRC):
                def cons(tg, p_, pk, cc=cc):
                    S.act(I("activation", out=kvaT[:, cc, tg * TG:(tg + 1) * TG], in_=p_, func=AF.Copy), reads=[pk], writes=[("kvaT", cc)])
                inproj(c.c_kva + cc * 128, 128, cons, nxt=((c.c_kva + (cc + 1) * 128, 128) if cc + 1 < c.KVRC else None))
            rms_rstd(ph, kvaT, c.KVRC, c.KVR, rkv, "kv", "kvaT", "rkv")
            prk = BK[7][:, 128:128 + TT]
            for tt in range(TT):
                S.pe(I("matmul", prk[:, tt:tt + 1], lhsT=rkv[:, tt * 128:(tt + 1) * 128], rhs=identf[:, 0:1], start=True, stop=True),
                     reads=[("rkv", tt * 128 // TG), "identf"], writes=["prk"])
            S.dve(I("tensor_copy", out=rkt[:], in_=prk[:]), reads=["prk"], writes=["rkt"])
            kr32 = sb("kr32", [64, S_], F32, ph); krT = sb("krTs", [64, S_], BF16, ph)
            prot = BK[6][:, 0:TG]
            t1 = sb("ropet1b", [128, TG], F32, ph); t2 = sb("ropet2b", [128, TG], F32, ph)

            def conskr(tg, p_, pk):
                tsl = slice(tg * TG, (tg + 1) * TG)
                S.act(I("activation", out=kr32[:, tsl], in_=p_, func=AF.Copy), reads=[pk], writes=[("kr32", tg)])
                rope((prot, t1, t2, cosS, sinS), kr32[:, tsl], tg, krT[:, tsl], ("krT", tg), ("kr32", tg))
                S.dma(I("dma_start", out=krTd[:, tsl], in_=krT[:, tsl]), reads=[("krT", tg)], writes=["krTd"])
            inproj(c.c_kr, 64, conskr)
            wkst = sb("wkst", [128, c.KVRC, 256], F32, ph); wkbf = sb("wkbf", [128, c.KVRC, 256], BF16, ph)
            pk_ = BK[2][:, 0:TG]
            pv_ = BK[3][:].rearrange("p (j t) -> p j t", t=128)
            ko = [sb(f"ko{i}", [128, TG], BF16, ph) for i in range(2)]
            vo = [sb(f"vo{i}", [128, 4, 128], BF16, ph) for i in range(2)]
            k_ = 0
            for h in range(c.HM):
                S.dma(I("dma_start", out=wkst[:], in_=dr["mla_wkv_b"][:, h * 256:(h + 1) * 256].rearrange("(c p) n -> p c n", p=128)), writes=["wkst"])
                for cc in range(c.KVRC):
                    S.pool(I("tensor_scalar", out=wkbf[:, cc, :], in0=wkst[:, cc, :], scalar1=pcol("gkv", cc), scalar2=None, op0=ALU.mult),
                           reads=["wkst", "pv"], writes=["wkbf"])
                for tg in range(NTG):
                    b = k_ % 2; k_ += 1
                    tsl = slice(tg * TG, (tg + 1) * TG)
                    for cc in range(c.KVRC):
                        S.pe(I("matmul", pk_[:], lhsT=wkbf[:, cc, 0:128], rhs=kvaT[:, cc, tsl], start=(cc == 0), stop=(cc == c.KVRC - 1)),
                             reads=["wkbf", ("kvaT", cc)], writes=["pk"])
                    S.dve(I("tensor_tensor", out=ko[b][:], in0=pk_[:], in1=rkv[:, tsl], op=ALU.mult), reads=["pk", ("rkv", tg)], writes=[("ko", b)])
                    S.dma(I("dma_start", out=kTn[h, :, tsl], in_=ko[b][:]), reads=[("ko", b)], writes=[("kTn", h)])
                for g in range(0, TT, 4):
                    n = min(4, TT - g)
                    b = k_ % 2; k_ += 1
                    for j in range(n):
                        tt = g + j
                        for cc in range(c.KVRC):
                            S.pe(I("matmul", pv_[:, j, :], lhsT=kvaT[:, cc, tt * 128:(tt + 1) * 128], rhs=wkbf[:, cc, 128:256],
                                   start=(cc == 0), stop=(cc == c.KVRC - 1)), reads=["wkbf", ("kvaT", cc)], writes=["pvv"])
                    for j in range(n):
                        S.act(I("activation", out=vo[b][:, j, :], in_=pv_[:, j, :], func=AF.Copy, scale=rkt[:, g + j:g + j + 1]),
                              reads=["pvv", "rkt"], writes=[("vo", b)])
                    S.dma(I("dma_start", out=Vtm[h, g * 128:(g + n) * 128, :].rearrange("(j p) d -> p j d", p=128), in_=vo[b][:, 0:n, :]),
                          reads=[("vo", b)], writes=[("Vtm", h)])
            S.flush()
            if stop is not None and S.nphase >= stop:
                return nc

        with contextlib.ExitStack() as ph:
            krT = sb("krA", [64, S_], BF16, ph)
            S.dma(I("dma_start", out=krT[:], in_=krTd), writes=["krA"])
            qn = [sb(f"qnA{i}", [128, S_], BF16, ph) for i in range(2)]
            qr = [sb(f"qrA{i}", [64, S_], BF16, ph) for i in range(2)]
            kn = [sb(f"knA{i}", [128, S_], BF16, ph) for i in range(2)]
            vt = [sb(f"vtA{i}", [128, TT, 128], BF16, ph) for i in range(2)]
            szm = [sb(f"szm{i}", [128, S_], BF16, ph) for i in range(2)]
            pS = [BK[2 + i][:, 0:TG] for i in range(2)]
            pO = BK[4][:, 0:TG]; pSum = BK[5][:, 0:TG]
            pt = [sb(f"ptA{i}", [128, TG], BF16, ph) for i in range(3)]
            rs = sb("rsA", [128, TG], F32, ph); y32 = sb("y32A", [128, TG], F32, ph)
            pacc = [sb(f"paccA{i}", [128, TG], F32, ph) for i in range(2)]
            yo = [sb(f"yoA{i}", [128, TG], BF16, ph) for i in range(2)]
            kk_ = 0; yy_ = 0
            for h in range(c.HM):
                sl = h % 2
                S.dma(I("dma_start", out=qn[sl][:], in_=qTn[h]), writes=[("qn", sl)])
                S.dma(I("dma_start", out=qr[sl][:], in_=qTr[h]), writes=[("qr", sl)])
                S.dma(I("dma_start", out=kn[sl][:], in_=kTn[h]), writes=[("kn", sl)])
                S.dma(I("dma_start", out=vt[sl][:], in_=Vtm[h].rearrange("(j p) d -> p j d", p=128)), writes=[("vt", sl)])

                def consz(tg, p_, pk, sl=sl):
                    S.act(I("activation", out=szm[sl][:, tg * TG:(tg + 1) * TG], in_=p_, func=AF.Silu), reads=[pk], writes=[("szm", sl, tg)])
                inproj(c.c_zm + h * 128, 128, consz)
                for qg in range(NTG):
                    qsl = slice(qg * TG, (qg + 1) * TG)
                    pend = None
                    for kt in range(TT):
                        b = kk_ % 2; p3 = kk_ % 3; kk_ += 1
                        ksl = slice(kt * 128, (kt + 1) * 128)
                        S.pe(I("matmul", pS[b][:], lhsT=kn[sl][:, ksl], rhs=qn[sl][:, qsl], start=True, stop=False),
                             reads=[("kn", sl), ("qn", sl)], writes=[("pS", b)])
                        S.pe(I("matmul", pS[b][:], lhsT=krT[:, ksl], rhs=qr[sl][:, qsl], start=False, stop=True),
                             reads=["krA", ("qr", sl)], writes=[("pS", b)])
                        S.act(I("activation", out=pt[p3][:], in_=pS[b][:], func=AF.Exp), reads=[("pS", b)], writes=[("pt", p3)])
                        for (kt_, p3_) in ([pend] if pend is not None else []) + ([(kt, p3)] if kt == TT - 1 else []):
                            S.pe(I("matmul", pO[:], lhsT=vt[sl][:, kt_, :], rhs=pt[p3_][:], start=(kt_ == 0), stop=(kt_ == TT - 1)),
                                 reads=[("vt", sl), ("pt", p3_)], writes=["pO"])
                            pa_ = pacc[yy_ % 2]
                            if kt_ == 0:
                                S.dve(I("tensor_copy", out=pa_[:], in_=pt[p3_][:]), reads=[("pt", p3_)], writes=[("pacc", yy_ % 2)])
                            else:
                                S.dve(I("tensor_tensor", out=pa_[:], in0=pa_[:], in1=pt[p3_][:], op=ALU.add), reads=[("pt", p3_), ("pacc", yy_ % 2)], writes=[("pacc", yy_ % 2)])
                            if kt_ == TT - 1:
                                S.pe(I("matmul", pSum[:], lhsT=ones32[:], rhs=pa_[:], start=True, stop=True), reads=[("pacc", yy_ % 2)], writes=["pSum"])
                        pend = (kt, p3)
                    yb = yy_ % 2; yy_ += 1
                    S.dve(I("reciprocal", out=rs[:], in_=pSum[:]), reads=["pSum"], writes=["rsA"])
                    S.dve(I("tensor_tensor", out=y32[:], in0=pO[:], in1=rs[:], op=ALU.mult), reads=["pO", "rsA"], writes=["y32A"])
                    S.pool(I("tensor_tensor", out=yo[yb][:], in0=y32[:], in1=szm[sl][:, qsl], op=ALU.mult),
                           reads=["y32A", ("szm", sl, qg)], writes=[("yo", yb)])
                    S.dma(I("dma_start", out=ymz[h, :, qsl], in_=yo[yb][:]), reads=[("yo", yb)], writes=["ymz"])
            S.flush()
            if stop is not None and S.nphase >= stop:
                return nc

        def lerp(raw, P, mucol, out32, outkey, rkey):
            tmpa, tmpb = lerp_t
            S.pool(I("tensor_tensor", out=tmpa[0:P, :], in0=raw[0:P, 0:S_], in1=raw[0:P, 2:S_ + 2], op=ALU.add), reads=[rkey], writes=["lta"])
            S.dve(I("tensor_scalar", out=tmpb[0:P, :], in0=raw[0:P, 1:S_ + 1], scalar1=pvo[0:P, mucol:mucol + 1], scalar2=None, op0=ALU.mult),
                  reads=[rkey, "pvo"], writes=["ltb"])
            S.dve(I("scalar_tensor_tensor", out=out32, in0=tmpa[0:P, :], scalar=pvh[0:P, mucol:mucol + 1], in1=tmpb[0:P, :], op0=ALU.mult, op1=ALU.add),
                  reads=["lta", "ltb", "pvh"], writes=[outkey])

        with contextlib.ExitStack() as ph:
            raws = [sb(f"raw{i}", [128, S_ + 2], F32, ph) for i in range(2)]
            lerp_t = (sb("lta", [128, S_], F32, ph), sb("ltb", [128, S_], F32, ph))
            o32 = [sb(f"o32R{i}", [128, S_], F32, ph) for i in range(2)]
            kk32 = sb("kk32", [128, S_], F32, ph); sqk = sb("sqk", [128, S_], BF16, ph)
            nrm = sb("nrmk", [128, TG], F32, ph)
            lwo = sb("lwo", [96, S_], BF16, ph)
            szo = [sb(f"szo{i}", [128, S_], BF16, ph) for i in range(2)]
            pkk = BK[2][:, 0:TG]
            for i_ in range(2):
                S.pool(I("memset", raws[i_][:], 0.0), writes=[("raw", i_)])
            rw = [0]
            oi = 0
            for i in range(4):
                rb = rw[0] % 2; rw[0] += 1
                raw = raws[rb]

                def consl(tg, p_, pk, raw=raw, rb=rb):
                    S.act(I("activation", out=raw[0:96, 1 + tg * TG:1 + (tg + 1) * TG], in_=p_, func=AF.Copy), reads=[pk], writes=[("raw", rb)])
                inproj(c.c_lora + i * 96, 96, consl, nxt=((c.c_lora + (i + 1) * 96, 96) if i < 3 else (c.c_r, 128)))
                ob = oi % 2; oi += 1
                lerp(raw, 96, col["mul"] + i, o32[ob][0:96, :], ("o32", ob), ("raw", rb))
                S.act(I("activation", out=lwo[:], in_=o32[ob][0:96, :], func=(AF.Tanh if i < 2 else AF.Copy)), reads=[("o32", ob)], writes=["lwo"])
                S.dma(I("dma_start", out=lwT[i], in_=lwo[:]), reads=["lwo"], writes=["lwT"])
            for hp in range(c.NP):
                for j, cbase in enumerate([c.c_r, c.c_k, c.c_v]):
                    rb = rw[0] % 2; rw[0] += 1
                    raw = raws[rb]

                    def consr(tg, p_, pk, raw=raw, rb=rb):
                        S.act(I("activation", out=raw[:, 1 + tg * TG:1 + (tg + 1) * TG], in_=p_, func=AF.Copy), reads=[pk], writes=[("raw", rb)])
                    nx_ = ([c.c_r, c.c_k, c.c_v][j + 1] + hp * 128, 128) if j < 2 else (c.c_zr + hp * 128, 128)
                    inproj(cbase + hp * 128, 128, consr, nxt=nx_)
                    ob = oi % 2; oi += 1
                    lerp(raw, 128, col["mu"] + j * c.NP + hp, o32[ob][:], ("o32", ob), ("raw", rb))
                    S.dma(I("dma_start", out=rinT[j * c.NP + hp], in_=o32[ob][:]), reads=[("o32", ob)], writes=["rinT"])
                    if j == 1:
                        S.dve(I("tensor_scalar", out=kk32[:], in0=o32[ob][:], scalar1=pcol("k_k", hp), scalar2=None, op0=ALU.mult),
                              reads=[("o32", ob), "pv"], writes=["kk32"])
                        S.act(I("activation", out=sqk[:], in_=kk32[:], func=AF.Square), reads=["kk32"], writes=["sqk"])
                        for tg in range(NTG):
                            tsl = slice(tg * TG, (tg + 1) * TG)
                            S.pe(I("matmul", pkk[:], lhsT=blk1[:], rhs=sqk[:, tsl], start=True, stop=True), reads=["sqk"], writes=["pkk"])
                            S.act(I("activation", out=nrm[:], in_=pkk[:], func=AF.Sqrt), reads=["pkk"], writes=["nrmk"])
                            S.dve(I("tensor_scalar", out=nrm[:], in0=nrm[:], scalar1=1e-12, scalar2=None, op0=ALU.max), reads=["nrmk"], writes=["nrmk"])
                            S.dve(I("reciprocal", out=nrm[:], in_=nrm[:]), reads=["nrmk"], writes=["nrmk"])
                            S.dve(I("tensor_tensor", out=kk32[:, tsl], in0=kk32[:, tsl], in1=nrm[:], op=ALU.mult), reads=["kk32", "nrmk"], writes=["kk32"])
                        S.dma(I("dma_start", out=kknD[hp], in_=kk32[:]), reads=["kk32"], writes=["kknD"])
                zb = hp % 2

                def conszr(tg, p_, pk, zb=zb):
                    S.act(I("activation", out=szo[zb][:, tg * TG:(tg + 1) * TG], in_=p_, func=AF.Silu), reads=[pk], writes=[("szo", zb)])
                inproj(c.c_zr + hp * 128, 128, conszr, nxt=((c.c_r + (hp + 1) * 128, 128) if hp + 1 < c.NP else None))
                S.dma(I("dma_start", out=szrD[hp], in_=szo[zb][:]), reads=[("szo", zb)], writes=["szrD"])
            S.flush()
            if stop is not None and S.nphase >= stop:
                return nc

        RTG = min(256, S_); RNTG = S_ // RTG; RCK = RTG // 128
        stg["stack"].close()
        hstack.close()
        for hp0 in range(0, c.NP, 2):
            hps = [hp_ for hp_ in (hp0, hp0 + 1) if hp_ < c.NP]
            pp = contextlib.ExitStack()
            PP = []
            for pi, hp in enumerate(hps):
                PP.append(dict(w2st=sb("w2st", [96, 4, 128], F32, pp), w2bf=sb("w2bf", [96, 4, 128], BF16, pp),
                               Ytm=sb("Ytm", [128, TT, 128], F32, pp), bon=sb("bon", [128, S_], F32, pp)))
            with contextlib.ExitStack() as ph:
                S.alias = {"pHr": "pW", "pYr": "pW", "pz": "pA", "pTr": "pA", "pB": "pA"}
                S.shared = {"Ytm", "bon", "w2bf"}
                for pi, hp in enumerate(hps):
                    for i, nm in enumerate(["rwkv_w2_f", "rwkv_w2_b", "rwkv_a2_f", "rwkv_a2_b"]):
                        S.dma(I("dma_start", out=PP[pi]["w2st"][:, i, :], in_=dr[nm][:, hp * 128:(hp + 1) * 128]), writes=[("w2st", pi)])
                    S.dve(I("tensor_copy", out=PP[pi]["w2bf"][:], in_=PP[pi]["w2st"][:]), reads=[("w2st", pi)], writes=[("w2bf", pi)])
                    S.pool(I("memset", PP[pi]["Ytm"][:], 0.0), writes=[("Ytm", pi, t_) for t_ in range(TT)])
                    S.pool(I("memset", PP[pi]["bon"][:], 0.0), writes=[("bon", pi, t_) for t_ in range(RNTG)])

                def stream(d, pi, hp, sidx):
                    w2bf = PP[pi]["w2bf"]; Ytm = PP[pi]["Ytm"]; bon = PP[pi]["bon"]
                    f32n = ["r", "k", "v", "kkn", "sg", "a", "cum", "tmp", "ex", "ep", "en", "eh", "ka", "kf", "pre"]
                    T = {n: sb("R_" + n, [128, RTG], F32, ph) for n in f32n}
                    lwd = sb("lwd", [96, RTG], BF16, ph); lad = sb("lad", [96, RTG], BF16, ph)
                    AR = sb("AR", [128, RCK, 256], BF16, ph)
                    ZA = sb("ZA", [128, 128], BF16, ph); ZV = sb("ZV", [128, 128], BF16, ph)
                    BtZ = sb("BtZ", [128, 2, RTG], BF16, ph); KtZ = sb("KtZ", [128, 2, RTG], BF16, ph)
                    S.pool(I("memset", BtZ[:], 0.0), writes=["Bt"])
                    S.pool(I("memset", KtZ[:], 0.0), writes=["Kt"])
                    Bh = sb("Bh", [128, RTG], BF16, ph); Kh = sb("Kh", [128, RTG], BF16, ph)
                    vb = sb("vb", [128, RTG], BF16, ph); prb = sb("prb", [128, RTG], BF16, ph)
                    bA = BK[2 * sidx]; bB = bA; bC = BK[2 * sidx + 1]
                    pz = bA[:, 0:RTG]
                    pA = bA[:].rearrange("p (h t) -> p h t", t=256)
                    pB = bB[:].rearrange("p (h t) -> p h t", t=256)
                    pW = bC[:, 0:256].rearrange("p (h t) -> p h t", t=128)
                    pTr = bB[:].bitcast(BF16)[:, 0:512].rearrange("p (j t) -> p j t", t=128)
                    pY = bC[:, 256:384]
                    pH = bC[:, 384:512]
                    TM = sb("TM", [128, 4, 128], BF16, ph)
                    NA = sb("NA", [128, 2, 256], BF16, ph)
                    KA = sb("KA", [128, 2, 256], BF16, ph)
                    XX = [sb(f"XX{i}", [128, 2, 2, 128], BF16, ph) for i in range(2)]
                    W = [sb(f"W{i}", [128, 2, 128], BF16, ph) for i in range(2)]
                    RhT = sb("RhT", [128, 128], BF16, ph)
                    MT = sb("MTbd", [128, 128], BF16, ph)
                    H32 = sb("H32", [128, 128], F32, ph); Hb = sb("Hb", [128, 128], BF16, ph)
                    Htmp = sb("Htmp", [128, 128], F32, ph)
                    ptot = sb("ptot", [128, RCK], F32, ph)
                    Mm = Mf if d == 0 else Mb
                    Nm = Nf if d == 0 else Nb
                    w0c = pcol("w0_f" if d == 0 else "w0_b", hp)
                    a0c = pcol("a0_f" if d == 0 else "a0_b", hp)
                    S.dve(I("memset", H32[:], 0.0), writes=["H32"])
                    S.dve(I("memset", Hb[:], 0.0), writes=["Hb"])
                    tgs = range(RNTG) if d == 0 else range(RNTG - 1, -1, -1)
                    for tg in tgs:
                        tsl = slice(tg * RTG, (tg + 1) * RTG)
                        S.dma(I("dma_start", out=T["r"][:], in_=rinT[0 * c.NP + hp, :, tsl]), writes=["R_r"])
                        S.dma(I("dma_start", out=T["k"][:], in_=rinT[1 * c.NP + hp, :, tsl]), writes=["R_k"])
                        S.dma(I("dma_start", out=T["v"][:], in_=rinT[2 * c.NP + hp, :, tsl]), writes=["R_v"])
                        S.dma(I("dma_start", out=T["kkn"][:], in_=kknD[hp, :, tsl]), writes=["R_kkn"])
                        S.dma(I("dma_start", out=lwd[:], in_=lwT[d, :, tsl]), writes=["lwd"])
                        S.dma(I("dma_start", out=lad[:], in_=lwT[2 + d, :, tsl]), writes=["lad"])
                        S.pe(I("matmul", pz[:], lhsT=w2bf[:, d, :], rhs=lwd[:], start=True, stop=True), reads=[("w2bf", pi), "lwd"], writes=["pz"])
                        S.act(I("activation", out=T["sg"][:], in_=pz[:], func=AF.Sigmoid, bias=w0c), reads=["pz", "pv"], writes=["R_sg"])
                        S.pe(I("matmul", pz[:], lhsT=w2bf[:, 2 + d, :], rhs=lad[:], start=True, stop=True), reads=[("w2bf", pi), "lad"], writes=["pz"])
                        S.act(I("activation", out=T["a"][:], in_=pz[:], func=AF.Sigmoid, bias=a0c), reads=["pz", "pv"], writes=["R_a"])
                        S.dve(I("tensor_tensor_scan", out=T["pre"][:], data0=scanmask[:, 0:RTG], data1=T["sg"][:], initial=0.0, op0=ALU.mult, op1=ALU.add),
                              reads=["R_sg", "scanmask"], writes=["R_pre"])
                        pre3 = T["pre"][:].rearrange("p (c t) -> p c t", t=128)
                        cum3 = T["cum"][:].rearrange("p (c t) -> p c t", t=128)
                        tot_bc = bc_last(pre3[:, :, 127:128], 128)
                        S.dve(I("tensor_copy", out=ptot[:].rearrange("p (c o) -> p c o", o=1), in_=pre3[:, :, 127:128]), reads=["R_pre"], writes=["ptot"])
                        if d == 0:
                            S.pool(I("tensor_copy", out=T["cum"][:], in_=T["pre"][:]), reads=["R_pre"], writes=["R_cum"])
                        else:
                            S.dve(I("tensor_tensor", out=cum3, in0=tot_bc, in1=pre3, op=ALU.subtract), reads=["R_pre"], writes=["R_cum"])
                            S.dve(I("tensor_tensor", out=T["cum"][:], in0=T["cum"][:], in1=T["sg"][:], op=ALU.add), reads=["R_cum", "R_sg"], writes=["R_cum"])
                        S.pool(I("tensor_tensor", out=T["tmp"][:], in0=T["cum"][:], in1=T["sg"][:], op=ALU.subtract), reads=["R_cum", "R_sg"], writes=["R_tmp"])
                        S.act(I("activation", out=T["ex"][:], in_=T["tmp"][:], func=AF.Exp, scale=-C0), reads=["R_tmp"], writes=["R_ex"])
                        S.act(I("activation", out=T["ep"][:], in_=T["cum"][:], func=AF.Exp, scale=-C0), reads=["R_cum"], writes=["R_ep"])
                        S.act(I("activation", out=T["en"][:], in_=T["cum"][:], func=AF.Exp, scale=C0), reads=["R_cum"], writes=["R_en"])
                        S.dve(I("tensor_tensor", out=T["tmp"][:].rearrange("p (c t) -> p c t", t=128), in0=tot_bc, in1=cum3, op=ALU.subtract),
                              reads=["R_pre", "R_cum", "R_ex"], writes=["R_tmp"])
                        S.act(I("activation", out=T["eh"][:], in_=T["tmp"][:], func=AF.Exp, scale=-C0), reads=["R_tmp"], writes=["R_eh"])
                        S.pool(I("tensor_tensor", out=T["ka"][:], in0=T["kkn"][:], in1=T["a"][:], op=ALU.mult), reads=["R_kkn", "R_a"], writes=["R_ka"])
                        S.dve(I("tensor_scalar", out=T["kf"][:], in0=T["a"][:], scalar1=pcol("k_a", hp), scalar2=pvo[:, col["k_a"] + hp:col["k_a"] + hp + 1],
                                op0=ALU.mult, op1=ALU.add), reads=["R_a", "pv", "pvo"], writes=["R_kf"])
                        S.dve(I("tensor_tensor", out=T["kf"][:], in0=T["kf"][:], in1=T["k"][:], op=ALU.mult), reads=["R_kf", "R_k"], writes=["R_kf"])
                        S.dve(I("scalar_tensor_tensor", out=prb[:], in0=T["kf"][:], scalar=pcol("r_k", hp), in1=T["r"][:], op0=ALU.mult, op1=ALU.mult),
                              reads=["R_kf", "R_r"], writes=["prb"])
                        S.pe(I("matmul", pz[:], lhsT=blk1[:], rhs=prb[:], start=True, stop=True), reads=["prb"], writes=["pz"])
                        S.dve(I("tensor_tensor", out=T["pre"][:], in0=pz[:], in1=T["v"][:], op=ALU.mult), reads=["pz", "R_v"], writes=["R_pre"])
                        S.pool(I("tensor_tensor", out=bon[:, tsl], in0=bon[:, tsl], in1=T["pre"][:], op=ALU.add), reads=["R_pre", ("bon", pi, tg)], writes=[("bon", pi, tg)])
                        S.dve(I("scalar_tensor_tensor", out=AR[:, :, 0:128], in0=T["kkn"][:].rearrange("p (c t) -> p c t", t=128), scalar=-1.0,
                                in1=T["ex"][:].rearrange("p (c t) -> p c t", t=128), op0=ALU.mult, op1=ALU.mult),
                              reads=["R_kkn", "R_ex"], writes=["AR"])
                        S.pool(I("tensor_tensor", out=AR[:, :, 128:256], in0=T["r"][:].rearrange("p (c t) -> p c t", t=128),
                                 in1=T["ep"][:].rearrange("p (c t) -> p c t", t=128), op=ALU.mult), reads=["R_r", "R_ep"], writes=["AR"])
                        for h in range(2):
                            hs = slice(h * 64, (h + 1) * 64)
                            S.dve(I("tensor_tensor", out=BtZ[hs, h, :], in0=T["ka"][hs, :], in1=T["en"][hs, :], op=ALU.mult), reads=["R_ka", "R_en"], writes=["Bt"])
                            S.pool(I("tensor_tensor", out=KtZ[hs, h, :], in0=T["kf"][hs, :], in1=T["en"][hs, :], op=ALU.mult), reads=["R_kf", "R_en"], writes=["Kt"])
                        S.dve(I("tensor_tensor", out=Bh[:], in0=T["ka"][:], in1=T["eh"][:], op=ALU.mult), reads=["R_ka", "R_eh"], writes=["Bh"])
                        S.pool(I("tensor_tensor", out=Kh[:], in0=T["kf"][:], in1=T["eh"][:], op=ALU.mult), reads=["R_kf", "R_eh"], writes=["Kh"])
                        S.act(I("activation", out=vb[:], in_=T["v"][:], func=AF.Copy), reads=["R_v"], writes=["vb"])
                        S.act(I("activation", out=ptot[:], in_=ptot[:], func=AF.Exp, scale=-C0), reads=["ptot"], writes=["ptot"])
                        cks = range(RCK) if d == 0 else range(RCK - 1, -1, -1)
                        for ck in cks:
                            csl = slice(ck * 128, (ck + 1) * 128)
                            tt = tg * RCK + ck
                            for j, src in enumerate([AR[:, ck, 0:128], Bh[:, csl], Kh[:, csl], vb[:, csl]]):
                                S.pe(I("transpose", out=pTr[:, j, :], in_=src, identity=ident[:]), reads=["AR", "Bh", "Kh", "vb", "ident"], writes=["pTr"])
                            S.act(I("activation", out=TM[:], in_=pTr[:], func=AF.Copy), reads=["pTr"], writes=["TM"])
                            for h in range(2):
                                S.pe(I("matmul", pA[:, h, :], lhsT=BtZ[:, h, csl], rhs=AR[:, ck, :], start=True, stop=True), reads=["Bt", "AR"], writes=["pA"])
                            S.dve(I("tensor_tensor", out=NA[:], in0=pA[:], in1=Mm[:], op=ALU.mult), reads=["pA"], writes=["NA"])
                            for h in range(2):
                                S.pe(I("matmul", pB[:, h, :], lhsT=KtZ[:, h, csl], rhs=AR[:, ck, :], start=True, stop=True), reads=["Kt", "AR"], writes=["pB"])
                            S.dve(I("tensor_tensor", out=KA[:], in0=pB[:], in1=Mm[:], op=ALU.mult), reads=["pB"], writes=["KA"])
                            for h in range(2):
                                hs = slice(h * 64, (h + 1) * 64)
                                S.pe(I("matmul", pW[:, h, :], lhsT=AR[:, ck, 0:128], rhs=BtZ[:, h, csl], start=True, stop=True), reads=["AR", "Bt"], writes=["pW"])
                            S.dve(I("tensor_tensor", out=XX[0][:, :, 0, :], in0=pW[:], in1=Nm[:], op=ALU.mult), reads=["pW"], writes=[("XX", 0)])
                            S.act(I("activation", out=XX[0][:, :, 1, :], in_=NA[:, :, 0:128], func=AF.Copy), reads=["NA"], writes=[("XX", 0)])
                            for h in range(2):
                                S.pe(I("matmul", pW[:, h, 64:128], lhsT=KA[:, h, 0:128], rhs=TM[:, 3, h * 64:(h + 1) * 64], start=True, stop=True),
                                     reads=["KA", "TM", ("XX", 0)], writes=["pW"])
                            S.act(I("activation", out=W[0][:, :, 64:128], in_=pW[:, :, 64:128], func=AF.Copy), reads=["pW"], writes=[("W", 0)])
                            S.dve(I("tensor_copy", out=W[0][:, :, 0:64], in_=TM[:, 0, :].rearrange("p (h j) -> p h j", j=64)), reads=["TM"], writes=[("W", 0)])
                            nlev = 7
                            for lv in range(nlev):
                                a_ = lv % 2; b_ = 1 - a_
                                for h in range(2):
                                    S.pe(I("matmul", pW[:, h, :], lhsT=ident[:], rhs=W[a_][:, h, :], start=True, stop=False), reads=[("W", a_), "ident"], writes=["pW"])
                                    S.pe(I("matmul", pW[:, h, :], lhsT=XX[a_][:, h, 1, :], rhs=W[a_][:, h, :], start=False, stop=True),
                                         reads=[("W", a_), ("XX", a_)], writes=["pW"])
                                S.act(I("activation", out=W[b_][:], in_=pW[:], func=AF.Copy), reads=["pW"], writes=[("W", b_)])
                                if lv < nlev - 1:
                                    pX = pA if lv % 2 == 0 else pB
                                    pXk = "pA" if lv % 2 == 0 else "pB"
                                    for h in range(2):
                                        S.pe(I("matmul", pX[:, h, 0:128], lhsT=XX[a_][:, h, 1, :], rhs=XX[a_][:, h, 0, :], start=True, stop=True),
                                             reads=[("XX", a_)], writes=[pXk])
                                        S.pe(I("matmul", pX[:, h, 128:256], lhsT=XX[a_][:, h, 0, :], rhs=XX[a_][:, h, 1, :], start=True, stop=True),
                                             reads=[("XX", a_)], writes=[pXk])
                                    S.dve(I("tensor_copy", out=XX[b_][:].rearrange("p h x t -> p h (x t)"), in_=pX[:]), reads=[pXk], writes=[("XX", b_)])
                            Z = W[nlev % 2]
                            zk = ("W", nlev % 2)
                            S.act(I("activation", out=ZA[:].rearrange("p (h j) -> p h j", j=64), in_=Z[:, :, 0:64], func=AF.Copy), reads=[zk], writes=["ZA"])
                            S.dve(I("tensor_copy", out=ZV[:].rearrange("p (h j) -> p h j", j=64), in_=Z[:, :, 64:128]), reads=[zk], writes=["ZV"])
                            for h in range(2):
                                S.pe(I("matmul", bA[:, h * 128:(h + 1) * 128], lhsT=ZA[:], rhs=NA[:, h, 128:256], start=True, stop=True), reads=["ZA", "NA"], writes=["pA"])
                            S.dve(I("tensor_tensor", out=RhT[0:64, :], in0=bA[0:64, 0:128], in1=AR[0:64, ck, 128:256], op=ALU.add), reads=["pA", "AR"], writes=["RhT"])
                            S.dve(I("tensor_tensor", out=RhT[64:128, :], in0=bA[64:128, 128:256], in1=AR[64:128, ck, 128:256], op=ALU.add), reads=["pA", "AR"], writes=["RhT"])
                            S.pe(I("matmul", pB[:, 0, 0:128], lhsT=ZA[:], rhs=TM[:, 1, :], start=True, stop=True), reads=["ZA", "TM"], writes=["pB"])
                            S.dve(I("tensor_tensor", out=MT[:], in0=pB[:, 0, 0:128], in1=bdmask[:], op=ALU.mult), reads=["pB", "bdmask"], writes=["MTbd"])
                            for h in range(2):
                                hs = slice(h * 64, (h + 1) * 64)
                                S.pe(I("matmul", pY[:, hs], lhsT=NA[:, h, 128:256], rhs=Z[:, h, 64:128], start=True, stop=False), reads=["NA", zk], writes=["pYr"])
                                S.pe(I("matmul", pY[:, hs], lhsT=KA[:, h, 128:256], rhs=TM[:, 3, hs], start=False, stop=False), reads=["KA", "TM"], writes=["pYr"])
                                S.pe(I("matmul", pY[:, hs], lhsT=RhT[:], rhs=Hb[:, hs], start=False, stop=True), reads=["RhT", "Hb"], writes=["pYr"])
                            S.dve(I("tensor_tensor", out=Ytm[:, tt, :], in0=pY[:], in1=Ytm[:, tt, :], op=ALU.add), reads=["pYr", ("Ytm", pi, tt)], writes=[("Ytm", pi, tt)])
                            S.pe(I("matmul", pH[:], lhsT=MT[:], rhs=Hb[:], start=True, stop=False), reads=["MTbd", "Hb"], writes=["pHr"])
                            S.pe(I("matmul", pH[:], lhsT=TM[:, 1, :], rhs=ZV[:], start=False, stop=False), reads=["TM", "ZV"], writes=["pHr"])
                            S.pe(I("matmul", pH[:], lhsT=TM[:, 2, :], rhs=TM[:, 3, :], start=False, stop=True), reads=["TM"], writes=["pHr"])
                            S.dve(I("scalar_tensor_tensor", out=Htmp[:], in0=H32[:], scalar=ptot[:, ck:ck + 1], in1=pH[:], op0=ALU.mult, op1=ALU.add),
                                  reads=["H32", "ptot", "pHr"], writes=["Htmp"])
                            S.dve(I("tensor_tensor", out=H32[:], in0=Htmp[:], in1=bdmask[:], op=ALU.mult), reads=["Htmp", "bdmask"], writes=["H32"])
                            S.act(I("activation", out=Hb[:], in_=H32[:], func=AF.Copy), reads=["H32"], writes=["Hb"])

                lists = []
                for pi, hp in enumerate(hps):
                    for d in range(2):
                        base_ops = S.ops
                        S.ops = []
                        S.suffix = (pi, d)
                        stream(d, pi, hp, 2 * pi + d)
                        S.suffix = None
                        lists.append(S.ops)
                        S.ops = base_ops
                for i_ in range(max(len(l_) for l_ in lists)):
                    for l_ in lists:
                        if i_ < len(l_):
                            o_ = l_[i_]
                            o_.idx = len(S.ops)
                            S.ops.append(o_)
                S.flush()
                if stop is not None and S.nphase >= stop:
                    pp.close()
                    return nc
            for pi, hp in enumerate(hps):
              Ytm = PP[pi]["Ytm"]; bon = PP[pi]["bon"]
              with contextlib.ExitStack() as ph:
                szr = sb("szrS", [128, S_], BF16, ph)
                S.dma(I("dma_start", out=szr[:], in_=szrD[hp]), writes=["szrS"])
                pTr = BK[6][:].bitcast(BF16)[:, 0:512].rearrange("p (j t) -> p j t", t=128)
                Y4 = Ytm[:].rearrange("p t (h i) -> p (t h) i", i=64)
                NG = 2 * TT
                s1 = sb("gn_s1", [128, NG], F32, ph); s2 = sb("gn_s2", [128, NG], F32, ph)
                ysq = sb("gn_sq", [128, TT, 128], F32, ph)
                yn = sb("gn_yn", [128, TT, 128], BF16, ph)
                S.dve(I("tensor_reduce", out=s1[:], in_=Y4, axis=AX.X, op=ALU.add), reads=[("Ytm", t_) for t_ in range(TT)], writes=["gn_s1"])
                S.act(I("activation", out=ysq[:], in_=Ytm[:], func=AF.Square), reads=[("Ytm", t_) for t_ in range(TT)], writes=["gn_sq"])
                S.dve(I("tensor_reduce", out=s2[:], in_=ysq[:].rearrange("p t (h i) -> p (t h) i", i=64), axis=AX.X, op=ALU.add), reads=["gn_sq"], writes=["gn_s2"])
                S.dve(I("tensor_scalar", out=s1[:], in0=s1[:], scalar1=1.0 / 64, scalar2=None, op0=ALU.mult), reads=["gn_s1"], writes=["gn_s1"])
                S.dve(I("tensor_scalar", out=s2[:], in0=s2[:], scalar1=1.0 / 64, scalar2=GN_EPS, op0=ALU.mult, op1=ALU.add), reads=["gn_s2"], writes=["gn_s2"])
                msq = sb("gn_msq", [128, NG], F32, ph)
                S.dve(I("tensor_tensor", out=msq[:], in0=s1[:], in1=s1[:], op=ALU.mult), reads=["gn_s1"], writes=["gn_msq"])
                S.dve(I("tensor_tensor", out=s2[:], in0=s2[:], in1=msq[:], op=ALU.subtract), reads=["gn_s2", "gn_msq"], writes=["gn_s2"])
                S.act(I("activation", out=s2[:], in_=s2[:], func=AF.Sqrt), reads=["gn_s2"], writes=["gn_s2"])
                S.dve(I("reciprocal", out=s2[:], in_=s2[:]), reads=["gn_s2"], writes=["gn_s2"])
                s1b = bc_last(s1[:].rearrange("p (g o) -> p g o", o=1), 64)
                s2b = bc_last(s2[:].rearrange("p (g o) -> p g o", o=1), 64)
                ysq4 = ysq[:].rearrange("p t (h i) -> p (t h) i", i=64)
                S.dve(I("tensor_tensor", out=ysq4, in0=Y4, in1=s1b, op=ALU.subtract), reads=[("Ytm", t_) for t_ in range(TT)] + ["gn_s1", "gn_sq"], writes=["gn_sq"])
                S.dve(I("tensor_tensor", out=yn[:].rearrange("p t (h i) -> p (t h) i", i=64), in0=ysq4, in1=s2b, op=ALU.mult), reads=["gn_sq", "gn_s2"], writes=["gn_yn"])
                yfm = sb("gn_yfm", [128, 4, 128], F32, ph)
                yob = [sb(f"gn_yo{i}", [128, 4, 128], BF16, ph) for i in range(2)]
                gi = 0
                for g in range(0, TT, 4):
                    n = min(4, TT - g)
                    for j in range(n):
                        S.pe(I("transpose", out=pTr[:, j, :], in_=yn[:, g + j, :], identity=ident[:]), reads=["gn_yn", "ident"], writes=["pTr"])
                    fsl = slice(g * 128, (g + n) * 128)
                    S.act(I("activation", out=yfm[:, 0:n, :], in_=pTr[:, 0:n, :], func=AF.Identity, scale=pcol("gn_g", hp), bias=pcol("gn_b", hp)),
                          reads=["pTr", "pv"], writes=["gn_yfm"])
                    S.dve(I("tensor_tensor", out=yfm[:, 0:n, :], in0=yfm[:, 0:n, :], in1=bon[:, fsl].rearrange("p (j t) -> p j t", t=128), op=ALU.add),
                          reads=["gn_yfm"] + [("bon", t_) for t_ in range(NTG)], writes=["gn_yfm"])
                    ob = gi % 2; gi += 1
                    S.dve(I("tensor_tensor", out=yob[ob][:, 0:n, :], in0=yfm[:, 0:n, :], in1=szr[:, fsl].rearrange("p (j t) -> p j t", t=128), op=ALU.mult),
                          reads=["gn_yfm", "szrS"], writes=[("gn_yo", ob)])
                    S.dma(I("dma_start", out=yrz[hp, :, fsl].rearrange("p (j t) -> p j t", t=128), in_=yob[ob][:, 0:n, :]), reads=[("gn_yo", ob)], writes=["yrz"])
                S.flush()
                if stop is not None and S.nphase >= stop:
                            return nc
            pp.close()
        hstack2 = contextlib.ExitStack()
        es.enter_context(hstack2)
        hT2 = sb("hT2", [128, DC, S_], BF16, hstack2)
        hts["hT"] = hT2
        alloc_stg()
        with contextlib.ExitStack() as ph:
            S.dma(I("dma_start", out=hT2[:], in_=hTd.rearrange("c p s -> p c s")), writes=["hT"])
            ymzS = sb("ymzS", [128, c.MWC, S_], BF16, ph); yrzS = sb("yrzS", [128, c.NP, S_], BF16, ph)
            S.dma(I("dma_start", out=ymzS[:], in_=ymz.rearrange("c p s -> p c s")), writes=["ymzS"])
            S.dma(I("dma_start", out=yrzS[:], in_=yrz.rearrange("c p s -> p c s")), writes=["yrzS"])
            wmst = sb("wmst", [128, c.MWC, 128], F32, ph); wmbf = sb("wmbf", [128, c.MWC, 128], BF16, ph)
            wrst = sb("wrst", [128, c.NP, 128], F32, ph); wrbf = sb("wrbf", [128, c.NP, 128], BF16, ph)
            sgm = sb("sgm", [128, S_], F32, ph); sgr = sb("sgr", [128, S_], F32, ph)
            pu = [BK[2 + i][:, 0:TG] for i in range(2)]
            t1f = sb("t1f", [128, TG], F32, ph)
            mo = [sb(f"mo{i}", [128, TG], BF16, ph) for i in range(2)]
            mi = 0
            for fc in range(DC):
                fsl = slice(fc * 128, (fc + 1) * 128)
                S.dma(I("dma_start", out=wmst[:], in_=dr["w_br_mla"][:, fsl].rearrange("(c p) n -> p c n", p=128)), writes=["wmst"])
                S.dve(I("tensor_copy", out=wmbf[:], in_=wmst[:]), reads=["wmst"], writes=["wmbf"])
                S.dma(I("dma_start", out=wrst[:], in_=dr["w_br_rwkv"][:, fsl].rearrange("(c p) n -> p c n", p=128)), writes=["wrst"])
                S.dve(I("tensor_copy", out=wrbf[:], in_=wrst[:]), reads=["wrst"], writes=["wrbf"])

                def consgm(tg, p_, pk):
                    S.act(I("activation", out=sgm[:, tg * TG:(tg + 1) * TG], in_=p_, func=AF.Sigmoid), reads=[pk], writes=[("sgm", tg)])

                def consgr(tg, p_, pk):
                    S.act(I("activation", out=sgr[:, tg * TG:(tg + 1) * TG], in_=p_, func=AF.Sigmoid), reads=[pk], writes=[("sgr", tg)])
                inproj(c.c_gm + fc * 128, 128, consgm, nxt=(c.c_gr + fc * 128, 128))
                inproj(c.c_gr + fc * 128, 128, consgr, nxt=((c.c_gm + (fc + 1) * 128, 128) if fc + 1 < DC else None))
                for tg in range(NTG):
                    tsl = slice(tg * TG, (tg + 1) * TG)
                    for wc in range(c.MWC):
                        S.pe(I("matmul", pu[0][:], lhsT=wmbf[:, wc, :], rhs=ymzS[:, wc, tsl], start=(wc == 0), stop=(wc == c.MWC - 1)),
                             reads=["wmbf", "ymzS"], writes=["pu0"])
                    for wc in range(c.NP):
                        S.pe(I("matmul", pu[1][:], lhsT=wrbf[:, wc, :], rhs=yrzS[:, wc, tsl], start=(wc == 0), stop=(wc == c.NP - 1)),
                             reads=["wrbf", "yrzS"], writes=["pu1"])
                    S.dve(I("tensor_tensor", out=t1f[:], in0=pu[0][:], in1=sgm[:, tsl], op=ALU.mult), reads=["pu0", ("sgm", tg)], writes=["t1f"])
                    mb = mi % 2; mi += 1
                    S.dve(I("tensor_tensor", out=mo[mb][:], in0=pu[1][:], in1=sgr[:, tsl], op=ALU.mult), reads=["pu1", ("sgr", tg)], writes=[("mo", mb)])
                    S.pool(I("tensor_tensor", out=mo[mb][:], in0=mo[mb][:], in1=t1f[:], op=ALU.add), reads=[("mo", mb), "t1f"], writes=[("mo", mb)])
                    S.dma(I("dma_start", out=mrg[fc, :, tsl], in_=mo[mb][:]), reads=[("mo", mb)], writes=["mrg"])
            S.flush()
            if stop is not None and S.nphase >= stop:
                return nc
        stg["stack"].close()
        hstack2.close()

        with contextlib.ExitStack() as ph:
            CB = min(512, D); NCB = D // CB
            wob = sb("wob", [128, DC, D], BF16, ph)
            wos = [sb(f"wos{i}", [128, D], F32, ph) for i in range(2)]
            for fc in range(DC):
                S.dma(I("dma_start", out=wos[fc % 2][:], in_=dr["w_out"][fc * 128:(fc + 1) * 128, :]), writes=[("wos", fc % 2)])
                (S.dve if fc % 2 == 0 else S.pool)(I("tensor_copy", out=wob[:, fc, :], in_=wos[fc % 2][:]), reads=[("wos", fc % 2)], writes=["wob"])
            gpb = sb("gpb", [128, D], F32, ph)
            S.dma(I("dma_start", out=gpb[:], in_=bass.AP(dr["g_post"].tensor, 0, [[0, 128], [1, D]])), writes=["gpb"])
            mT = [sb(f"mT{i}", [128, DC, 128], BF16, ph) for i in range(2)]
            xt = [sb(f"xtF{i}", [128, D], F32, ph) for i in range(2)]
            o32 = sb("o32F", [128, D], F32, ph)
            junk = sb("junkF", [128, CB], BF16, ph)
            ssp = sb("ssp", [128, TT, NCB + 4], F32, ph)
            po = [BK[i][:, 0:CB] for i in range(2)]
            ot = [sb(f"otF{i}", [128, D], F32, ph) for i in range(2)]
            pi_ = 0
            for tt in range(TT):
                sl = tt % 2
                tsl = slice(tt * 128, (tt + 1) * 128)
                S.dma(I("dma_start", out=mT[sl][:], in_=mrg[:, :, tsl].rearrange("c p t -> p c t")), writes=[("mT", sl)])
                S.dma(I("dma_start", out=xt[sl][:], in_=dr["x"][tsl, :]), writes=[("xtF", sl)])
                for cb in range(NCB):
                    b = pi_ % 2; pi_ += 1
                    for fc in range(DC):
                        S.pe(I("matmul", po[b][:], lhsT=mT[sl][:, fc, :], rhs=wob[:, fc, cb * CB:(cb + 1) * CB], start=(fc == 0), stop=(fc == DC - 1)),
                             reads=[("mT", sl), "wob"], writes=[("po", b)])
                    S.act(I("activation", out=o32[:, cb * CB:(cb + 1) * CB], in_=po[b][:], func=AF.Copy), reads=[("po", b)], writes=[("o32F", cb)])
                    S.act(I("activation", out=junk[:], in_=po[b][:], func=AF.Square, accum_out=ssp[:, tt, cb:cb + 1]), reads=[("po", b)], writes=["junkF", ("ssp", tt)])
                sc = NCB
                S.dve(I("tensor_reduce", out=ssp[:, tt, sc:sc + 1], in_=ssp[:, tt, 0:NCB], axis=AX.X, op=ALU.add), reads=[("ssp", tt)], writes=[("ssp", tt)])
                S.dve(I("tensor_scalar", out=ssp[:, tt, sc + 1:sc + 2], in0=ssp[:, tt, sc:sc + 1], scalar1=1.0 / D, scalar2=NORM_EPS, op0=ALU.mult, op1=ALU.add),
                      reads=[("ssp", tt)], writes=[("ssp", tt)])
                S.act(I("activation", out=ssp[:, tt, sc + 2:sc + 3], in_=ssp[:, tt, sc + 1:sc + 2], func=AF.Sqrt), reads=[("ssp", tt)], writes=[("ssp", tt)])
                S.dve(I("reciprocal", out=ssp[:, tt, sc + 3:sc + 4], in_=ssp[:, tt, sc + 2:sc + 3]), reads=[("ssp", tt)], writes=[("ssp", tt)])
                S.dve(I("scalar_tensor_tensor", out=ot[sl][:], in0=o32[:], scalar=ssp[:, tt, sc + 3:sc + 4], in1=gpb[:], op0=ALU.mult, op1=ALU.mult),
                      reads=[("o32F", cb_) for cb_ in range(NCB)] + [("ssp", tt), "gpb"], writes=[("otF", sl)])
                S.pool(I("tensor_tensor", out=ot[sl][:], in0=ot[sl][:], in1=xt[sl][:], op=ALU.add), reads=[("otF", sl), ("xtF", sl)], writes=[("otF", sl)])
                S.dma(I("dma_start", out=out[tsl, :], in_=ot[sl][:]), reads=[("otF", sl)], writes=["out"], final=True)
            S.flush(last=True)
    return nc


def rope_tables(S):
    pos = np.arange(S, dtype=np.float32)
    inv = np.power(np.float32(10000.0), -np.arange(0, 64, 2, dtype=np.float32) / np.float32(64)).astype(np.float32)
    ang = (pos[:, None] * inv[None, :]).astype(np.float32)
    cosT = np.concatenate([np.cos(ang), np.cos(ang)], axis=1).T.astype(np.float32)
    sinT = np.concatenate([np.sin(ang), np.sin(ang)], axis=1).T.astype(np.float32)
    return np.ascontiguousarray(cosT), np.ascontiguousarray(sinT)


def make_in_maps(c, inputs):
    B = inputs["x"].shape[0]
    cosT, sinT = rope_tables(c.S)
    shared = {}
    for n in list(MAT_SHAPES(c).keys()) + list(VEC_LENS(c).keys()):
        a = np.ascontiguousarray(np.asarray(inputs[n], dtype=np.float32))
        if n == "rwkv_r_k":
            a = a.reshape(-1)
        shared[n] = a
    shared["cosT"] = cosT; shared["sinT"] = sinT
    maps = []
    for b in range(B):
        m = dict(shared)
        m["x"] = np.ascontiguousarray(np.asarray(inputs["x"][b], dtype=np.float32))
        maps.append(m)
    return maps


def kernel(**inputs):
    c = Cfg()
    nc = build(c)
    maps = make_in_maps(c, inputs)
    res = run_bass_kernel_spmd(nc, maps, core_ids=list(range(8)))
    return np.stack([np.asarray(r["out"], dtype=np.float32) for r in res.results], axis=0)
```

```python
import contextlib
import math
import numpy as np
import concourse.bass as bass
import concourse.mybir as mybir
from concourse.bass_utils import run_bass_kernel_spmd

F32 = mybir.dt.float32
BF16 = mybir.dt.bfloat16
AF = mybir.ActivationFunctionType
ALU = mybir.AluOpType
AX = mybir.AxisListType

ENGS = ["tensor", "vector", "scalar", "gpsimd", "sync"]
EPOCH = 16000
NEPOCH = 6
NDMA_SEM = 24


def I(method, *a, **k):
    return lambda e: getattr(e, method)(*a, **k)


class Op:
    __slots__ = ("eng", "fn", "reads", "writes", "dma", "idx", "waits", "signal",
                 "seq", "dsem", "dval", "final", "bar")


class Sched:
    def __init__(self, nc, es):
        self.nc = nc
        self.ops = []
        self.esem = {(e, i): es.enter_context(nc.semaphore(f"s_{e}_{i}")) for e in ENGS for i in range(NEPOCH)}
        self.dsem = [es.enter_context(nc.semaphore(f"d_{i}")) for i in range(NDMA_SEM)]
        self.bsem = {e: es.enter_context(nc.semaphore(f"b_{e}")) for e in ENGS}
        self.cnt = {e: 0 for e in ENGS}
        self.duse = [0] * NDMA_SEM
        self.dn = 0
        self.nphase = 0
        self.barrier_fns = None
        self.final_ops = []
        self.alias = {}
        self.limit = None
        self.suffix = None
        self.shared = set()

    def kmap(self, k):
        k = self.alias.get(k, k)
        if self.suffix is None:
            return k
        b_ = k[0] if isinstance(k, tuple) else k
        if b_ in self.shared:
            return k
        return ("S", self.suffix, k)

    def add(self, eng, fn, reads=(), writes=(), dma=False, final=False):
        op = Op()
        if self.limit is not None and len(self.ops) >= self.limit:
            op.writes = ()
            return op
        op.eng = eng; op.fn = fn
        op.reads = tuple(self.kmap(k) for k in reads)
        op.writes = tuple(self.kmap(k) for k in writes)
        op.dma = dma; op.idx = len(self.ops); op.waits = []
        op.signal = False; op.seq = None; op.dsem = None; op.dval = None
        op.final = final
        op.bar = False
        self.ops.append(op)
        return op

    def pe(self, fn, reads=(), writes=()): return self.add("tensor", fn, reads, writes)
    def dve(self, fn, reads=(), writes=()): return self.add("vector", fn, reads, writes)
    def act(self, fn, reads=(), writes=()): return self.add("scalar", fn, reads, writes)
    def pool(self, fn, reads=(), writes=()): return self.add("gpsimd", fn, reads, writes)
    def dma(self, fn, reads=(), writes=(), final=False):
        return self.add("sync", fn, reads, writes, dma=True, final=final)

    def analyze(self):
        lw = {}; rd = {}
        ops = self.ops
        for op in ops:
            deps = set()
            for k in op.reads:
                w = lw.get(k)
                if w is not None:
                    deps.add(w)
            for k in op.writes:
                w = lw.get(k)
                if w is not None:
                    deps.add(w)
                for r in rd.get(k, ()):
                    deps.add(r)
            for k in op.reads:
                rd.setdefault(k, []).append(op.idx)
            for k in op.writes:
                lw[k] = op.idx
                rd[k] = []
            deps.discard(op.idx)
            best = {}; dmas = []
            for d in deps:
                p = ops[d]
                if p.dma:
                    dmas.append(d)
                    continue
                if p.eng == op.eng and not op.dma:
                    if op.eng == "tensor":
                        continue
                if p.eng not in best or best[p.eng] < d:
                    best[p.eng] = d
            if op.bar and op.eng == "tensor":
                for e2 in ("vector", "scalar", "gpsimd"):
                    lastop = [o2.idx for o2 in ops if o2.eng == e2 and not o2.bar and not o2.dma]
                    if lastop and (e2 not in best or best[e2] < lastop[-1]):
                        best[e2] = lastop[-1]
            op.waits = sorted(best.values()) + sorted(dmas)
            for d in op.waits:
                ops[d].signal = True

    def flush(self, last=False):
        nc = self.nc
        if self.limit is not None:
            print("phase", self.nphase, "nops", len(self.ops), flush=True)
        self.limit = None
        for e_ in ENGS:
            bop = self.add(e_, self.barrier_fns[e_], reads=["dmy" + e_],
                           writes=["bar" + e_] + (["ppv", "prk", "pYr", "pHr"] if e_ == "tensor" else []))
            bop.bar = True
        self.analyze()
        ops = self.ops
        for op in ops:
            if op.bar:
                continue
            if op.dma:
                s = self.dn % NDMA_SEM
                self.dn += 1
                self.duse[s] += 1
                op.dsem = s
                op.dval = 16 * self.duse[s]
            elif op.signal:
                self.cnt[op.eng] += 1
                op.seq = self.cnt[op.eng]
                assert op.seq <= EPOCH * NEPOCH
        phase = self.nphase
        self.nphase += 1
        dsem_end = list(self.duse)
        with nc.Block() as block:
            def make(engname):
                def body(e):
                    waited = {}
                    if phase > 0:
                        for pe_ in ENGS:
                            e.wait_ge(self.bsem[pe_], phase * (16 if pe_ == "sync" else 1))
                    for op in ops:
                        if op.eng != engname:
                            continue
                        for d in op.waits:
                            p = ops[d]
                            if p.dma:
                                key = ("d", p.dsem); val = p.dval; sem = self.dsem[p.dsem]
                            else:
                                ep = (p.seq - 1) // EPOCH
                                key = ("e", p.eng, ep); val = p.seq - ep * EPOCH
                                sem = self.esem[(p.eng, ep)]
                            if waited.get(key, 0) >= val:
                                continue
                            waited[key] = val
                            e.wait_ge(sem, val)
                        if op.bar:
                            if engname == "sync":
                                for s in range(NDMA_SEM):
                                    if dsem_end[s] > 0 and waited.get(("d", s), 0) < 16 * dsem_end[s]:
                                        e.wait_ge(self.dsem[s], 16 * dsem_end[s])
                                op.fn(e).then_inc(self.bsem["sync"], 16)
                                if last:
                                    e.wait_ge(self.bsem["sync"], 16 * (phase + 1))
                            else:
                                op.fn(e).then_inc(self.bsem[engname], 1)
                        elif op.dma:
                            sem = self.dsem[op.dsem]
                            key = ("d", op.dsem)
                            if op.dval > 16 and waited.get(key, 0) < op.dval - 16:
                                e.wait_ge(sem, op.dval - 16)
                                waited[key] = op.dval - 16
                            op.fn(e).then_inc(sem, 16)
                        else:
                            ins = op.fn(e)
                            if op.signal:
                                ep = (op.seq - 1) // EPOCH
                                ins.then_inc(self.esem[(op.eng, ep)], 1)
                return body
            for engname in ENGS:
                getattr(block, engname)(make(engname))
        self.ops = []
        self.alias = {}
        self.shared = set()


class Cfg:
    def __init__(s, S=2048, D=2048, HM=8, QR=512, KVR=512, HR=16):
        s.S = S; s.D = D; s.HM = HM; s.QR = QR; s.KVR = KVR; s.HR = HR
        s.MW = HM * 128; s.RW = HR * 64; s.NP = s.RW // 128
        s.MLA_IN = QR + KVR + 64; s.RWKV_IN = 3 * s.RW + 4 * 96
        s.D_IN = s.MLA_IN + s.RWKV_IN + s.MW + s.RW + 2 * D
        s.TG = min(512, S); s.NTG = S // s.TG; s.TT = S // 128; s.DC = D // 128
        s.QRC = QR // 128; s.KVRC = KVR // 128; s.MWC = s.MW // 128
        s.c_qa = 0; s.c_kva = QR; s.c_kr = QR + KVR
        s.c_r = s.MLA_IN; s.c_k = s.c_r + s.RW; s.c_v = s.c_k + s.RW; s.c_lora = s.c_v + s.RW
        s.c_zm = s.MLA_IN + s.RWKV_IN; s.c_zr = s.c_zm + s.MW
        s.c_gm = s.c_zr + s.RW; s.c_gr = s.c_gm + D
        s.CK = s.TG // 128


VEC_NAMES = ["g_pre", "mla_q_norm", "mla_kv_norm", "rwkv_mu", "rwkv_w0_f", "rwkv_w0_b", "rwkv_a0_f",
             "rwkv_a0_b", "rwkv_k_k", "rwkv_k_a", "rwkv_r_k", "rwkv_gn_g", "rwkv_gn_b", "g_post"]
MAT_SHAPES = lambda c: {
    "w_in": [c.D, c.D_IN], "mla_wq_b": [c.QR, c.HM * 192], "mla_wkv_b": [c.KVR, c.HM * 256],
    "rwkv_w2_f": [96, c.RW], "rwkv_w2_b": [96, c.RW], "rwkv_a2_f": [96, c.RW], "rwkv_a2_b": [96, c.RW],
    "w_br_mla": [c.MW, c.D], "w_br_rwkv": [c.RW, c.D], "w_out": [c.D, c.D]}
VEC_LENS = lambda c: {
    "g_pre": c.D, "mla_q_norm": c.QR, "mla_kv_norm": c.KVR, "rwkv_mu": c.RWKV_IN, "rwkv_w0_f": c.RW,
    "rwkv_w0_b": c.RW, "rwkv_a0_f": c.RW, "rwkv_a0_b": c.RW, "rwkv_k_k": c.RW, "rwkv_k_a": c.RW,
    "rwkv_r_k": c.RW, "rwkv_gn_g": c.RW, "rwkv_gn_b": c.RW, "g_post": c.D}

C0 = math.exp(-0.5)
DEBUG_LIMIT = None
NORM_EPS = 1e-6
GN_EPS = 64e-5


def bc_last(ap3, n):
    pat = [list(x) for x in ap3.ap]
    pat[-1] = [0, n]
    return bass.AP(ap3.tensor, ap3.offset, pat)


def build(c: Cfg, stop=None):
    nc = bass.Bass("TRN2", target_bir_lowering=False)
    S_, D, TG, NTG, TT, DC = c.S, c.D, c.TG, c.NTG, c.TT, c.DC
    dr = {}
    dr["x"] = nc.dram_tensor("x", [S_, D], F32, kind="ExternalInput").ap()
    for n, sh in MAT_SHAPES(c).items():
        dr[n] = nc.dram_tensor(n, sh, F32, kind="ExternalInput").ap()
    for n, ln in VEC_LENS(c).items():
        dr[n] = nc.dram_tensor(n, [ln], F32, kind="ExternalInput").ap()
    dr["cosT"] = nc.dram_tensor("cosT", [64, S_], F32, kind="ExternalInput").ap()
    dr["sinT"] = nc.dram_tensor("sinT", [64, S_], F32, kind="ExternalInput").ap()
    out = nc.dram_tensor("out", [S_, D], F32, kind="ExternalOutput").ap()

    def scr(name, shape, dt):
        return nc.dram_tensor(name, shape, dt, kind="Internal").ap()
    qTn = scr("qTn", [c.HM, 128, S_], BF16); qTr = scr("qTr", [c.HM, 64, S_], BF16)
    kTn = scr("kTn", [c.HM, 128, S_], BF16); krTd = scr("krTd", [64, S_], BF16)
    Vtm = scr("Vtm", [c.HM, S_, 128], BF16)
    ymz = scr("ymz", [c.MWC, 128, S_], BF16); yrz = scr("yrz", [c.NP, 128, S_], BF16)
    mrg = scr("mrg", [DC, 128, S_], BF16)
    rinT = scr("rinT", [3 * c.NP + 1, 128, S_], F32)
    kknD = scr("kknD", [c.NP, 128, S_], F32)
    lwT = scr("lwT", [4, 96, S_], BF16)
    szrD = scr("szrD", [c.NP, 128, S_], BF16)
    dummyD = scr("dummyD", [1, 16], F32)

    with contextlib.ExitStack() as es:
        S = Sched(nc, es)

        uniq = [0]

        def sb(name, shape, dt, stack=es):
            uniq[0] += 1
            return stack.enter_context(nc.sbuf_tensor(f"{name}_u{uniq[0]}", shape, dt))

        def ps(name, shape, dt, stack=es):
            return stack.enter_context(nc.psum_tensor(name, shape, dt))

        ident = sb("ident", [128, 128], BF16)
        identf = sb("identf", [128, 128], F32)
        ones128 = sb("ones128", [128, 128], BF16)
        blk1 = sb("blk1", [128, 128], BF16)
        bdmask = sb("bdmask", [128, 128], F32)
        mU = sb("mU", [128, 128], BF16); mSU = sb("mSU", [128, 128], BF16)
        mL = sb("mL", [128, 128], BF16); mSL = sb("mSL", [128, 128], BF16)
        Mf = sb("Mf", [128, 2, 256], BF16); Mb = sb("Mb", [128, 2, 256], BF16)
        Nf = sb("Nf", [128, 2, 128], BF16); Nb = sb("Nb", [128, 2, 128], BF16)
        scanmask = sb("scanmask", [128, TG], F32)
        RT = sb("RT", [64, 64], F32); RTa = sb("RTa", [64, 64], F32)
        pvst = sb("pvst", [128, 128], F32)
        pv = sb("pv", [128, 128], F32); pvo = sb("pvo", [128, 128], F32); pvh = sb("pvh", [128, 128], F32)
        dmy = {e: sb(f"dmy_{e}", [128, 8], F32) for e in ENGS}
        BK = [ps(f"bank{i}", [128, 512], F32) for i in range(8)]
        pdm = BK[7][:, 504:512]
        S.barrier_fns = {
            "vector": I("memset", dmy["vector"][:, 0:1], 0.0),
            "gpsimd": I("memset", dmy["gpsimd"][:, 0:1], 0.0),
            "scalar": I("activation", out=dmy["scalar"][:, 0:1], in_=dmy["scalar"][:, 1:2], func=AF.Copy),
            "tensor": I("matmul", pdm[0:1, 0:1], lhsT=identf[0:1, 0:1], rhs=identf[0:1, 0:1], start=True, stop=True),
            "sync": I("dma_start", out=dummyD[0:1, 0:8], in_=dmy["sync"][0:1, 0:8]),
        }

        def tri(t, pat, cm, op, key):
            S.pool(I("memset", t[:], 1.0), writes=[key])
            S.pool(I("affine_select", out=t[:], in_=t[:], pattern=pat, compare_op=op, fill=0.0, base=0,
                     channel_multiplier=cm), reads=[key], writes=[key])
        for e_ in ENGS:
            S.pool(I("memset", dmy[e_][:], 0.0), writes=["dmy" + e_])
        tri(ident, [[-1, 128]], 1, ALU.is_equal, "ident")
        tri(identf, [[-1, 128]], 1, ALU.is_equal, "identf")
        tri(mU, [[1, 128]], -1, ALU.is_ge, "mU"); tri(mSU, [[1, 128]], -1, ALU.is_gt, "mSU")
        tri(mL, [[-1, 128]], 1, ALU.is_ge, "mL"); tri(mSL, [[-1, 128]], 1, ALU.is_gt, "mSL")
        S.pool(I("memset", ones128[:], 1.0), writes=["ones128"])
        for t, key in ((blk1, "blk1"), (bdmask, "bdmask")):
            S.pool(I("memset", t[:], 0.0), writes=[key])
            S.pool(I("memset", t[0:64, 0:64], 1.0), writes=[key])
            S.pool(I("memset", t[64:128, 64:128], 1.0), writes=[key])
        for h in range(2):
            S.pool(I("tensor_copy", out=Mf[:, h, 0:128], in_=mSU[:]), reads=["mSU"], writes=["Mf"])
            S.pool(I("tensor_copy", out=Mf[:, h, 128:256], in_=mU[:]), reads=["mU"], writes=["Mf"])
            S.pool(I("tensor_copy", out=Mb[:, h, 0:128], in_=mSL[:]), reads=["mSL"], writes=["Mb"])
            S.pool(I("tensor_copy", out=Mb[:, h, 128:256], in_=mL[:]), reads=["mL"], writes=["Mb"])
            S.pool(I("tensor_copy", out=Nf[:, h, :], in_=mSL[:]), reads=["mSL"], writes=["Nf"])
            S.pool(I("tensor_copy", out=Nb[:, h, :], in_=mSU[:]), reads=["mSU"], writes=["Nb"])
        S.pool(I("memset", scanmask[:], 1.0), writes=["scanmask"])
        for ck in range(c.CK):
            S.pool(I("memset", scanmask[:, ck * 128:ck * 128 + 1], 0.0), writes=["scanmask"])
        S.pool(I("memset", RTa[:], 1.0), writes=["RTa"])
        S.pool(I("affine_select", out=RTa[:], in_=RTa[:], pattern=[[-1, 64]], compare_op=ALU.is_equal, fill=0.0,
                 base=-32, channel_multiplier=1), reads=["RTa"], writes=["RTa"])
        S.pool(I("memset", RT[:], 1.0), writes=["RT"])
        S.pool(I("affine_select", out=RT[:], in_=RT[:], pattern=[[1, 64]], compare_op=ALU.is_equal, fill=0.0,
                 base=-32, channel_multiplier=-1), reads=["RT"], writes=["RT"])
        S.pool(I("tensor_tensor", out=RT[:], in0=RT[:], in1=RTa[:], op=ALU.subtract), reads=["RT", "RTa"], writes=["RT"])

        S.pool(I("memset", pvst[:], 0.0), writes=["pvst"])
        col = {}
        nrow = [0]

        def vec_rows(name, ap, n, L=128):
            col[name] = nrow[0]
            S.dma(I("dma_start", out=pvst[nrow[0]:nrow[0] + n, 0:L], in_=ap.rearrange("(c p) -> c p", p=L)),
                  reads=["pvst0"], writes=["pvst"])
            nrow[0] += n
        S.ops[-1].writes = ("pvst", "pvst0")
        vec_rows("g_pre", dr["g_pre"], DC)
        vec_rows("gq", dr["mla_q_norm"], c.QRC)
        vec_rows("gkv", dr["mla_kv_norm"], c.KVRC)
        vec_rows("mu", dr["rwkv_mu"][0:3 * c.RW], 3 * c.NP)
        vec_rows("mul", dr["rwkv_mu"][3 * c.RW:3 * c.RW + 384], 4, L=96)
        for nm in ["w0_f", "w0_b", "a0_f", "a0_b", "k_k", "k_a", "r_k", "gn_g", "gn_b"]:
            vec_rows(nm, dr["rwkv_" + nm], c.NP)
        assert nrow[0] <= 128
        ppv = BK[7][:, 0:128]
        S.pe(I("matmul", ppv[:], lhsT=pvst[:], rhs=identf[:], start=True, stop=True), reads=["pvst", "identf"], writes=["ppv"])
        S.dve(I("tensor_copy", out=pv[:], in_=ppv[:]), reads=["ppv"], writes=["pv"])
        S.dve(I("tensor_scalar", out=pvo[:], in0=pv[:], scalar1=-1.0, scalar2=1.0, op0=ALU.mult, op1=ALU.add), reads=["pv"], writes=["pvo"])
        S.dve(I("tensor_scalar", out=pvh[:], in0=pv[:], scalar1=0.5, scalar2=None, op0=ALU.mult), reads=["pv"], writes=["pvh"])
        S.flush()
        if stop is not None and S.nphase >= stop:
            return nc

        def pcol(name, i=0):
            return pv[:, col[name] + i:col[name] + i + 1]

        hstack = contextlib.ExitStack()
        es.enter_context(hstack)
        hT = sb("hT", [128, DC, S_], BF16, hstack)
        hts = {"hT": hT}
        hTd = nc.dram_tensor("hTd", [DC, 128, S_], BF16, kind="Internal").ap()
        stg = {}

        def alloc_stg():
            stg["stack"] = contextlib.ExitStack()
            es.enter_context(stg["stack"])
            stg["wst"] = [sb(f"wst{i}", [128, DC, 128], F32, stg["stack"]) for i in range(2)]
            stg["wbf"] = [sb(f"wbf{i}", [128, DC, 128], BF16, stg["stack"]) for i in range(2)]
        alloc_stg()
        pin = [BK[i][:, 0:TG] for i in range(2)]
        st = {"w": 0, "p": 0, "alt": 0}

        def alt_evac():
            st["alt"] ^= 1
            return st["alt"]

        def wload(col0, ncols):
            wst = stg["wst"]; wbf = stg["wbf"]
            sl = st["w"]; st["w"] ^= 1
            S.dma(I("dma_start", out=wst[sl][:, :, 0:ncols],
                    in_=dr["w_in"][:, col0:col0 + ncols].rearrange("(dc p) n -> p dc n", p=128)),
                  writes=[("wst", sl)])
            S.pool(I("tensor_copy", out=wbf[sl][:, :, 0:ncols], in_=wst[sl][:, :, 0:ncols]),
                   reads=[("wst", sl)], writes=[("wbf", sl)])
            st["pref"] = (col0, ncols, sl)

        def inproj(col0, ncols, consume, nxt=None):
            wbf = stg["wbf"]
            pf = st.get("pref")
            if pf is None or pf[0] != col0 or pf[1] != ncols:
                wload(col0, ncols)
                pf = st["pref"]
            sl = pf[2]
            st["pref"] = None
            if nxt is not None:
                wload(nxt[0], nxt[1])
            for tg in range(NTG):
                b = st["p"]; st["p"] ^= 1
                for dc in range(DC):
                    S.pe(I("matmul", pin[b][0:ncols, :], lhsT=wbf[sl][:, dc, 0:ncols],
                           rhs=hts["hT"][:, dc, tg * TG:(tg + 1) * TG], start=(dc == 0), stop=(dc == DC - 1)),
                         reads=[("wbf", sl), "hT"], writes=[("pin", b)])
                consume(tg, pin[b][0:ncols, :], ("pin", b))

        with contextlib.ExitStack() as ph:
            xt = [sb(f"xt{i}", [128, D], F32, ph) for i in range(2)]
            xs = [sb(f"xs{i}", [128, D], BF16, ph) for i in range(2)]
            junk = sb("junkA", [128, D], BF16, ph)
            sta = sb("sta", [128, TT, 4], F32, ph)
            pT = [BK[2 + i][:].bitcast(BF16)[:, 0:512].rearrange("p (j t) -> p j t", t=128) for i in range(2)]
            r_ = 0
            for tt in range(TT):
                sl = tt % 2
                S.dma(I("dma_start", out=xt[sl][:], in_=dr["x"][tt * 128:(tt + 1) * 128, :]), writes=[("xt", sl)])
                S.act(I("activation", out=junk[:], in_=xt[sl][:], func=AF.Square, accum_out=sta[:, tt, 0:1]),
                      reads=[("xt", sl)], writes=["junk", ("sta", tt)])
                S.dve(I("tensor_scalar", out=sta[:, tt, 1:2], in0=sta[:, tt, 0:1], scalar1=1.0 / D, scalar2=NORM_EPS,
                        op0=ALU.mult, op1=ALU.add), reads=[("sta", tt)], writes=[("sta", tt)])
                S.act(I("activation", out=sta[:, tt, 2:3], in_=sta[:, tt, 1:2], func=AF.Sqrt), reads=[("sta", tt)], writes=[("sta", tt)])
                S.dve(I("reciprocal", out=sta[:, tt, 3:4], in_=sta[:, tt, 2:3]), reads=[("sta", tt)], writes=[("sta", tt)])
                S.dve(I("tensor_scalar", out=xs[sl][:], in0=xt[sl][:], scalar1=sta[:, tt, 3:4], scalar2=None, op0=ALU.mult),
                      reads=[("xt", sl), ("sta", tt)], writes=[("xs", sl)])
                for g in range(0, DC, 4):
                    n = min(4, DC - g)
                    pb = r_ % 2; r_ += 1
                    for j in range(n):
                        S.pe(I("transpose", out=pT[pb][:, j, :], in_=xs[sl][:, (g + j) * 128:(g + j + 1) * 128], identity=ident[:]),
                             reads=[("xs", sl), "ident"], writes=[("pTA", pb)])
                    for j in range(n):
                        fn = I("tensor_scalar", out=hT[:, g + j, tt * 128:(tt + 1) * 128], in0=pT[pb][:, j, :],
                               scalar1=pcol("g_pre", g + j), scalar2=None, op0=ALU.mult)
                        (S.dve if (j % 2 == 0) else S.pool if False else S.dve)(fn, reads=[("pTA", pb), "pv"], writes=[("hT", tt, g + j)])
            S.dma(I("dma_start", out=hTd.rearrange("c p s -> p c s"), in_=hT[:]),
                  reads=[("hT", t_, g_) for t_ in range(TT) for g_ in range(DC)], writes=["hTd"])
            S.flush()
            if stop is not None and S.nphase >= stop:
                return nc

        def rms_rstd(ph, srcT, nchunk, rdim, rbc, tagp, skey, rkey):
            sq = [sb(f"sq{tagp}{i}", [128, S_], BF16, ph) for i in range(2)]
            pss = [BK[2 + i][:, 0:TG] for i in range(NTG)]
            for cc in range(nchunk):
                S.act(I("activation", out=sq[cc % 2][:], in_=srcT[:, cc, :], func=AF.Square),
                      reads=[(skey, cc)], writes=[("sq" + tagp, cc % 2)])
                for tg in range(NTG):
                    S.pe(I("matmul", pss[tg][:], lhsT=ones128[:], rhs=sq[cc % 2][:, tg * TG:(tg + 1) * TG],
                           start=(cc == 0), stop=(cc == nchunk - 1)), reads=[("sq" + tagp, cc % 2)], writes=[("pss" + tagp, tg)])
            for tg in range(NTG):
                sl_ = rbc[:, tg * TG:(tg + 1) * TG]
                S.dve(I("tensor_scalar", out=sl_, in0=pss[tg][:], scalar1=1.0 / rdim, scalar2=NORM_EPS, op0=ALU.mult, op1=ALU.add),
                      reads=[("pss" + tagp, tg)], writes=[(rkey, tg)])
                S.act(I("activation", out=sl_, in_=sl_, func=AF.Sqrt), reads=[(rkey, tg)], writes=[(rkey, tg)])
                S.dve(I("reciprocal", out=sl_, in_=sl_), reads=[(rkey, tg)], writes=[(rkey, tg)])

        def rope(ph_tiles, src32, tg, outbf, okey, skey):
            prot, t1, t2, cosS, sinS = ph_tiles
            S.pe(I("matmul", prot[0:64, :], lhsT=RT[:], rhs=src32, start=True, stop=True), reads=[skey, "RT"], writes=["prot"])
            S.pool(I("tensor_tensor", out=t1[0:64, :], in0=src32, in1=cosS[:, tg * TG:(tg + 1) * TG], op=ALU.mult),
                   reads=[skey, "cosS"], writes=["ropet1"])
            S.dve(I("tensor_tensor", out=t2[0:64, :], in0=prot[0:64, :], in1=sinS[:, tg * TG:(tg + 1) * TG], op=ALU.mult),
                  reads=["prot", "sinS"], writes=["ropet2"])
            S.dve(I("tensor_tensor", out=outbf, in0=t1[0:64, :], in1=t2[0:64, :], op=ALU.add),
                  reads=["ropet1", "ropet2"], writes=[okey])

        scale = (128 + 64) ** -0.5
        with contextlib.ExitStack() as ph:
            S.alias = {"pq0": ("bank", 2), "pq1": ("bank", 3), "prot": ("bank", 4)}
            S.alias.update({("pssq", i): ("bank", 2 + i) for i in range(NTG)})
            qaT = sb("qaT", [128, c.QRC, S_], BF16, ph)
            rq = sb("rq", [128, S_], F32, ph)
            cosS = sb("cosS", [64, S_], F32, ph); sinS = sb("sinS", [64, S_], F32, ph)
            S.dma(I("dma_start", out=cosS[:], in_=dr["cosT"]), writes=["cosS"])
            S.dma(I("dma_start", out=sinS[:], in_=dr["sinT"]), writes=["sinS"])
            for cc in range(c.QRC):
                def cons(tg, p_, pk, cc=cc):
                    S.act(I("activation", out=qaT[:, cc, tg * TG:(tg + 1) * TG], in_=p_, func=AF.Copy), reads=[pk], writes=[("qaT", cc)])
                inproj(c.c_qa + cc * 128, 128, cons, nxt=((c.c_qa + (cc + 1) * 128, 128) if cc + 1 < c.QRC else None))
            rms_rstd(ph, qaT, c.QRC, c.QR, rq, "q", "qaT", "rq")
            wqst = sb("wqst", [128, c.QRC, 192], F32, ph); wqbf = sb("wqbf", [128, c.QRC, 192], BF16, ph)
            pq = [BK[2 + i][:, 0:TG] for i in range(2)]
            prot = BK[4][:, 0:TG]
            t1 = sb("ropet1", [128, TG], F32, ph); t2 = sb("ropet2", [128, TG], F32, ph)
            q32 = sb("q32", [64, TG], F32, ph)
            qo = [sb(f"qo{i}", [128, TG], BF16, ph) for i in range(2)]
            qro = [sb(f"qro{i}", [64, TG], BF16, ph) for i in range(2)]
            k_ = 0
            for h in range(c.HM):
                S.dma(I("dma_start", out=wqst[:], in_=dr["mla_wq_b"][:, h * 192:(h + 1) * 192].rearrange("(c p) n -> p c n", p=128)), writes=["wqst"])
                for cc in range(c.QRC):
                    S.pool(I("tensor_scalar", out=wqbf[:, cc, :], in0=wqst[:, cc, :], scalar1=pcol("gq", cc), scalar2=None, op0=ALU.mult),
                           reads=["wqst", "pv"], writes=["wqbf"])
                for tg in range(NTG):
                    b = k_ % 2; k_ += 1
                    tsl = slice(tg * TG, (tg + 1) * TG)
                    for cc in range(c.QRC):
                        S.pe(I("matmul", pq[0][:], lhsT=wqbf[:, cc, 0:128], rhs=qaT[:, cc, tsl], start=(cc == 0), stop=(cc == c.QRC - 1)),
                             reads=["wqbf", ("qaT", cc)], writes=["pq0"])
                    S.dve(I("scalar_tensor_tensor", out=qo[b][:], in0=pq[0][:], scalar=scale, in1=rq[:, tsl], op0=ALU.mult, op1=ALU.mult),
                          reads=["pq0", ("rq", tg)], writes=[("qo", b)])
                    S.dma(I("dma_start", out=qTn[h, :, tsl], in_=qo[b][:]), reads=[("qo", b)], writes=[("qTn", h)])
                    for cc in range(c.QRC):
                        S.pe(I("matmul", pq[1][0:64, :], lhsT=wqbf[:, cc, 128:192], rhs=qaT[:, cc, tsl], start=(cc == 0), stop=(cc == c.QRC - 1)),
                             reads=["wqbf", ("qaT", cc)], writes=["pq1"])
                    S.dve(I("scalar_tensor_tensor", out=q32[:], in0=pq[1][0:64, :], scalar=scale, in1=rq[0:64, tsl], op0=ALU.mult, op1=ALU.mult),
                          reads=["pq1", ("rq", tg)], writes=["q32"])
                    rope((prot, t1, t2, cosS, sinS), q32[:], tg, qro[b][:], ("qro", b), "q32")
                    S.dma(I("dma_start", out=qTr[h, :, tsl], in_=qro[b][:]), reads=[("qro", b)], writes=[("qTr", h)])
            S.flush()
            if stop is not None and S.nphase >= stop:
                return nc

        with contextlib.ExitStack() as ph:
            S.alias = {"pk": ("bank", 2), "pvv": ("bank", 3)}
            S.alias.update({("psskv", i): ("bank", 2 + i) for i in range(NTG)})
            kvaT = sb("kvaT", [128, c.KVRC, S_], BF16, ph)
            rkv = sb("rkv", [128, S_], F32, ph)
            rkt = sb("rkt", [128, TT], F32, ph)
            cosS = sb("cosS2", [64, S_], F32, ph); sinS = sb("sinS2", [64, S_], F32, ph)
            S.dma(I("dma_start", out=cosS[:], in_=dr["cosT"]), writes=["cosS"])
            S.dma(I("dma_start", out=sinS[:], in_=dr["sinT"]), writes=["sinS"])
            for cc in range(c.KVRC):
                def cons(tg, p_, pk, cc=cc):
                    S.act(I("activation", out=kvaT[:, cc, tg * TG:(tg + 1) * TG], in_=p_, func=AF.Copy), reads=[pk], writes=[("kvaT", cc)])
                inproj(c.c_kva + cc * 128, 128, cons, nxt=((c.c_kva + (cc + 1) * 128, 128) if cc + 1 < c.KVRC else None))
            rms_rstd(ph, kvaT, c.KVRC, c.KVR, rkv, "kv", "kvaT", "rkv")
            prk = BK[7][:, 128:128 + TT]
            for tt in range(TT):
                S.pe(I("matmul", prk[:, tt:tt + 1], lhsT=rkv[:, tt * 128:(tt + 1) * 128], rhs=identf[:, 0:1], start=True, stop=True),
                     reads=[("rkv", tt * 128 // TG), "identf"], writes=["prk"])
            S.dve(I("tensor_copy", out=rkt[:], in_=prk[:]), reads=["prk"], writes=["rkt"])
            kr32 = sb("kr32", [64, S_], F32, ph); krT = sb("krTs", [64, S_], BF16, ph)
            prot = BK[6][:, 0:TG]
            t1 = sb("ropet1b", [128, TG], F32, ph); t2 = sb("ropet2b", [128, TG], F32, ph)

            def conskr(tg, p_, pk):
                tsl = slice(tg * TG, (tg + 1) * TG)
                S.act(I("activation", out=kr32[:, tsl], in_=p_, func=AF.Copy), reads=[pk], writes=[("kr32", tg)])
                rope((prot, t1, t2, cosS, sinS), kr32[:, tsl], tg, krT[:, tsl], ("krT", tg), ("kr32", tg))
                S.dma(I("dma_start", out=krTd[:, tsl], in_=krT[:, tsl]), reads=[("krT", tg)], writes=["krTd"])
            inproj(c.c_kr, 64, conskr)
            wkst = sb("wkst", [128, c.KVRC, 256], F32, ph); wkbf = sb("wkbf", [128, c.KVRC, 256], BF16, ph)
            pk_ = BK[2][:, 0:TG]
            pv_ = BK[3][:].rearrange("p (j t) -> p j t", t=128)
            ko = [sb(f"ko{i}", [128, TG], BF16, ph) for i in range(2)]
            vo = [sb(f"vo{i}", [128, 4, 128], BF16, ph) for i in range(2)]
            k_ = 0
            for h in range(c.HM):
                S.dma(I("dma_start", out=wkst[:], in_=dr["mla_wkv_b"][:, h * 256:(h + 1) * 256].rearrange("(c p) n -> p c n", p=128)), writes=["wkst"])
                for cc in range(c.KVRC):
                    S.pool(I("tensor_scalar", out=wkbf[:, cc, :], in0=wkst[:, cc, :], scalar1=pcol("gkv", cc), scalar2=None, op0=ALU.mult),
                           reads=["wkst", "pv"], writes=["wkbf"])
                for tg in range(NTG):
                    b = k_ % 2; k_ += 1
                    tsl = slice(tg * TG, (tg + 1) * TG)
                    for cc in range(c.KVRC):
                        S.pe(I("matmul", pk_[:], lhsT=wkbf[:, cc, 0:128], rhs=kvaT[:, cc, tsl], start=(cc == 0), stop=(cc == c.KVRC - 1)),
                             reads=["wkbf", ("kvaT", cc)], writes=["pk"])
                    S.dve(I("tensor_tensor", out=ko[b][:], in0=pk_[:], in1=rkv[:, tsl], op=ALU.mult), reads=["pk", ("rkv", tg)], writes=[("ko", b)])
                    S.dma(I("dma_start", out=kTn[h, :, tsl], in_=ko[b][:]), reads=[("ko", b)], writes=[("kTn", h)])
                for g in range(0, TT, 4):
                    n = min(4, TT - g)
                    b = k_ % 2; k_ += 1
                    for j in range(n):
                        tt = g + j
                        for cc in range(c.KVRC):
                            S.pe(I("matmul", pv_[:, j, :], lhsT=kvaT[:, cc, tt * 128:(tt + 1) * 128], rhs=wkbf[:, cc, 128:256],
                                   start=(cc == 0), stop=(cc == c.KVRC - 1)), reads=["wkbf", ("kvaT", cc)], writes=["pvv"])
                    for j in range(n):
                        S.act(I("activation", out=vo[b][:, j, :], in_=pv_[:, j, :], func=AF.Copy, scale=rkt[:, g + j:g + j + 1]),
                              reads=["pvv", "rkt"], writes=[("vo", b)])
                    S.dma(I("dma_start", out=Vtm[h, g * 128:(g + n) * 128, :].rearrange("(j p) d -> p j d", p=128), in_=vo[b][:, 0:n, :]),
                          reads=[("vo", b)], writes=[("Vtm", h)])
            S.flush()
            if stop is not None and S.nphase >= stop:
                return nc

        with contextlib.ExitStack() as ph:
            krT = sb("krA", [64, S_], BF16, ph)
            S.dma(I("dma_start", out=krT[:], in_=krTd), writes=["krA"])
            qn = [sb(f"qnA{i}", [128, S_], BF16, ph) for i in range(2)]
            qr = [sb(f"qrA{i}", [64, S_], BF16, ph) for i in range(2)]
            kn = [sb(f"knA{i}", [128, S_], BF16, ph) for i in range(2)]
            vt = [sb(f"vtA{i}", [128, TT, 128], BF16, ph) for i in range(2)]
            szm = [sb(f"szm{i}", [128, S_], BF16, ph) for i in range(2)]
            pS = [BK[j_][:, 0:TG] for j_ in (2, 3, 6)]
            pO = BK[4][:, 0:TG]; pSum = BK[5][:, 0:TG]
            pt = [sb(f"ptA{i}", [128, TG], BF16, ph) for i in range(4)]
            rs = sb("rsA", [128, TG], F32, ph); y32 = sb("y32A", [128, TG], F32, ph)
            yo = [sb(f"yoA{i}", [128, TG], BF16, ph) for i in range(2)]
            kk_ = 0; yy_ = 0
            for h in range(c.HM):
                sl = h % 2
                S.dma(I("dma_start", out=qn[sl][:], in_=qTn[h]), writes=[("qn", sl)])
                S.dma(I("dma_start", out=qr[sl][:], in_=qTr[h]), writes=[("qr", sl)])
                S.dma(I("dma_start", out=kn[sl][:], in_=kTn[h]), writes=[("kn", sl)])
                S.dma(I("dma_start", out=vt[sl][:], in_=Vtm[h].rearrange("(j p) d -> p j d", p=128)), writes=[("vt", sl)])

                def consz(tg, p_, pk, sl=sl):
                    S.act(I("activation", out=szm[sl][:, tg * TG:(tg + 1) * TG], in_=p_, func=AF.Silu), reads=[pk], writes=[("szm", sl, tg)])
                inproj(c.c_zm + h * 128, 128, consz)
                for qg in range(NTG):
                    qsl = slice(qg * TG, (qg + 1) * TG)
                    pend = []
                    for kt in range(TT):
                        b = kk_ % 3; p3 = kk_ % 4; kk_ += 1
                        ksl = slice(kt * 128, (kt + 1) * 128)
                        S.pe(I("matmul", pS[b][:], lhsT=kn[sl][:, ksl], rhs=qn[sl][:, qsl], start=True, stop=False),
                             reads=[("kn", sl), ("qn", sl)], writes=[("pS", b)])
                        S.pe(I("matmul", pS[b][:], lhsT=krT[:, ksl], rhs=qr[sl][:, qsl], start=False, stop=True),
                             reads=["krA", ("qr", sl)], writes=[("pS", b)])
                        S.act(I("activation", out=pt[p3][:], in_=pS[b][:], func=AF.Exp), reads=[("pS", b)], writes=[("pt", p3)])
                        pend.append((kt, p3))
                        todo = []
                        if len(pend) > 2:
                            todo.append(pend.pop(0))
                        if kt == TT - 1:
                            todo += pend
                            pend = []
                        for (kt_, p3_) in todo:
                            S.pe(I("matmul", pO[:], lhsT=vt[sl][:, kt_, :], rhs=pt[p3_][:], start=(kt_ == 0), stop=(kt_ == TT - 1)),
                                 reads=[("vt", sl), ("pt", p3_)], writes=["pO"])
                            S.pe(I("matmul", pSum[:], lhsT=ones128[:], rhs=pt[p3_][:], start=(kt_ == 0), stop=(kt_ == TT - 1)),
                                 reads=[("pt", p3_)], writes=["pSum"])
                    yb = yy_ % 2; yy_ += 1
                    S.dve(I("reciprocal", out=rs[:], in_=pSum[:]), reads=["pSum"], writes=["rsA"])
                    S.dve(I("tensor_tensor", out=y32[:], in0=pO[:], in1=rs[:], op=ALU.mult), reads=["pO", "rsA"], writes=["y32A"])
                    S.pool(I("tensor_tensor", out=yo[yb][:], in0=y32[:], in1=szm[sl][:, qsl], op=ALU.mult),
                           reads=["y32A", ("szm", sl, qg)], writes=[("yo", yb)])
                    S.dma(I("dma_start", out=ymz[h, :, qsl], in_=yo[yb][:]), reads=[("yo", yb)], writes=["ymz"])
            S.flush()
            if stop is not None and S.nphase >= stop:
                return nc

        def lerp(raw, P, mucol, out32, outkey, rkey):
            tmpa, tmpb = lerp_t
            S.pool(I("tensor_tensor", out=tmpa[0:P, :], in0=raw[0:P, 0:S_], in1=raw[0:P, 2:S_ + 2], op=ALU.add), reads=[rkey], writes=["lta"])
            S.dve(I("tensor_scalar", out=tmpb[0:P, :], in0=raw[0:P, 1:S_ + 1], scalar1=pvo[0:P, mucol:mucol + 1], scalar2=None, op0=ALU.mult),
                  reads=[rkey, "pvo"], writes=["ltb"])
            S.dve(I("scalar_tensor_tensor", out=out32, in0=tmpa[0:P, :], scalar=pvh[0:P, mucol:mucol + 1], in1=tmpb[0:P, :], op0=ALU.mult, op1=ALU.add),
                  reads=["lta", "ltb", "pvh"], writes=[outkey])

        with contextlib.ExitStack() as ph:
            raws = [sb(f"raw{i}", [128, S_ + 2], F32, ph) for i in range(2)]
            lerp_t = (sb("lta", [128, S_], F32, ph), sb("ltb", [128, S_], F32, ph))
            o32 = [sb(f"o32R{i}", [128, S_], F32, ph) for i in range(2)]
            kk32 = sb("kk32", [128, S_], F32, ph); sqk = sb("sqk", [128, S_], BF16, ph)
            nrm = sb("nrmk", [128, TG], F32, ph)
            lwo = sb("lwo", [96, S_], BF16, ph)
            szo = [sb(f"szo{i}", [128, S_], BF16, ph) for i in range(2)]
            pkk = BK[2][:, 0:TG]
            for i_ in range(2):
                S.pool(I("memset", raws[i_][:], 0.0), writes=[("raw", i_)])
            rw = [0]
            oi = 0
            for i in range(4):
                rb = rw[0] % 2; rw[0] += 1
                raw = raws[rb]

                def consl(tg, p_, pk, raw=raw, rb=rb):
                    S.act(I("activation", out=raw[0:96, 1 + tg * TG:1 + (tg + 1) * TG], in_=p_, func=AF.Copy), reads=[pk], writes=[("raw", rb)])
                inproj(c.c_lora + i * 96, 96, consl, nxt=((c.c_lora + (i + 1) * 96, 96) if i < 3 else (c.c_r, 128)))
                ob = oi % 2; oi += 1
                lerp(raw, 96, col["mul"] + i, o32[ob][0:96, :], ("o32", ob), ("raw", rb))
                S.act(I("activation", out=lwo[:], in_=o32[ob][0:96, :], func=(AF.Tanh if i < 2 else AF.Copy)), reads=[("o32", ob)], writes=["lwo"])
                S.dma(I("dma_start", out=lwT[i], in_=lwo[:]), reads=["lwo"], writes=["lwT"])
            for hp in range(c.NP):
                for j, cbase in enumerate([c.c_r, c.c_k, c.c_v]):
                    rb = rw[0] % 2; rw[0] += 1
                    raw = raws[rb]

                    def consr(tg, p_, pk, raw=raw, rb=rb):
                        S.act(I("activation", out=raw[:, 1 + tg * TG:1 + (tg + 1) * TG], in_=p_, func=AF.Copy), reads=[pk], writes=[("raw", rb)])
                    nx_ = ([c.c_r, c.c_k, c.c_v][j + 1] + hp * 128, 128) if j < 2 else (c.c_zr + hp * 128, 128)
                    inproj(cbase + hp * 128, 128, consr, nxt=nx_)
                    ob = oi % 2; oi += 1
                    lerp(raw, 128, col["mu"] + j * c.NP + hp, o32[ob][:], ("o32", ob), ("raw", rb))
                    S.dma(I("dma_start", out=rinT[j * c.NP + hp], in_=o32[ob][:]), reads=[("o32", ob)], writes=["rinT"])
                    if j == 1:
                        S.dve(I("tensor_scalar", out=kk32[:], in0=o32[ob][:], scalar1=pcol("k_k", hp), scalar2=None, op0=ALU.mult),
                              reads=[("o32", ob), "pv"], writes=["kk32"])
                        S.act(I("activation", out=sqk[:], in_=kk32[:], func=AF.Square), reads=["kk32"], writes=["sqk"])
                        for tg in range(NTG):
                            tsl = slice(tg * TG, (tg + 1) * TG)
                            S.pe(I("matmul", pkk[:], lhsT=blk1[:], rhs=sqk[:, tsl], start=True, stop=True), reads=["sqk"], writes=["pkk"])
                            S.act(I("activation", out=nrm[:], in_=pkk[:], func=AF.Sqrt), reads=["pkk"], writes=["nrmk"])
                            S.dve(I("tensor_scalar", out=nrm[:], in0=nrm[:], scalar1=1e-12, scalar2=None, op0=ALU.max), reads=["nrmk"], writes=["nrmk"])
                            S.dve(I("reciprocal", out=nrm[:], in_=nrm[:]), reads=["nrmk"], writes=["nrmk"])
                            S.dve(I("tensor_tensor", out=kk32[:, tsl], in0=kk32[:, tsl], in1=nrm[:], op=ALU.mult), reads=["kk32", "nrmk"], writes=["kk32"])
                        S.dma(I("dma_start", out=kknD[hp], in_=kk32[:]), reads=["kk32"], writes=["kknD"])
                zb = hp % 2

                def conszr(tg, p_, pk, zb=zb):
                    S.act(I("activation", out=szo[zb][:, tg * TG:(tg + 1) * TG], in_=p_, func=AF.Silu), reads=[pk], writes=[("szo", zb)])
                inproj(c.c_zr + hp * 128, 128, conszr, nxt=((c.c_r + (hp + 1) * 128, 128) if hp + 1 < c.NP else None))
                S.dma(I("dma_start", out=szrD[hp], in_=szo[zb][:]), reads=[("szo", zb)], writes=["szrD"])
            S.flush()
            if stop is not None and S.nphase >= stop:
                return nc

        RTG = min(256, S_); RNTG = S_ // RTG; RCK = RTG // 128
        stg["stack"].close()
        hstack.close()
        for hp0 in range(0, c.NP, 2):
            hps = [hp_ for hp_ in (hp0, hp0 + 1) if hp_ < c.NP]
            pp = contextlib.ExitStack()
            PP = []
            for pi, hp in enumerate(hps):
                PP.append(dict(w2st=sb("w2st", [96, 4, 128], F32, pp), w2bf=sb("w2bf", [96, 4, 128], BF16, pp),
                               Ytm=sb("Ytm", [128, TT, 128], F32, pp), bon=sb("bon", [128, S_], F32, pp)))
            with contextlib.ExitStack() as ph:
                S.alias = {"pHr": "pW", "pYr": "pW", "pz": "pA", "pTr": "pA", "pB": "pA"}
                S.shared = {"Ytm", "bon", "w2bf"}
                for pi, hp in enumerate(hps):
                    for i, nm in enumerate(["rwkv_w2_f", "rwkv_w2_b", "rwkv_a2_f", "rwkv_a2_b"]):
                        S.dma(I("dma_start", out=PP[pi]["w2st"][:, i, :], in_=dr[nm][:, hp * 128:(hp + 1) * 128]), writes=[("w2st", pi)])
                    S.dve(I("tensor_copy", out=PP[pi]["w2bf"][:], in_=PP[pi]["w2st"][:]), reads=[("w2st", pi)], writes=[("w2bf", pi)])
                    S.pool(I("memset", PP[pi]["Ytm"][:], 0.0), writes=[("Ytm", pi, t_) for t_ in range(TT)])
                    S.pool(I("memset", PP[pi]["bon"][:], 0.0), writes=[("bon", pi, t_) for t_ in range(RNTG)])

                def stream(d, pi, hp, sidx):
                    w2bf = PP[pi]["w2bf"]; Ytm = PP[pi]["Ytm"]; bon = PP[pi]["bon"]
                    f32n = ["r", "k", "v", "kkn", "sg", "a", "cum", "tmp", "ex", "ep", "en", "eh", "ka", "kf", "pre"]
                    T = {n: sb("R_" + n, [128, RTG], F32, ph) for n in f32n}
                    lwd = sb("lwd", [96, RTG], BF16, ph); lad = sb("lad", [96, RTG], BF16, ph)
                    AR = sb("AR", [128, RCK, 256], BF16, ph)
                    ZA = sb("ZA", [128, 128], BF16, ph); ZV = sb("ZV", [128, 128], BF16, ph)
                    BtZ = sb("BtZ", [128, 2, RTG], BF16, ph); KtZ = sb("KtZ", [128, 2, RTG], BF16, ph)
                    S.pool(I("memset", BtZ[:], 0.0), writes=["Bt"])
                    S.pool(I("memset", KtZ[:], 0.0), writes=["Kt"])
                    Bh = sb("Bh", [128, RTG], BF16, ph); Kh = sb("Kh", [128, RTG], BF16, ph)
                    vb = sb("vb", [128, RTG], BF16, ph); prb = sb("prb", [128, RTG], BF16, ph)
                    bA = BK[2 * sidx]; bB = bA; bC = BK[2 * sidx + 1]
                    pz = bA[:, 0:RTG]
                    pA = bA[:].rearrange("p (h t) -> p h t", t=256)
                    pB = bB[:].rearrange("p (h t) -> p h t", t=256)
                    pW = bC[:, 0:256].rearrange("p (h t) -> p h t", t=128)
                    pTr = bB[:].bitcast(BF16)[:, 0:512].rearrange("p (j t) -> p j t", t=128)
                    pY = bC[:, 256:384]
                    pH = bC[:, 384:512]
                    TM = sb("TM", [128, 4, 128], BF16, ph)
                    NA = sb("NA", [128, 2, 256], BF16, ph)
                    KA = sb("KA", [128, 2, 256], BF16, ph)
                    XX = [sb(f"XX{i}", [128, 2, 2, 128], BF16, ph) for i in range(2)]
                    W = [sb(f"W{i}", [128, 2, 128], BF16, ph) for i in range(2)]
                    RhT = sb("RhT", [128, 128], BF16, ph)
                    MT = sb("MTbd", [128, 128], BF16, ph)
                    H32 = sb("H32", [128, 128], F32, ph); Hb = sb("Hb", [128, 128], BF16, ph)
                    Htmp = sb("Htmp", [128, 128], F32, ph)
                    ptot = sb("ptot", [128, RCK], F32, ph)
                    Mm = Mf if d == 0 else Mb
                    Nm = Nf if d == 0 else Nb
                    w0c = pcol("w0_f" if d == 0 else "w0_b", hp)
                    a0c = pcol("a0_f" if d == 0 else "a0_b", hp)
                    S.dve(I("memset", H32[:], 0.0), writes=["H32"])
                    S.dve(I("memset", Hb[:], 0.0), writes=["Hb"])
                    tgs = range(RNTG) if d == 0 else range(RNTG - 1, -1, -1)
                    for tg in tgs:
                        tsl = slice(tg * RTG, (tg + 1) * RTG)
                        S.dma(I("dma_start", out=T["r"][:], in_=rinT[0 * c.NP + hp, :, tsl]), writes=["R_r"])
                        S.dma(I("dma_start", out=T["k"][:], in_=rinT[1 * c.NP + hp, :, tsl]), writes=["R_k"])
                        S.dma(I("dma_start", out=T["v"][:], in_=rinT[2 * c.NP + hp, :, tsl]), writes=["R_v"])
                        S.dma(I("dma_start", out=T["kkn"][:], in_=kknD[hp, :, tsl]), writes=["R_kkn"])
                        S.dma(I("dma_start", out=lwd[:], in_=lwT[d, :, tsl]), writes=["lwd"])
                        S.dma(I("dma_start", out=lad[:], in_=lwT[2 + d, :, tsl]), writes=["lad"])
                        S.pe(I("matmul", pz[:], lhsT=w2bf[:, d, :], rhs=lwd[:], start=True, stop=True), reads=[("w2bf", pi), "lwd"], writes=["pz"])
                        S.act(I("activation", out=T["sg"][:], in_=pz[:], func=AF.Sigmoid, bias=w0c), reads=["pz", "pv"], writes=["R_sg"])
                        S.pe(I("matmul", pz[:], lhsT=w2bf[:, 2 + d, :], rhs=lad[:], start=True, stop=True), reads=[("w2bf", pi), "lad"], writes=["pz"])
                        S.act(I("activation", out=T["a"][:], in_=pz[:], func=AF.Sigmoid, bias=a0c), reads=["pz", "pv"], writes=["R_a"])
                        S.dve(I("tensor_tensor_scan", out=T["pre"][:], data0=scanmask[:, 0:RTG], data1=T["sg"][:], initial=0.0, op0=ALU.mult, op1=ALU.add),
                              reads=["R_sg", "scanmask"], writes=["R_pre"])
                        pre3 = T["pre"][:].rearrange("p (c t) -> p c t", t=128)
                        cum3 = T["cum"][:].rearrange("p (c t) -> p c t", t=128)
                        tot_bc = bc_last(pre3[:, :, 127:128], 128)
                        S.dve(I("tensor_copy", out=ptot[:].rearrange("p (c o) -> p c o", o=1), in_=pre3[:, :, 127:128]), reads=["R_pre"], writes=["ptot"])
                        if d == 0:
                            S.pool(I("tensor_copy", out=T["cum"][:], in_=T["pre"][:]), reads=["R_pre"], writes=["R_cum"])
                        else:
                            S.dve(I("tensor_tensor", out=cum3, in0=tot_bc, in1=pre3, op=ALU.subtract), reads=["R_pre"], writes=["R_cum"])
                            S.dve(I("tensor_tensor", out=T["cum"][:], in0=T["cum"][:], in1=T["sg"][:], op=ALU.add), reads=["R_cum", "R_sg"], writes=["R_cum"])
                        S.pool(I("tensor_tensor", out=T["tmp"][:], in0=T["cum"][:], in1=T["sg"][:], op=ALU.subtract), reads=["R_cum", "R_sg"], writes=["R_tmp"])
                        S.act(I("activation", out=T["ex"][:], in_=T["tmp"][:], func=AF.Exp, scale=-C0), reads=["R_tmp"], writes=["R_ex"])
                        S.act(I("activation", out=T["ep"][:], in_=T["cum"][:], func=AF.Exp, scale=-C0), reads=["R_cum"], writes=["R_ep"])
                        S.act(I("activation", out=T["en"][:], in_=T["cum"][:], func=AF.Exp, scale=C0), reads=["R_cum"], writes=["R_en"])
                        S.dve(I("tensor_tensor", out=T["tmp"][:].rearrange("p (c t) -> p c t", t=128), in0=tot_bc, in1=cum3, op=ALU.subtract),
                              reads=["R_pre", "R_cum", "R_ex"], writes=["R_tmp"])
                        S.act(I("activation", out=T["eh"][:], in_=T["tmp"][:], func=AF.Exp, scale=-C0), reads=["R_tmp"], writes=["R_eh"])
                        S.pool(I("tensor_tensor", out=T["ka"][:], in0=T["kkn"][:], in1=T["a"][:], op=ALU.mult), reads=["R_kkn", "R_a"], writes=["R_ka"])
                        S.dve(I("tensor_scalar", out=T["kf"][:], in0=T["a"][:], scalar1=pcol("k_a", hp), scalar2=pvo[:, col["k_a"] + hp:col["k_a"] + hp + 1],
                                op0=ALU.mult, op1=ALU.add), reads=["R_a", "pv", "pvo"], writes=["R_kf"])
                        S.dve(I("tensor_tensor", out=T["kf"][:], in0=T["kf"][:], in1=T["k"][:], op=ALU.mult), reads=["R_kf", "R_k"], writes=["R_kf"])
                        S.dve(I("scalar_tensor_tensor", out=prb[:], in0=T["kf"][:], scalar=pcol("r_k", hp), in1=T["r"][:], op0=ALU.mult, op1=ALU.mult),
                              reads=["R_kf", "R_r"], writes=["prb"])
                        S.pe(I("matmul", pz[:], lhsT=blk1[:], rhs=prb[:], start=True, stop=True), reads=["prb"], writes=["pz"])
                        S.dve(I("tensor_tensor", out=T["pre"][:], in0=pz[:], in1=T["v"][:], op=ALU.mult), reads=["pz", "R_v"], writes=["R_pre"])
                        S.pool(I("tensor_tensor", out=bon[:, tsl], in0=bon[:, tsl], in1=T["pre"][:], op=ALU.add), reads=["R_pre", ("bon", pi, tg)], writes=[("bon", pi, tg)])
                        S.dve(I("scalar_tensor_tensor", out=AR[:, :, 0:128], in0=T["kkn"][:].rearrange("p (c t) -> p c t", t=128), scalar=-1.0,
                                in1=T["ex"][:].rearrange("p (c t) -> p c t", t=128), op0=ALU.mult, op1=ALU.mult),
                              reads=["R_kkn", "R_ex"], writes=["AR"])
                        S.pool(I("tensor_tensor", out=AR[:, :, 128:256], in0=T["r"][:].rearrange("p (c t) -> p c t", t=128),
                                 in1=T["ep"][:].rearrange("p (c t) -> p c t", t=128), op=ALU.mult), reads=["R_r", "R_ep"], writes=["AR"])
                        for h in range(2):
                            hs = slice(h * 64, (h + 1) * 64)
                            S.dve(I("tensor_tensor", out=BtZ[hs, h, :], in0=T["ka"][hs, :], in1=T["en"][hs, :], op=ALU.mult), reads=["R_ka", "R_en"], writes=["Bt"])
                            S.pool(I("tensor_tensor", out=KtZ[hs, h, :], in0=T["kf"][hs, :], in1=T["en"][hs, :], op=ALU.mult), reads=["R_kf", "R_en"], writes=["Kt"])
                        S.dve(I("tensor_tensor", out=Bh[:], in0=T["ka"][:], in1=T["eh"][:], op=ALU.mult), reads=["R_ka", "R_eh"], writes=["Bh"])
                        S.pool(I("tensor_tensor", out=Kh[:], in0=T["kf"][:], in1=T["eh"][:], op=ALU.mult), reads=["R_kf", "R_eh"], writes=["Kh"])
                        S.act(I("activation", out=vb[:], in_=T["v"][:], func=AF.Copy), reads=["R_v"], writes=["vb"])
                        S.act(I("activation", out=ptot[:], in_=ptot[:], func=AF.Exp, scale=-C0), reads=["ptot"], writes=["ptot"])
                        cks = range(RCK) if d == 0 else range(RCK - 1, -1, -1)
                        for ck in cks:
                            csl = slice(ck * 128, (ck + 1) * 128)
                            tt = tg * RCK + ck
                            for j, src in enumerate([AR[:, ck, 0:128], Bh[:, csl], Kh[:, csl], vb[:, csl]]):
                                S.pe(I("transpose", out=pTr[:, j, :], in_=src, identity=ident[:]), reads=["AR", "Bh", "Kh", "vb", "ident"], writes=["pTr"])
                            S.act(I("activation", out=TM[:], in_=pTr[:], func=AF.Copy), reads=["pTr"], writes=["TM"])
                            for h in range(2):
                                S.pe(I("matmul", pA[:, h, :], lhsT=BtZ[:, h, csl], rhs=AR[:, ck, :], start=True, stop=True), reads=["Bt", "AR"], writes=["pA"])
                            S.dve(I("tensor_tensor", out=NA[:], in0=pA[:], in1=Mm[:], op=ALU.mult), reads=["pA"], writes=["NA"])
                            for h in range(2):
                                S.pe(I("matmul", pB[:, h, :], lhsT=KtZ[:, h, csl], rhs=AR[:, ck, :], start=True, stop=True), reads=["Kt", "AR"], writes=["pB"])
                            S.dve(I("tensor_tensor", out=KA[:], in0=pB[:], in1=Mm[:], op=ALU.mult), reads=["pB"], writes=["KA"])
                            for h in range(2):
                                hs = slice(h * 64, (h + 1) * 64)
                                S.pe(I("matmul", pW[:, h, :], lhsT=AR[:, ck, 0:128], rhs=BtZ[:, h, csl], start=True, stop=True), reads=["AR", "Bt"], writes=["pW"])
                            S.dve(I("tensor_tensor", out=XX[0][:, :, 0, :], in0=pW[:], in1=Nm[:], op=ALU.mult), reads=["pW"], writes=[("XX", 0)])
                            S.act(I("activation", out=XX[0][:, :, 1, :], in_=NA[:, :, 0:128], func=AF.Copy), reads=["NA"], writes=[("XX", 0)])
                            for h in range(2):
                                S.pe(I("matmul", pW[:, h, 64:128], lhsT=KA[:, h, 0:128], rhs=TM[:, 3, h * 64:(h + 1) * 64], start=True, stop=True),
                                     reads=["KA", "TM", ("XX", 0)], writes=["pW"])
                            S.act(I("activation", out=W[0][:, :, 64:128], in_=pW[:, :, 64:128], func=AF.Copy), reads=["pW"], writes=[("W", 0)])
                            S.dve(I("tensor_copy", out=W[0][:, :, 0:64], in_=TM[:, 0, :].rearrange("p (h j) -> p h j", j=64)), reads=["TM"], writes=[("W", 0)])
                            nlev = 7
                            for lv in range(nlev):
                                a_ = lv % 2; b_ = 1 - a_
                                for h in range(2):
                                    S.pe(I("matmul", pW[:, h, :], lhsT=ident[:], rhs=W[a_][:, h, :], start=True, stop=False), reads=[("W", a_), "ident"], writes=["pW"])
                                    S.pe(I("matmul", pW[:, h, :], lhsT=XX[a_][:, h, 1, :], rhs=W[a_][:, h, :], start=False, stop=True),
                                         reads=[("W", a_), ("XX", a_)], writes=["pW"])
                                S.act(I("activation", out=W[b_][:], in_=pW[:], func=AF.Copy), reads=["pW"], writes=[("W", b_)])
                                if lv < nlev - 1:
                                    pX = pA if lv % 2 == 0 else pB
                                    pXk = "pA" if lv % 2 == 0 else "pB"
                                    for h in range(2):
                                        S.pe(I("matmul", pX[:, h, 0:128], lhsT=XX[a_][:, h, 1, :], rhs=XX[a_][:, h, 0, :], start=True, stop=True),
                                             reads=[("XX", a_)], writes=[pXk])
                                        S.pe(I("matmul", pX[:, h, 128:256], lhsT=XX[a_][:, h, 0, :], rhs=XX[a_][:, h, 1, :], start=True, stop=True),
                                             reads=[("XX", a_)], writes=[pXk])
                                    S.dve(I("tensor_copy", out=XX[b_][:].rearrange("p h x t -> p h (x t)"), in_=pX[:]), reads=[pXk], writes=[("XX", b_)])
                            Z = W[nlev % 2]
                            zk = ("W", nlev % 2)
                            S.act(I("activation", out=ZA[:].rearrange("p (h j) -> p h j", j=64), in_=Z[:, :, 0:64], func=AF.Copy), reads=[zk], writes=["ZA"])
                            S.dve(I("tensor_copy", out=ZV[:].rearrange("p (h j) -> p h j", j=64), in_=Z[:, :, 64:128]), reads=[zk], writes=["ZV"])
                            for h in range(2):
                                S.pe(I("matmul", bA[:, h * 128:(h + 1) * 128], lhsT=ZA[:], rhs=NA[:, h, 128:256], start=True, stop=True), reads=["ZA", "NA"], writes=["pA"])
                            S.dve(I("tensor_tensor", out=RhT[0:64, :], in0=bA[0:64, 0:128], in1=AR[0:64, ck, 128:256], op=ALU.add), reads=["pA", "AR"], writes=["RhT"])
                            S.dve(I("tensor_tensor", out=RhT[64:128, :], in0=bA[64:128, 128:256], in1=AR[64:128, ck, 128:256], op=ALU.add), reads=["pA", "AR"], writes=["RhT"])
                            S.pe(I("matmul", pB[:, 0, 0:128], lhsT=ZA[:], rhs=TM[:, 1, :], start=True, stop=True), reads=["ZA", "TM"], writes=["pB"])
                            S.dve(I("tensor_tensor", out=MT[:], in0=pB[:, 0, 0:128], in1=bdmask[:], op=ALU.mult), reads=["pB", "bdmask"], writes=["MTbd"])
                            for h in range(2):
                                hs = slice(h * 64, (h + 1) * 64)
                                S.pe(I("matmul", pY[:, hs], lhsT=NA[:, h, 128:256], rhs=Z[:, h, 64:128], start=True, stop=False), reads=["NA", zk], writes=["pYr"])
                                S.pe(I("matmul", pY[:, hs], lhsT=KA[:, h, 128:256], rhs=TM[:, 3, hs], start=False, stop=False), reads=["KA", "TM"], writes=["pYr"])
                                S.pe(I("matmul", pY[:, hs], lhsT=RhT[:], rhs=Hb[:, hs], start=False, stop=True), reads=["RhT", "Hb"], writes=["pYr"])
                            S.dve(I("tensor_tensor", out=Ytm[:, tt, :], in0=pY[:], in1=Ytm[:, tt, :], op=ALU.add), reads=["pYr", ("Ytm", pi, tt)], writes=[("Ytm", pi, tt)])
                            S.pe(I("matmul", pH[:], lhsT=MT[:], rhs=Hb[:], start=True, stop=False), reads=["MTbd", "Hb"], writes=["pHr"])
                            S.pe(I("matmul", pH[:], lhsT=TM[:, 1, :], rhs=ZV[:], start=False, stop=False), reads=["TM", "ZV"], writes=["pHr"])
                            S.pe(I("matmul", pH[:], lhsT=TM[:, 2, :], rhs=TM[:, 3, :], start=False, stop=True), reads=["TM"], writes=["pHr"])
                            S.dve(I("scalar_tensor_tensor", out=Htmp[:], in0=H32[:], scalar=ptot[:, ck:ck + 1], in1=pH[:], op0=ALU.mult, op1=ALU.add),
                                  reads=["H32", "ptot", "pHr"], writes=["Htmp"])
                            S.dve(I("tensor_tensor", out=H32[:], in0=Htmp[:], in1=bdmask[:], op=ALU.mult), reads=["Htmp", "bdmask"], writes=["H32"])
                            S.act(I("activation", out=Hb[:], in_=H32[:], func=AF.Copy), reads=["H32"], writes=["Hb"])

                lists = []
                for pi, hp in enumerate(hps):
                    for d in range(2):
                        base_ops = S.ops
                        S.ops = []
                        S.suffix = (pi, d)
                        stream(d, pi, hp, 2 * pi + d)
                        S.suffix = None
                        lists.append(S.ops)
                        S.ops = base_ops
                for i_ in range(max(len(l_) for l_ in lists)):
                    for l_ in lists:
                        if i_ < len(l_):
                            o_ = l_[i_]
                            o_.idx = len(S.ops)
                            S.ops.append(o_)
                S.flush()
                if stop is not None and S.nphase >= stop:
                    pp.close()
                    return nc
            for pi, hp in enumerate(hps):
              Ytm = PP[pi]["Ytm"]; bon = PP[pi]["bon"]
              with contextlib.ExitStack() as ph:
                szr = sb("szrS", [128, S_], BF16, ph)
                S.dma(I("dma_start", out=szr[:], in_=szrD[hp]), writes=["szrS"])
                pTr = BK[6][:].bitcast(BF16)[:, 0:512].rearrange("p (j t) -> p j t", t=128)
                Y4 = Ytm[:].rearrange("p t (h i) -> p (t h) i", i=64)
                NG = 2 * TT
                s1 = sb("gn_s1", [128, NG], F32, ph); s2 = sb("gn_s2", [128, NG], F32, ph)
                ysq = sb("gn_sq", [128, TT, 128], F32, ph)
                yn = sb("gn_yn", [128, TT, 128], BF16, ph)
                S.dve(I("tensor_reduce", out=s1[:], in_=Y4, axis=AX.X, op=ALU.add), reads=[("Ytm", t_) for t_ in range(TT)], writes=["gn_s1"])
                S.act(I("activation", out=ysq[:], in_=Ytm[:], func=AF.Square), reads=[("Ytm", t_) for t_ in range(TT)], writes=["gn_sq"])
                S.dve(I("tensor_reduce", out=s2[:], in_=ysq[:].rearrange("p t (h i) -> p (t h) i", i=64), axis=AX.X, op=ALU.add), reads=["gn_sq"], writes=["gn_s2"])
                S.dve(I("tensor_scalar", out=s1[:], in0=s1[:], scalar1=1.0 / 64, scalar2=None, op0=ALU.mult), reads=["gn_s1"], writes=["gn_s1"])
                S.dve(I("tensor_scalar", out=s2[:], in0=s2[:], scalar1=1.0 / 64, scalar2=GN_EPS, op0=ALU.mult, op1=ALU.add), reads=["gn_s2"], writes=["gn_s2"])
                msq = sb("gn_msq", [128, NG], F32, ph)
                S.dve(I("tensor_tensor", out=msq[:], in0=s1[:], in1=s1[:], op=ALU.mult), reads=["gn_s1"], writes=["gn_msq"])
                S.dve(I("tensor_tensor", out=s2[:], in0=s2[:], in1=msq[:], op=ALU.subtract), reads=["gn_s2", "gn_msq"], writes=["gn_s2"])
                S.act(I("activation", out=s2[:], in_=s2[:], func=AF.Sqrt), reads=["gn_s2"], writes=["gn_s2"])
                S.dve(I("reciprocal", out=s2[:], in_=s2[:]), reads=["gn_s2"], writes=["gn_s2"])
                s1b = bc_last(s1[:].rearrange("p (g o) -> p g o", o=1), 64)
                s2b = bc_last(s2[:].rearrange("p (g o) -> p g o", o=1), 64)
                ysq4 = ysq[:].rearrange("p t (h i) -> p (t h) i", i=64)
                S.dve(I("tensor_tensor", out=ysq4, in0=Y4, in1=s1b, op=ALU.subtract), reads=[("Ytm", t_) for t_ in range(TT)] + ["gn_s1", "gn_sq"], writes=["gn_sq"])
                S.dve(I("tensor_tensor", out=yn[:].rearrange("p t (h i) -> p (t h) i", i=64), in0=ysq4, in1=s2b, op=ALU.mult), reads=["gn_sq", "gn_s2"], writes=["gn_yn"])
                yfm = sb("gn_yfm", [128, 4, 128], F32, ph)
                yob = [sb(f"gn_yo{i}", [128, 4, 128], BF16, ph) for i in range(2)]
                gi = 0
                for g in range(0, TT, 4):
                    n = min(4, TT - g)
                    for j in range(n):
                        S.pe(I("transpose", out=pTr[:, j, :], in_=yn[:, g + j, :], identity=ident[:]), reads=["gn_yn", "ident"], writes=["pTr"])
                    fsl = slice(g * 128, (g + n) * 128)
                    S.act(I("activation", out=yfm[:, 0:n, :], in_=pTr[:, 0:n, :], func=AF.Identity, scale=pcol("gn_g", hp), bias=pcol("gn_b", hp)),
                          reads=["pTr", "pv"], writes=["gn_yfm"])
                    S.dve(I("tensor_tensor", out=yfm[:, 0:n, :], in0=yfm[:, 0:n, :], in1=bon[:, fsl].rearrange("p (j t) -> p j t", t=128), op=ALU.add),
                          reads=["gn_yfm"] + [("bon", t_) for t_ in range(NTG)], writes=["gn_yfm"])
                    ob = gi % 2; gi += 1
                    S.dve(I("tensor_tensor", out=yob[ob][:, 0:n, :], in0=yfm[:, 0:n, :], in1=szr[:, fsl].rearrange("p (j t) -> p j t", t=128), op=ALU.mult),
                          reads=["gn_yfm", "szrS"], writes=[("gn_yo", ob)])
                    S.dma(I("dma_start", out=yrz[hp, :, fsl].rearrange("p (j t) -> p j t", t=128), in_=yob[ob][:, 0:n, :]), reads=[("gn_yo", ob)], writes=["yrz"])
                S.flush()
                if stop is not None and S.nphase >= stop:
                            return nc
            pp.close()
        hstack2 = contextlib.ExitStack()
        es.enter_context(hstack2)
        hT2 = sb("hT2", [128, DC, S_], BF16, hstack2)
        hts["hT"] = hT2
        alloc_stg()
        with contextlib.ExitStack() as ph:
            S.dma(I("dma_start", out=hT2[:], in_=hTd.rearrange("c p s -> p c s")), writes=["hT"])
            ymzS = sb("ymzS", [128, c.MWC, S_], BF16, ph); yrzS = sb("yrzS", [128, c.NP, S_], BF16, ph)
            S.dma(I("dma_start", out=ymzS[:], in_=ymz.rearrange("c p s -> p c s")), writes=["ymzS"])
            S.dma(I("dma_start", out=yrzS[:], in_=yrz.rearrange("c p s -> p c s")), writes=["yrzS"])
            wmst = sb("wmst", [128, c.MWC, 128], F32, ph); wmbf = sb("wmbf", [128, c.MWC, 128], BF16, ph)
            wrst = sb("wrst", [128, c.NP, 128], F32, ph); wrbf = sb("wrbf", [128, c.NP, 128], BF16, ph)
            sgm = sb("sgm", [128, S_], F32, ph); sgr = sb("sgr", [128, S_], F32, ph)
            pu = [BK[2 + i][:, 0:TG] for i in range(2)]
            t1f = sb("t1f", [128, TG], F32, ph)
            mo = [sb(f"mo{i}", [128, TG], BF16, ph) for i in range(2)]
            mi = 0
            for fc in range(DC):
                fsl = slice(fc * 128, (fc + 1) * 128)
                S.dma(I("dma_start", out=wmst[:], in_=dr["w_br_mla"][:, fsl].rearrange("(c p) n -> p c n", p=128)), writes=["wmst"])
                S.dve(I("tensor_copy", out=wmbf[:], in_=wmst[:]), reads=["wmst"], writes=["wmbf"])
                S.dma(I("dma_start", out=wrst[:], in_=dr["w_br_rwkv"][:, fsl].rearrange("(c p) n -> p c n", p=128)), writes=["wrst"])
                S.dve(I("tensor_copy", out=wrbf[:], in_=wrst[:]), reads=["wrst"], writes=["wrbf"])

                def consgm(tg, p_, pk):
                    S.act(I("activation", out=sgm[:, tg * TG:(tg + 1) * TG], in_=p_, func=AF.Sigmoid), reads=[pk], writes=[("sgm", tg)])

                def consgr(tg, p_, pk):
                    S.act(I("activation", out=sgr[:, tg * TG:(tg + 1) * TG], in_=p_, func=AF.Sigmoid), reads=[pk], writes=[("sgr", tg)])
                inproj(c.c_gm + fc * 128, 128, consgm, nxt=(c.c_gr + fc * 128, 128))
                inproj(c.c_gr + fc * 128, 128, consgr, nxt=((c.c_gm + (fc + 1) * 128, 128) if fc + 1 < DC else None))
                for tg in range(NTG):
                    tsl = slice(tg * TG, (tg + 1) * TG)
                    for wc in range(c.MWC):
                        S.pe(I("matmul", pu[0][:], lhsT=wmbf[:, wc, :], rhs=ymzS[:, wc, tsl], start=(wc == 0), stop=(wc == c.MWC - 1)),
                             reads=["wmbf", "ymzS"], writes=["pu0"])
                    for wc in range(c.NP):
                        S.pe(I("matmul", pu[1][:], lhsT=wrbf[:, wc, :], rhs=yrzS[:, wc, tsl], start=(wc == 0), stop=(wc == c.NP - 1)),
                             reads=["wrbf", "yrzS"], writes=["pu1"])
                    S.dve(I("tensor_tensor", out=t1f[:], in0=pu[0][:], in1=sgm[:, tsl], op=ALU.mult), reads=["pu0", ("sgm", tg)], writes=["t1f"])
                    mb = mi % 2; mi += 1
                    S.dve(I("tensor_tensor", out=mo[mb][:], in0=pu[1][:], in1=sgr[:, tsl], op=ALU.mult), reads=["pu1", ("sgr", tg)], writes=[("mo", mb)])
                    S.pool(I("tensor_tensor", out=mo[mb][:], in0=mo[mb][:], in1=t1f[:], op=ALU.add), reads=[("mo", mb), "t1f"], writes=[("mo", mb)])
                    S.dma(I("dma_start", out=mrg[fc, :, tsl], in_=mo[mb][:]), reads=[("mo", mb)], writes=["mrg"])
            S.flush()
            if stop is not None and S.nphase >= stop:
                return nc
        stg["stack"].close()
        hstack2.close()

        with contextlib.ExitStack() as ph:
            CB = min(512, D); NCB = D // CB
            wob = sb("wob", [128, DC, D], BF16, ph)
            wos = [sb(f"wos{i}", [128, D], F32, ph) for i in range(2)]
            for fc in range(DC):
                S.dma(I("dma_start", out=wos[fc % 2][:], in_=dr["w_out"][fc * 128:(fc + 1) * 128, :]), writes=[("wos", fc % 2)])
                (S.dve if fc % 2 == 0 else S.pool)(I("tensor_copy", out=wob[:, fc, :], in_=wos[fc % 2][:]), reads=[("wos", fc % 2)], writes=["wob"])
            gpb = sb("gpb", [128, D], F32, ph)
            S.dma(I("dma_start", out=gpb[:], in_=bass.AP(dr["g_post"].tensor, 0, [[0, 128], [1, D]])), writes=["gpb"])
            mT = [sb(f"mT{i}", [128, DC, 128], BF16, ph) for i in range(2)]
            xt = [sb(f"xtF{i}", [128, D], F32, ph) for i in range(2)]
            o32 = sb("o32F", [128, D], F32, ph)
            junk = sb("junkF", [128, CB], BF16, ph)
            ssp = sb("ssp", [128, TT, NCB + 4], F32, ph)
            po = [BK[i][:, 0:CB] for i in range(2)]
            ot = [sb(f"otF{i}", [128, D], F32, ph) for i in range(2)]
            pi_ = 0
            for tt in range(TT):
                sl = tt % 2
                tsl = slice(tt * 128, (tt + 1) * 128)
                S.dma(I("dma_start", out=mT[sl][:], in_=mrg[:, :, tsl].rearrange("c p t -> p c t")), writes=[("mT", sl)])
                S.dma(I("dma_start", out=xt[sl][:], in_=dr["x"][tsl, :]), writes=[("xtF", sl)])
                for cb in range(NCB):
                    b = pi_ % 2; pi_ += 1
                    for fc in range(DC):
                        S.pe(I("matmul", po[b][:], lhsT=mT[sl][:, fc, :], rhs=wob[:, fc, cb * CB:(cb + 1) * CB], start=(fc == 0), stop=(fc == DC - 1)),
                             reads=[("mT", sl), "wob"], writes=[("po", b)])
                    S.act(I("activation", out=o32[:, cb * CB:(cb + 1) * CB], in_=po[b][:], func=AF.Copy), reads=[("po", b)], writes=[("o32F", cb)])
                    S.act(I("activation", out=junk[:], in_=po[b][:], func=AF.Square, accum_out=ssp[:, tt, cb:cb + 1]), reads=[("po", b)], writes=["junkF", ("ssp", tt)])
                sc = NCB
                S.dve(I("tensor_reduce", out=ssp[:, tt, sc:sc + 1], in_=ssp[:, tt, 0:NCB], axis=AX.X, op=ALU.add), reads=[("ssp", tt)], writes=[("ssp", tt)])
                S.dve(I("tensor_scalar", out=ssp[:, tt, sc + 1:sc + 2], in0=ssp[:, tt, sc:sc + 1], scalar1=1.0 / D, scalar2=NORM_EPS, op0=ALU.mult, op1=ALU.add),
                      reads=[("ssp", tt)], writes=[("ssp", tt)])
                S.act(I("activation", out=ssp[:, tt, sc + 2:sc + 3], in_=ssp[:, tt, sc + 1:sc + 2], func=AF.Sqrt), reads=[("ssp", tt)], writes=[("ssp", tt)])
                S.dve(I("reciprocal", out=ssp[:, tt, sc + 3:sc + 4], in_=ssp[:, tt, sc + 2:sc + 3]), reads=[("ssp", tt)], writes=[("ssp", tt)])
                S.dve(I("scalar_tensor_tensor", out=ot[sl][:], in0=o32[:], scalar=ssp[:, tt, sc + 3:sc + 4], in1=gpb[:], op0=ALU.mult, op1=ALU.mult),
                      reads=[("o32F", cb_) for cb_ in range(NCB)] + [("ssp", tt), "gpb"], writes=[("otF", sl)])
                S.pool(I("tensor_tensor", out=ot[sl][:], in0=ot[sl][:], in1=xt[sl][:], op=ALU.add), reads=[("otF", sl), ("xtF", sl)], writes=[("otF", sl)])
                S.dma(I("dma_start", out=out[tsl, :], in_=ot[sl][:]), reads=[("otF", sl)], writes=["out"], final=True)
            S.flush(last=True)
    return nc


def rope_tables(S):
    pos = np.arange(S, dtype=np.float32)
    inv = np.power(np.float32(10000.0), -np.arange(0, 64, 2, dtype=np.float32) / np.float32(64)).astype(np.float32)
    ang = (pos[:, None] * inv[None, :]).astype(np.float32)
    cosT = np.concatenate([np.cos(ang), np.cos(ang)], axis=1).T.astype(np.float32)
    sinT = np.concatenate([np.sin(ang), np.sin(ang)], axis=1).T.astype(np.float32)
    return np.ascontiguousarray(cosT), np.ascontiguousarray(sinT)


def make_in_maps(c, inputs):
    B = inputs["x"].shape[0]
    cosT, sinT = rope_tables(c.S)
    shared = {}
    for n in list(MAT_SHAPES(c).keys()) + list(VEC_LENS(c).keys()):
        a = np.ascontiguousarray(np.asarray(inputs[n], dtype=np.float32))
        if n == "rwkv_r_k":
            a = a.reshape(-1)
        shared[n] = a
    shared["cosT"] = cosT; shared["sinT"] = sinT
    maps = []
    for b in range(B):
        m = dict(shared)
        m["x"] = np.ascontiguousarray(np.asarray(inputs["x"][b], dtype=np.float32))
        maps.append(m)
    return maps


def kernel(**inputs):
    c = Cfg()
    nc = build(c)
    maps = make_in_maps(c, inputs)
    res = run_bass_kernel_spmd(nc, maps, core_ids=list(range(8)))
    return np.stack([np.asarray(r["out"], dtype=np.float32) for r in res.results], axis=0)
```

```python
import contextlib
import math
import numpy as np
import concourse.bass as bass
import concourse.mybir as mybir
from concourse.bass_utils import run_bass_kernel_spmd

F32 = mybir.dt.float32
BF16 = mybir.dt.bfloat16
AF = mybir.ActivationFunctionType
ALU = mybir.AluOpType
AX = mybir.AxisListType

ENGS = ["tensor", "vector", "scalar", "gpsimd", "sync"]
EPOCH = 16000
NEPOCH = 6
NDMA_SEM = 24


def I(method, *a, **k):
    return lambda e: getattr(e, method)(*a, **k)


class Op:
    __slots__ = ("eng", "fn", "reads", "writes", "dma", "idx", "waits", "signal",
                 "seq", "dsem", "dval", "final", "bar")


class Sched:
    def __init__(self, nc, es):
        self.nc = nc
        self.ops = []
        self.esem = {(e, i): es.enter_context(nc.semaphore(f"s_{e}_{i}")) for e in ENGS for i in range(NEPOCH)}
        self.dsem = [es.enter_context(nc.semaphore(f"d_{i}")) for i in range(NDMA_SEM)]
        self.bsem = {e: es.enter_context(nc.semaphore(f"b_{e}")) for e in ENGS}
        self.cnt = {e: 0 for e in ENGS}
        self.duse = [0] * NDMA_SEM
        self.dn = 0
        self.nphase = 0
        self.barrier_fns = None
        self.final_ops = []
        self.alias = {}
        self.limit = None
        self.suffix = None
        self.shared = set()

    def kmap(self, k):
        k = self.alias.get(k, k)
        if self.suffix is None:
            return k
        b_ = k[0] if isinstance(k, tuple) else k
        if b_ in self.shared:
            return k
        return ("S", self.suffix, k)

    def add(self, eng, fn, reads=(), writes=(), dma=False, final=False):
        op = Op()
        if self.limit is not None and len(self.ops) >= self.limit:
            op.writes = ()
            return op
        op.eng = eng; op.fn = fn
        op.reads = tuple(self.kmap(k) for k in reads)
        op.writes = tuple(self.kmap(k) for k in writes)
        op.dma = dma; op.idx = len(self.ops); op.waits = []
        op.signal = False; op.seq = None; op.dsem = None; op.dval = None
        op.final = final
        op.bar = False
        self.ops.append(op)
        return op

    def pe(self, fn, reads=(), writes=()): return self.add("tensor", fn, reads, writes)
    def dve(self, fn, reads=(), writes=()): return self.add("vector", fn, reads, writes)
    def act(self, fn, reads=(), writes=()): return self.add("scalar", fn, reads, writes)
    def pool(self, fn, reads=(), writes=()): return self.add("gpsimd", fn, reads, writes)
    def dma(self, fn, reads=(), writes=(), final=False):
        return self.add("sync", fn, reads, writes, dma=True, final=final)

    def analyze(self):
        lw = {}; rd = {}
        ops = self.ops
        for op in ops:
            deps = set()
            for k in op.reads:
                w = lw.get(k)
                if w is not None:
                    deps.add(w)
            for k in op.writes:
                w = lw.get(k)
                if w is not None:
                    deps.add(w)
                for r in rd.get(k, ()):
                    deps.add(r)
            for k in op.reads:
                rd.setdefault(k, []).append(op.idx)
            for k in op.writes:
                lw[k] = op.idx
                rd[k] = []
            deps.discard(op.idx)
            best = {}; dmas = []
            for d in deps:
                p = ops[d]
                if p.dma:
                    dmas.append(d)
                    continue
                if p.eng == op.eng and not op.dma:
                    if op.eng == "tensor":
                        continue
                if p.eng not in best or best[p.eng] < d:
                    best[p.eng] = d
            if op.bar and op.eng == "tensor":
                for e2 in ("vector", "scalar", "gpsimd"):
                    lastop = [o2.idx for o2 in ops if o2.eng == e2 and not o2.bar and not o2.dma]
                    if lastop and (e2 not in best or best[e2] < lastop[-1]):
                        best[e2] = lastop[-1]
            op.waits = sorted(best.values()) + sorted(dmas)
            for d in op.waits:
                ops[d].signal = True

    def flush(self, last=False):
        nc = self.nc
        if self.limit is not None:
            print("phase", self.nphase, "nops", len(self.ops), flush=True)
        self.limit = None
        for e_ in ENGS:
            bop = self.add(e_, self.barrier_fns[e_], reads=["dmy" + e_],
                           writes=["bar" + e_] + (["ppv", "prk", "pYr", "pHr"] if e_ == "tensor" else []))
            bop.bar = True
        self.analyze()
        ops = self.ops
        for op in ops:
            if op.bar:
                continue
            if op.dma:
                s = self.dn % NDMA_SEM
                self.dn += 1
                self.duse[s] += 1
                op.dsem = s
                op.dval = 16 * self.duse[s]
            elif op.signal:
                self.cnt[op.eng] += 1
                op.seq = self.cnt[op.eng]
                assert op.seq <= EPOCH * NEPOCH
        phase = self.nphase
        self.nphase += 1
        dsem_end = list(self.duse)
        with nc.Block() as block:
            def make(engname):
                def body(e):
                    waited = {}
                    if phase > 0:
                        for pe_ in ENGS:
                            e.wait_ge(self.bsem[pe_], phase * (16 if pe_ == "sync" else 1))
                    for op in ops:
                        if op.eng != engname:
                            continue
                        for d in op.waits:
                            p = ops[d]
                            if p.dma:
                                key = ("d", p.dsem); val = p.dval; sem = self.dsem[p.dsem]
                            else:
                                ep = (p.seq - 1) // EPOCH
                                key = ("e", p.eng, ep); val = p.seq - ep * EPOCH
                                sem = self.esem[(p.eng, ep)]
                            if waited.get(key, 0) >= val:
                                continue
                            waited[key] = val
                            e.wait_ge(sem, val)
                        if op.bar:
                            if engname == "sync":
                                for s in range(NDMA_SEM):
                                    if dsem_end[s] > 0 and waited.get(("d", s), 0) < 16 * dsem_end[s]:
                                        e.wait_ge(self.dsem[s], 16 * dsem_end[s])
                                op.fn(e).then_inc(self.bsem["sync"], 16)
                                if last:
                                    e.wait_ge(self.bsem["sync"], 16 * (phase + 1))
                            else:
                                op.fn(e).then_inc(self.bsem[engname], 1)
                        elif op.dma:
                            sem = self.dsem[op.dsem]
                            key = ("d", op.dsem)
                            if op.dval > 16 and waited.get(key, 0) < op.dval - 16:
                                e.wait_ge(sem, op.dval - 16)
                                waited[key] = op.dval - 16
                            op.fn(e).then_inc(sem, 16)
                        else:
                            ins = op.fn(e)
                            if op.signal:
                                ep = (op.seq - 1) // EPOCH
                                ins.then_inc(self.esem[(op.eng, ep)], 1)
                return body
            for engname in ENGS:
                getattr(block, engname)(make(engname))
        self.ops = []
        self.alias = {}
        self.shared = set()


class Cfg:
    def __init__(s, S=2048, D=2048, HM=8, QR=512, KVR=512, HR=16):
        s.S = S; s.D = D; s.HM = HM; s.QR = QR; s.KVR = KVR; s.HR = HR
        s.MW = HM * 128; s.RW = HR * 64; s.NP = s.RW // 128
        s.MLA_IN = QR + KVR + 64; s.RWKV_IN = 3 * s.RW + 4 * 96
        s.D_IN = s.MLA_IN + s.RWKV_IN + s.MW + s.RW + 2 * D
        s.TG = min(512, S); s.NTG = S // s.TG; s.TT = S // 128; s.DC = D // 128
        s.QRC = QR // 128; s.KVRC = KVR // 128; s.MWC = s.MW // 128
        s.c_qa = 0; s.c_kva = QR; s.c_kr = QR + KVR
        s.c_r = s.MLA_IN; s.c_k = s.c_r + s.RW; s.c_v = s.c_k + s.RW; s.c_lora = s.c_v + s.RW
        s.c_zm = s.MLA_IN + s.RWKV_IN; s.c_zr = s.c_zm + s.MW
        s.c_gm = s.c_zr + s.RW; s.c_gr = s.c_gm + D
        s.CK = s.TG // 128


VEC_NAMES = ["g_pre", "mla_q_norm", "mla_kv_norm", "rwkv_mu", "rwkv_w0_f", "rwkv_w0_b", "rwkv_a0_f",
             "rwkv_a0_b", "rwkv_k_k", "rwkv_k_a", "rwkv_r_k", "rwkv_gn_g", "rwkv_gn_b", "g_post"]
MAT_SHAPES = lambda c: {
    "w_in": [c.D, c.D_IN], "mla_wq_b": [c.QR, c.HM * 192], "mla_wkv_b": [c.KVR, c.HM * 256],
    "rwkv_w2_f": [96, c.RW], "rwkv_w2_b": [96, c.RW], "rwkv_a2_f": [96, c.RW], "rwkv_a2_b": [96, c.RW],
    "w_br_mla": [c.MW, c.D], "w_br_rwkv": [c.RW, c.D], "w_out": [c.D, c.D]}
VEC_LENS = lambda c: {
    "g_pre": c.D, "mla_q_norm": c.QR, "mla_kv_norm": c.KVR, "rwkv_mu": c.RWKV_IN, "rwkv_w0_f": c.RW,
    "rwkv_w0_b": c.RW, "rwkv_a0_f": c.RW, "rwkv_a0_b": c.RW, "rwkv_k_k": c.RW, "rwkv_k_a": c.RW,
    "rwkv_r_k": c.RW, "rwkv_gn_g": c.RW, "rwkv_gn_b": c.RW, "g_post": c.D}

C0 = math.exp(-0.5)
DEBUG_LIMIT = None
NORM_EPS = 1e-6
GN_EPS = 64e-5


def bc_last(ap3, n):
    pat = [list(x) for x in ap3.ap]
    pat[-1] = [0, n]
    return bass.AP(ap3.tensor, ap3.offset, pat)


def build(c: Cfg, stop=None):
    nc = bass.Bass("TRN2", target_bir_lowering=False)
    S_, D, TG, NTG, TT, DC = c.S, c.D, c.TG, c.NTG, c.TT, c.DC
    dr = {}
    dr["x"] = nc.dram_tensor("x", [S_, D], F32, kind="ExternalInput").ap()
    for n, sh in MAT_SHAPES(c).items():
        dr[n] = nc.dram_tensor(n, sh, F32, kind="ExternalInput").ap()
    for n, ln in VEC_LENS(c).items():
        dr[n] = nc.dram_tensor(n, [ln], F32, kind="ExternalInput").ap()
    dr["cosT"] = nc.dram_tensor("cosT", [64, S_], F32, kind="ExternalInput").ap()
    dr["sinT"] = nc.dram_tensor("sinT", [64, S_], F32, kind="ExternalInput").ap()
    out = nc.dram_tensor("out", [S_, D], F32, kind="ExternalOutput").ap()

    def scr(name, shape, dt):
        return nc.dram_tensor(name, shape, dt, kind="Internal").ap()
    qTn = scr("qTn", [c.HM, 128, S_], BF16); qTr = scr("qTr", [c.HM, 64, S_], BF16)
    kTn = scr("kTn", [c.HM, 128, S_], BF16); krTd = scr("krTd", [64, S_], BF16)
    Vtm = scr("Vtm", [c.HM, S_, 128], BF16)
    ymz = scr("ymz", [c.MWC, 128, S_], BF16); yrz = scr("yrz", [c.NP, 128, S_], BF16)
    mrg = scr("mrg", [DC, 128, S_], BF16)
    rinT = scr("rinT", [3 * c.NP + 1, 128, S_], F32)
    kknD = scr("kknD", [c.NP, 128, S_], F32)
    lwT = scr("lwT", [4, 96, S_], BF16)
    szrD = scr("szrD", [c.NP, 128, S_], BF16)
    dummyD = scr("dummyD", [1, 16], F32)

    with contextlib.ExitStack() as es:
        S = Sched(nc, es)

        uniq = [0]

        def sb(name, shape, dt, stack=es):
            uniq[0] += 1
            return stack.enter_context(nc.sbuf_tensor(f"{name}_u{uniq[0]}", shape, dt))

        def ps(name, shape, dt, stack=es):
            return stack.enter_context(nc.psum_tensor(name, shape, dt))

        ident = sb("ident", [128, 128], BF16)
        identf = sb("identf", [128, 128], F32)
        ones128 = sb("ones128", [128, 128], BF16)
        blk1 = sb("blk1", [128, 128], BF16)
        bdmask = sb("bdmask", [128, 128], F32)
        mU = sb("mU", [128, 128], BF16); mSU = sb("mSU", [128, 128], BF16)
        mL = sb("mL", [128, 128], BF16); mSL = sb("mSL", [128, 128], BF16)
        Mf = sb("Mf", [128, 2, 256], BF16); Mb = sb("Mb", [128, 2, 256], BF16)
        Nf = sb("Nf", [128, 2, 128], BF16); Nb = sb("Nb", [128, 2, 128], BF16)
        scanmask = sb("scanmask", [128, TG], F32)
        RT = sb("RT", [64, 64], F32); RTa = sb("RTa", [64, 64], F32)
        pvst = sb("pvst", [128, 128], F32)
        pv = sb("pv", [128, 128], F32); pvo = sb("pvo", [128, 128], F32); pvh = sb("pvh", [128, 128], F32)
        dmy = {e: sb(f"dmy_{e}", [128, 8], F32) for e in ENGS}
        BK = [ps(f"bank{i}", [128, 512], F32) for i in range(8)]
        pdm = BK[7][:, 504:512]
        S.barrier_fns = {
            "vector": I("memset", dmy["vector"][:, 0:1], 0.0),
            "gpsimd": I("memset", dmy["gpsimd"][:, 0:1], 0.0),
            "scalar": I("activation", out=dmy["scalar"][:, 0:1], in_=dmy["scalar"][:, 1:2], func=AF.Copy),
            "tensor": I("matmul", pdm[0:1, 0:1], lhsT=identf[0:1, 0:1], rhs=identf[0:1, 0:1], start=True, stop=True),
            "sync": I("dma_start", out=dummyD[0:1, 0:8], in_=dmy["sync"][0:1, 0:8]),
        }

        def tri(t, pat, cm, op, key):
            S.pool(I("memset", t[:], 1.0), writes=[key])
            S.pool(I("affine_select", out=t[:], in_=t[:], pattern=pat, compare_op=op, fill=0.0, base=0,
                     channel_multiplier=cm), reads=[key], writes=[key])
        for e_ in ENGS:
            S.pool(I("memset", dmy[e_][:], 0.0), writes=["dmy" + e_])
        tri(ident, [[-1, 128]], 1, ALU.is_equal, "ident")
        tri(identf, [[-1, 128]], 1, ALU.is_equal, "identf")
        tri(mU, [[1, 128]], -1, ALU.is_ge, "mU"); tri(mSU, [[1, 128]], -1, ALU.is_gt, "mSU")
        tri(mL, [[-1, 128]], 1, ALU.is_ge, "mL"); tri(mSL, [[-1, 128]], 1, ALU.is_gt, "mSL")
        S.pool(I("memset", ones128[:], 1.0), writes=["ones128"])
        for t, key in ((blk1, "blk1"), (bdmask, "bdmask")):
            S.pool(I("memset", t[:], 0.0), writes=[key])
            S.pool(I("memset", t[0:64, 0:64], 1.0), writes=[key])
            S.pool(I("memset", t[64:128, 64:128], 1.0), writes=[key])
        for h in range(2):
            S.pool(I("tensor_copy", out=Mf[:, h, 0:128], in_=mSU[:]), reads=["mSU"], writes=["Mf"])
            S.pool(I("tensor_copy", out=Mf[:, h, 128:256], in_=mU[:]), reads=["mU"], writes=["Mf"])
            S.pool(I("tensor_copy", out=Mb[:, h, 0:128], in_=mSL[:]), reads=["mSL"], writes=["Mb"])
            S.pool(I("tensor_copy", out=Mb[:, h, 128:256], in_=mL[:]), reads=["mL"], writes=["Mb"])
            S.pool(I("tensor_copy", out=Nf[:, h, :], in_=mSL[:]), reads=["mSL"], writes=["Nf"])
            S.pool(I("tensor_copy", out=Nb[:, h, :], in_=mSU[:]), reads=["mSU"], writes=["Nb"])
        S.pool(I("memset", scanmask[:], 1.0), writes=["scanmask"])
        for ck in range(c.CK):
            S.pool(I("memset", scanmask[:, ck * 128:ck * 128 + 1], 0.0), writes=["scanmask"])
        S.pool(I("memset", RTa[:], 1.0), writes=["RTa"])
        S.pool(I("affine_select", out=RTa[:], in_=RTa[:], pattern=[[-1, 64]], compare_op=ALU.is_equal, fill=0.0,
                 base=-32, channel_multiplier=1), reads=["RTa"], writes=["RTa"])
        S.pool(I("memset", RT[:], 1.0), writes=["RT"])
        S.pool(I("affine_select", out=RT[:], in_=RT[:], pattern=[[1, 64]], compare_op=ALU.is_equal, fill=0.0,
                 base=-32, channel_multiplier=-1), reads=["RT"], writes=["RT"])
        S.pool(I("tensor_tensor", out=RT[:], in0=RT[:], in1=RTa[:], op=ALU.subtract), reads=["RT", "RTa"], writes=["RT"])

        S.pool(I("memset", pvst[:], 0.0), writes=["pvst"])
        col = {}
        nrow = [0]

        def vec_rows(name, ap, n, L=128):
            col[name] = nrow[0]
            S.dma(I("dma_start", out=pvst[nrow[0]:nrow[0] + n, 0:L], in_=ap.rearrange("(c p) -> c p", p=L)),
                  reads=["pvst0"], writes=["pvst"])
            nrow[0] += n
        S.ops[-1].writes = ("pvst", "pvst0")
        vec_rows("g_pre", dr["g_pre"], DC)
        vec_rows("gq", dr["mla_q_norm"], c.QRC)
        vec_rows("gkv", dr["mla_kv_norm"], c.KVRC)
        vec_rows("mu", dr["rwkv_mu"][0:3 * c.RW], 3 * c.NP)
        vec_rows("mul", dr["rwkv_mu"][3 * c.RW:3 * c.RW + 384], 4, L=96)
        for nm in ["w0_f", "w0_b", "a0_f", "a0_b", "k_k", "k_a", "r_k", "gn_g", "gn_b"]:
            vec_rows(nm, dr["rwkv_" + nm], c.NP)
        assert nrow[0] <= 128
        ppv = BK[7][:, 0:128]
        S.pe(I("matmul", ppv[:], lhsT=pvst[:], rhs=identf[:], start=True, stop=True), reads=["pvst", "identf"], writes=["ppv"])
        S.dve(I("tensor_copy", out=pv[:], in_=ppv[:]), reads=["ppv"], writes=["pv"])
        S.dve(I("tensor_scalar", out=pvo[:], in0=pv[:], scalar1=-1.0, scalar2=1.0, op0=ALU.mult, op1=ALU.add), reads=["pv"], writes=["pvo"])
        S.dve(I("tensor_scalar", out=pvh[:], in0=pv[:], scalar1=0.5, scalar2=None, op0=ALU.mult), reads=["pv"], writes=["pvh"])
        S.flush()
        if stop is not None and S.nphase >= stop:
            return nc

        def pcol(name, i=0):
            return pv[:, col[name] + i:col[name] + i + 1]

        hstack = contextlib.ExitStack()
        es.enter_context(hstack)
        hT = sb("hT", [128, DC, S_], BF16, hstack)
        hts = {"hT": hT}
        hTd = nc.dram_tensor("hTd", [DC, 128, S_], BF16, kind="Internal").ap()
        stg = {}

        def alloc_stg():
            stg["stack"] = contextlib.ExitStack()
            es.enter_context(stg["stack"])
            stg["wst"] = [sb(f"wst{i}", [128, DC, 128], F32, stg["stack"]) for i in range(2)]
            stg["wbf"] = [sb(f"wbf{i}", [128, DC, 128], BF16, stg["stack"]) for i in range(2)]
        alloc_stg()
        pin = [BK[i][:, 0:TG] for i in range(2)]
        st = {"w": 0, "p": 0, "alt": 0}

        def alt_evac():
            st["alt"] ^= 1
            return st["alt"]

        def wload(col0, ncols):
            wst = stg["wst"]; wbf = stg["wbf"]
            sl = st["w"]; st["w"] ^= 1
            S.dma(I("dma_start", out=wst[sl][:, :, 0:ncols],
                    in_=dr["w_in"][:, col0:col0 + ncols].rearrange("(dc p) n -> p dc n", p=128)),
                  writes=[("wst", sl)])
            S.pool(I("tensor_copy", out=wbf[sl][:, :, 0:ncols], in_=wst[sl][:, :, 0:ncols]),
                   reads=[("wst", sl)], writes=[("wbf", sl)])
            st["pref"] = (col0, ncols, sl)

        def inproj(col0, ncols, consume, nxt=None):
            wbf = stg["wbf"]
            pf = st.get("pref")
            if pf is None or pf[0] != col0 or pf[1] != ncols:
                wload(col0, ncols)
                pf = st["pref"]
            sl = pf[2]
            st["pref"] = None
            if nxt is not None:
                wload(nxt[0], nxt[1])
            for tg in range(NTG):
                b = st["p"]; st["p"] ^= 1
                for dc in range(DC):
                    S.pe(I("matmul", pin[b][0:ncols, :], lhsT=wbf[sl][:, dc, 0:ncols],
                           rhs=hts["hT"][:, dc, tg * TG:(tg + 1) * TG], start=(dc == 0), stop=(dc == DC - 1)),
                         reads=[("wbf", sl), "hT"], writes=[("pin", b)])
                consume(tg, pin[b][0:ncols, :], ("pin", b))

        with contextlib.ExitStack() as ph:
            xt = [sb(f"xt{i}", [128, D], F32, ph) for i in range(2)]
            xs = [sb(f"xs{i}", [128, D], BF16, ph) for i in range(2)]
            junk = sb("junkA", [128, D], BF16, ph)
            sta = sb("sta", [128, TT, 4], F32, ph)
            pT = [BK[2 + i][:].bitcast(BF16)[:, 0:512].rearrange("p (j t) -> p j t", t=128) for i in range(2)]
            r_ = 0
            for tt in range(TT):
                sl = tt % 2
                S.dma(I("dma_start", out=xt[sl][:], in_=dr["x"][tt * 128:(tt + 1) * 128, :]), writes=[("xt", sl)])
                S.act(I("activation", out=junk[:], in_=xt[sl][:], func=AF.Square, accum_out=sta[:, tt, 0:1]),
                      reads=[("xt", sl)], writes=["junk", ("sta", tt)])
                S.dve(I("tensor_scalar", out=sta[:, tt, 1:2], in0=sta[:, tt, 0:1], scalar1=1.0 / D, scalar2=NORM_EPS,
                        op0=ALU.mult, op1=ALU.add), reads=[("sta", tt)], writes=[("sta", tt)])
                S.act(I("activation", out=sta[:, tt, 2:3], in_=sta[:, tt, 1:2], func=AF.Sqrt), reads=[("sta", tt)], writes=[("sta", tt)])
                S.dve(I("reciprocal", out=sta[:, tt, 3:4], in_=sta[:, tt, 2:3]), reads=[("sta", tt)], writes=[("sta", tt)])
                S.dve(I("tensor_scalar", out=xs[sl][:], in0=xt[sl][:], scalar1=sta[:, tt, 3:4], scalar2=None, op0=ALU.mult),
                      reads=[("xt", sl), ("sta", tt)], writes=[("xs", sl)])
                for g in range(0, DC, 4):
                    n = min(4, DC - g)
                    pb = r_ % 2; r_ += 1
                    for j in range(n):
                        S.pe(I("transpose", out=pT[pb][:, j, :], in_=xs[sl][:, (g + j) * 128:(g + j + 1) * 128], identity=ident[:]),
                             reads=[("xs", sl), "ident"], writes=[("pTA", pb)])
                    for j in range(n):
                        fn = I("tensor_scalar", out=hT[:, g + j, tt * 128:(tt + 1) * 128], in0=pT[pb][:, j, :],
                               scalar1=pcol("g_pre", g + j), scalar2=None, op0=ALU.mult)
                        (S.dve if (j % 2 == 0) else S.pool if False else S.dve)(fn, reads=[("pTA", pb), "pv"], writes=[("hT", tt, g + j)])
            S.dma(I("dma_start", out=hTd.rearrange("c p s -> p c s"), in_=hT[:]),
                  reads=[("hT", t_, g_) for t_ in range(TT) for g_ in range(DC)], writes=["hTd"])
            S.flush()
            if stop is not None and S.nphase >= stop:
                return nc

        def rms_rstd(ph, srcT, nchunk, rdim, rbc, tagp, skey, rkey):
            sq = [sb(f"sq{tagp}{i}", [128, S_], BF16, ph) for i in range(2)]
            pss = [BK[2 + i][:, 0:TG] for i in range(NTG)]
            for cc in range(nchunk):
                S.act(I("activation", out=sq[cc % 2][:], in_=srcT[:, cc, :], func=AF.Square),
                      reads=[(skey, cc)], writes=[("sq" + tagp, cc % 2)])
                for tg in range(NTG):
                    S.pe(I("matmul", pss[tg][:], lhsT=ones128[:], rhs=sq[cc % 2][:, tg * TG:(tg + 1) * TG],
                           start=(cc == 0), stop=(cc == nchunk - 1)), reads=[("sq" + tagp, cc % 2)], writes=[("pss" + tagp, tg)])
            for tg in range(NTG):
                sl_ = rbc[:, tg * TG:(tg + 1) * TG]
                S.dve(I("tensor_scalar", out=sl_, in0=pss[tg][:], scalar1=1.0 / rdim, scalar2=NORM_EPS, op0=ALU.mult, op1=ALU.add),
                      reads=[("pss" + tagp, tg)], writes=[(rkey, tg)])
                S.act(I("activation", out=sl_, in_=sl_, func=AF.Sqrt), reads=[(rkey, tg)], writes=[(rkey, tg)])
                S.dve(I("reciprocal", out=sl_, in_=sl_), reads=[(rkey, tg)], writes=[(rkey, tg)])

        def rope(ph_tiles, src32, tg, outbf, okey, skey):
            prot, t1, t2, cosS, sinS = ph_tiles
            S.pe(I("matmul", prot[0:64, :], lhsT=RT[:], rhs=src32, start=True, stop=True), reads=[skey, "RT"], writes=["prot"])
            S.pool(I("tensor_tensor", out=t1[0:64, :], in0=src32, in1=cosS[:, tg * TG:(tg + 1) * TG], op=ALU.mult),
                   reads=[skey, "cosS"], writes=["ropet1"])
            S.dve(I("tensor_tensor", out=t2[0:64, :], in0=prot[0:64, :], in1=sinS[:, tg * TG:(tg + 1) * TG], op=ALU.mult),
                  reads=["prot", "sinS"], writes=["ropet2"])
            S.dve(I("tensor_tensor", out=outbf, in0=t1[0:64, :], in1=t2[0:64, :], op=ALU.add),
                  reads=["ropet1", "ropet2"], writes=[okey])

        scale = (128 + 64) ** -0.5
        with contextlib.ExitStack() as ph:
            S.alias = {"pq0": ("bank", 2), "pq1": ("bank", 3), "prot": ("bank", 4)}
            S.alias.update({("pssq", i): ("bank", 2 + i) for i in range(NTG)})
            qaT = sb("qaT", [128, c.QRC, S_], BF16, ph)
            rq = sb("rq", [128, S_], F32, ph)
            cosS = sb("cosS", [64, S_], F32, ph); sinS = sb("sinS", [64, S_], F32, ph)
            S.dma(I("dma_start", out=cosS[:], in_=dr["cosT"]), writes=["cosS"])
            S.dma(I("dma_start", out=sinS[:], in_=dr["sinT"]), writes=["sinS"])
            for cc in range(c.QRC):
                def cons(tg, p_, pk, cc=cc):
                    S.act(I("activation", out=qaT[:, cc, tg * TG:(tg + 1) * TG], in_=p_, func=AF.Copy), reads=[pk], writes=[("qaT", cc)])
                inproj(c.c_qa + cc * 128, 128, cons, nxt=((c.c_qa + (cc + 1) * 128, 128) if cc + 1 < c.QRC else None))
            rms_rstd(ph, qaT, c.QRC, c.QR, rq, "q", "qaT", "rq")
            wqst = sb("wqst", [128, c.QRC, 192], F32, ph); wqbf = sb("wqbf", [128, c.QRC, 192], BF16, ph)
            pq = [BK[2 + i][:, 0:TG] for i in range(2)]
            prot = BK[4][:, 0:TG]
            t1 = sb("ropet1", [128, TG], F32, ph); t2 = sb("ropet2", [128, TG], F32, ph)
            q32 = sb("q32", [64, TG], F32, ph)
            qo = [sb(f"qo{i}", [128, TG], BF16, ph) for i in range(2)]
            qro = [sb(f"qro{i}", [64, TG], BF16, ph) for i in range(2)]
            k_ = 0
            for h in range(c.HM):
                S.dma(I("dma_start", out=wqst[:], in_=dr["mla_wq_b"][:, h * 192:(h + 1) * 192].rearrange("(c p) n -> p c n", p=128)), writes=["wqst"])
                for cc in range(c.QRC):
                    S.pool(I("tensor_scalar", out=wqbf[:, cc, :], in0=wqst[:, cc, :], scalar1=pcol("gq", cc), scalar2=None, op0=ALU.mult),
                           reads=["wqst", "pv"], writes=["wqbf"])
                for tg in range(NTG):
                    b = k_ % 2; k_ += 1
                    tsl = slice(tg * TG, (tg + 1) * TG)
                    for cc in range(c.QRC):
                        S.pe(I("matmul", pq[0][:], lhsT=wqbf[:, cc, 0:128], rhs=qaT[:, cc, tsl], start=(cc == 0), stop=(cc == c.QRC - 1)),
                             reads=["wqbf", ("qaT", cc)], writes=["pq0"])
                    S.dve(I("scalar_tensor_tensor", out=qo[b][:], in0=pq[0][:], scalar=scale, in1=rq[:, tsl], op0=ALU.mult, op1=ALU.mult),
                          reads=["pq0", ("rq", tg)], writes=[("qo", b)])
                    S.dma(I("dma_start", out=qTn[h, :, tsl], in_=qo[b][:]), reads=[("qo", b)], writes=[("qTn", h)])
                    for cc in range(c.QRC):
                        S.pe(I("matmul", pq[1][0:64, :], lhsT=wqbf[:, cc, 128:192], rhs=qaT[:, cc, tsl], start=(cc == 0), stop=(cc == c.QRC - 1)),
                             reads=["wqbf", ("qaT", cc)], writes=["pq1"])
                    S.dve(I("scalar_tensor_tensor", out=q32[:], in0=pq[1][0:64, :], scalar=scale, in1=rq[0:64, tsl], op0=ALU.mult, op1=ALU.mult),
                          reads=["pq1", ("rq", tg)], writes=["q32"])
                    rope((prot, t1, t2, cosS, sinS), q32[:], tg, qro[b][:], ("qro", b), "q32")
                    S.dma(I("dma_start", out=qTr[h, :, tsl], in_=qro[b][:]), reads=[("qro", b)], writes=[("qTr", h)])
            S.flush()
            if stop is not None and S.nphase >= stop:
                return nc

        with contextlib.ExitStack() as ph:
            S.alias = {"pk": ("bank", 2), "pvv": ("bank", 3)}
            S.alias.update({("psskv", i): ("bank", 2 + i) for i in range(NTG)})
            kvaT = sb("kvaT", [128, c.KVRC, S_], BF16, ph)
            rkv = sb("rkv", [128, S_], F32, ph)
            rkt = sb("rkt", [128, TT], F32, ph)
            cosS = sb("cosS2", [64, S_], F32, ph); sinS = sb("sinS2", [64, S_], F32, ph)
            S.dma(I("dma_start", out=cosS[:], in_=dr["cosT"]), writes=["cosS"])
            S.dma(I("dma_start", out=sinS[:], in_=dr["sinT"]), writes=["sinS"])
            for cc in range(c.KVRC):
                def cons(tg, p_, pk, cc=cc):
                    S.act(I("activation", out=kvaT[:, cc, tg * TG:(tg + 1) * TG], in_=p_, func=AF.Copy), reads=[pk], writes=[("kvaT", cc)])
                inproj(c.c_kva + cc * 128, 128, cons, nxt=((c.c_kva + (cc + 1) * 128, 128) if cc + 1 < c.KVRC else None))
            rms_rstd(ph, kvaT, c.KVRC, c.KVR, rkv, "kv", "kvaT", "rkv")
            prk = BK[7][:, 128:128 + TT]
            for tt in range(TT):
                S.pe(I("matmul", prk[:, tt:tt + 1], lhsT=rkv[:, tt * 128:(tt + 1) * 128], rhs=identf[:, 0:1], start=True, stop=True),
                     reads=[("rkv", tt * 128 // TG), "identf"], writes=["prk"])
            S.dve(I("tensor_copy", out=rkt[:], in_=prk[:]), reads=["prk"], writes=["rkt"])
            kr32 = sb("kr32", [64, S_], F32, ph); krT = sb("krTs", [64, S_], BF16, ph)
            prot = BK[6][:, 0:TG]
            t1 = sb("ropet1b", [128, TG], F32, ph); t2 = sb("ropet2b", [128, TG], F32, ph)

            def conskr(tg, p_, pk):
                tsl = slice(tg * TG, (tg + 1) * TG)
                S.act(I("activation", out=kr32[:, tsl], in_=p_, func=AF.Copy), reads=[pk], writes=[("kr32", tg)])
                rope((prot, t1, t2, cosS, sinS), kr32[:, tsl], tg, krT[:, tsl], ("krT", tg), ("kr32", tg))
                S.dma(I("dma_start", out=krTd[:, tsl], in_=krT[:, tsl]), reads=[("krT", tg)], writes=["krTd"])
            inproj(c.c_kr, 64, conskr)
            wkst = sb("wkst", [128, c.KVRC, 256], F32, ph); wkbf = sb("wkbf", [128, c.KVRC, 256], BF16, ph)
            pk_ = BK[2][:, 0:TG]
            pv_ = BK[3][:].rearrange("p (j t) -> p j t", t=128)
            ko = [sb(f"ko{i}", [128, TG], BF16, ph) for i in range(2)]
            vo = [sb(f"vo{i}", [128, 4, 128], BF16, ph) for i in range(2)]
            k_ = 0
            for h in range(c.HM):
                S.dma(I("dma_start", out=wkst[:], in_=dr["mla_wkv_b"][:, h * 256:(h + 1) * 256].rearrange("(c p) n -> p c n", p=128)), writes=["wkst"])
                for cc in range(c.KVRC):
                    S.pool(I("tensor_scalar", out=wkbf[:, cc, :], in0=wkst[:, cc, :], scalar1=pcol("gkv", cc), scalar2=None, op0=ALU.mult),
                           reads=["wkst", "pv"], writes=["wkbf"])
                for tg in range(NTG):
                    b = k_ % 2; k_ += 1
                    tsl = slice(tg * TG, (tg + 1) * TG)
                    for cc in range(c.KVRC):
                        S.pe(I("matmul", pk_[:], lhsT=wkbf[:, cc, 0:128], rhs=kvaT[:, cc, tsl], start=(cc == 0), stop=(cc == c.KVRC - 1)),
                             reads=["wkbf", ("kvaT", cc)], writes=["pk"])
                    S.dve(I("tensor_tensor", out=ko[b][:], in0=pk_[:], in1=rkv[:, tsl], op=ALU.mult), reads=["pk", ("rkv", tg)], writes=[("ko", b)])
                    S.dma(I("dma_start", out=kTn[h, :, tsl], in_=ko[b][:]), reads=[("ko", b)], writes=[("kTn", h)])
                for g in range(0, TT, 4):
                    n = min(4, TT - g)
                    b = k_ % 2; k_ += 1
                    for j in range(n):
                        tt = g + j
                        for cc in range(c.KVRC):
                            S.pe(I("matmul", pv_[:, j, :], lhsT=kvaT[:, cc, tt * 128:(tt + 1) * 128], rhs=wkbf[:, cc, 128:256],
                                   start=(cc == 0), stop=(cc == c.KVRC - 1)), reads=["wkbf", ("kvaT", cc)], writes=["pvv"])
                    for j in range(n):
                        S.act(I("activation", out=vo[b][:, j, :], in_=pv_[:, j, :], func=AF.Copy, scale=rkt[:, g + j:g + j + 1]),
                              reads=["pvv", "rkt"], writes=[("vo", b)])
                    S.dma(I("dma_start", out=Vtm[h, g * 128:(g + n) * 128, :].rearrange("(j p) d -> p j d", p=128), in_=vo[b][:, 0:n, :]),
                          reads=[("vo", b)], writes=[("Vtm", h)])
            S.flush()
            if stop is not None and S.nphase >= stop:
                return nc

        with contextlib.ExitStack() as ph:
            krT = sb("krA", [64, S_], BF16, ph)
            S.dma(I("dma_start", out=krT[:], in_=krTd), writes=["krA"])
            qn = [sb(f"qnA{i}", [128, S_], BF16, ph) for i in range(2)]
            qr = [sb(f"qrA{i}", [64, S_], BF16, ph) for i in range(2)]
            kn = [sb(f"knA{i}", [128, S_], BF16, ph) for i in range(2)]
            vt = [sb(f"vtA{i}", [128, TT, 128], BF16, ph) for i in range(2)]
            szm = [sb(f"szm{i}", [128, S_], BF16, ph) for i in range(2)]
            pS = [BK[j_][:, 0:TG] for j_ in (2, 3, 6)]
            pO = BK[4][:, 0:TG]; pSum = BK[5][:, 0:TG]
            pt = [sb(f"ptA{i}", [128, TG], BF16, ph) for i in range(4)]
            rs = sb("rsA", [128, TG], F32, ph); y32 = sb("y32A", [128, TG], F32, ph)
            yo = [sb(f"yoA{i}", [128, TG], BF16, ph) for i in range(2)]
            kk_ = 0; yy_ = 0
            def head_loads(h):
                sl = h % 2
                S.dma(I("dma_start", out=qn[sl][:], in_=qTn[h]), writes=[("qn", sl)])
                S.dma(I("dma_start", out=qr[sl][:], in_=qTr[h]), writes=[("qr", sl)])
                S.dma(I("dma_start", out=kn[sl][:], in_=kTn[h]), writes=[("kn", sl)])
                S.dma(I("dma_start", out=vt[sl][:], in_=Vtm[h].rearrange("(j p) d -> p j d", p=128)), writes=[("vt", sl)])
            head_loads(0)
            for h in range(c.HM):
                sl = h % 2
                if h + 1 < c.HM:
                    head_loads(h + 1)

                def consz(tg, p_, pk, sl=sl):
                    S.act(I("activation", out=szm[sl][:, tg * TG:(tg + 1) * TG], in_=p_, func=AF.Silu), reads=[pk], writes=[("szm", sl, tg)])
                inproj(c.c_zm + h * 128, 128, consz, nxt=((c.c_zm + (h + 1) * 128, 128) if h + 1 < c.HM else None))
                for qg in range(NTG):
                    qsl = slice(qg * TG, (qg + 1) * TG)
                    pend = []
                    for kt in range(TT):
                        b = kk_ % 3; p3 = kk_ % 4; kk_ += 1
                        ksl = slice(kt * 128, (kt + 1) * 128)
                        S.pe(I("matmul", pS[b][:], lhsT=kn[sl][:, ksl], rhs=qn[sl][:, qsl], start=True, stop=False),
                             reads=[("kn", sl), ("qn", sl)], writes=[("pS", b)])
                        S.pe(I("matmul", pS[b][:], lhsT=krT[:, ksl], rhs=qr[sl][:, qsl], start=False, stop=True),
                             reads=["krA", ("qr", sl)], writes=[("pS", b)])
                        S.act(I("activation", out=pt[p3][:], in_=pS[b][:], func=AF.Exp), reads=[("pS", b)], writes=[("pt", p3)])
                        pend.append((kt, p3))
                        todo = []
                        if len(pend) > 2:
                            todo.append(pend.pop(0))
                        if kt == TT - 1:
                            todo += pend
                            pend = []
                        for (kt_, p3_) in todo:
                            S.pe(I("matmul", pO[:], lhsT=vt[sl][:, kt_, :], rhs=pt[p3_][:], start=(kt_ == 0), stop=(kt_ == TT - 1)),
                                 reads=[("vt", sl), ("pt", p3_)], writes=["pO"])
                            S.pe(I("matmul", pSum[:], lhsT=ones128[:], rhs=pt[p3_][:], start=(kt_ == 0), stop=(kt_ == TT - 1)),
                                 reads=[("pt", p3_)], writes=["pSum"])
                    yb = yy_ % 2; yy_ += 1
                    S.dve(I("reciprocal", out=rs[:], in_=pSum[:]), reads=["pSum"], writes=["rsA"])
                    S.dve(I("tensor_tensor", out=y32[:], in0=pO[:], in1=rs[:], op=ALU.mult), reads=["pO", "rsA"], writes=["y32A"])
                    S.pool(I("tensor_tensor", out=yo[yb][:], in0=y32[:], in1=szm[sl][:, qsl], op=ALU.mult),
                           reads=["y32A", ("szm", sl, qg)], writes=[("yo", yb)])
                    S.dma(I("dma_start", out=ymz[h, :, qsl], in_=yo[yb][:]), reads=[("yo", yb)], writes=["ymz"])
            S.flush()
            if stop is not None and S.nphase >= stop:
                return nc

        def lerp(raw, P, mucol, out32, outkey, rkey):
            tmpa, tmpb = lerp_t
            S.pool(I("tensor_tensor", out=tmpa[0:P, :], in0=raw[0:P, 0:S_], in1=raw[0:P, 2:S_ + 2], op=ALU.add), reads=[rkey], writes=["lta"])
            S.dve(I("tensor_scalar", out=tmpb[0:P, :], in0=raw[0:P, 1:S_ + 1], scalar1=pvo[0:P, mucol:mucol + 1], scalar2=None, op0=ALU.mult),
                  reads=[rkey, "pvo"], writes=["ltb"])
            S.dve(I("scalar_tensor_tensor", out=out32, in0=tmpa[0:P, :], scalar=pvh[0:P, mucol:mucol + 1], in1=tmpb[0:P, :], op0=ALU.mult, op1=ALU.add),
                  reads=["lta", "ltb", "pvh"], writes=[outkey])

        with contextlib.ExitStack() as ph:
            raws = [sb(f"raw{i}", [128, S_ + 2], F32, ph) for i in range(2)]
            lerp_t = (sb("lta", [128, S_], F32, ph), sb("ltb", [128, S_], F32, ph))
            o32 = [sb(f"o32R{i}", [128, S_], F32, ph) for i in range(2)]
            kk32 = sb("kk32", [128, S_], F32, ph); sqk = sb("sqk", [128, S_], BF16, ph)
            nrm = sb("nrmk", [128, TG], F32, ph)
            lwo = sb("lwo", [96, S_], BF16, ph)
            szo = [sb(f"szo{i}", [128, S_], BF16, ph) for i in range(2)]
            pkk = BK[2][:, 0:TG]
            for i_ in range(2):
                S.pool(I("memset", raws[i_][:], 0.0), writes=[("raw", i_)])
            rw = [0]
            oi = 0
            for i in range(4):
                rb = rw[0] % 2; rw[0] += 1
                raw = raws[rb]

                def consl(tg, p_, pk, raw=raw, rb=rb):
                    S.act(I("activation", out=raw[0:96, 1 + tg * TG:1 + (tg + 1) * TG], in_=p_, func=AF.Copy), reads=[pk], writes=[("raw", rb)])
                inproj(c.c_lora + i * 96, 96, consl, nxt=((c.c_lora + (i + 1) * 96, 96) if i < 3 else (c.c_r, 128)))
                ob = oi % 2; oi += 1
                lerp(raw, 96, col["mul"] + i, o32[ob][0:96, :], ("o32", ob), ("raw", rb))
                S.act(I("activation", out=lwo[:], in_=o32[ob][0:96, :], func=(AF.Tanh if i < 2 else AF.Copy)), reads=[("o32", ob)], writes=["lwo"])
                S.dma(I("dma_start", out=lwT[i], in_=lwo[:]), reads=["lwo"], writes=["lwT"])
            for hp in range(c.NP):
                for j, cbase in enumerate([c.c_r, c.c_k, c.c_v]):
                    rb = rw[0] % 2; rw[0] += 1
                    raw = raws[rb]

                    def consr(tg, p_, pk, raw=raw, rb=rb):
                        S.act(I("activation", out=raw[:, 1 + tg * TG:1 + (tg + 1) * TG], in_=p_, func=AF.Copy), reads=[pk], writes=[("raw", rb)])
                    nx_ = ([c.c_r, c.c_k, c.c_v][j + 1] + hp * 128, 128) if j < 2 else (c.c_zr + hp * 128, 128)
                    inproj(cbase + hp * 128, 128, consr, nxt=nx_)
                    ob = oi % 2; oi += 1
                    lerp(raw, 128, col["mu"] + j * c.NP + hp, o32[ob][:], ("o32", ob), ("raw", rb))
                    S.dma(I("dma_start", out=rinT[j * c.NP + hp], in_=o32[ob][:]), reads=[("o32", ob)], writes=["rinT"])
                    if j == 1:
                        S.dve(I("tensor_scalar", out=kk32[:], in0=o32[ob][:], scalar1=pcol("k_k", hp), scalar2=None, op0=ALU.mult),
                              reads=[("o32", ob), "pv"], writes=["kk32"])
                        S.act(I("activation", out=sqk[:], in_=kk32[:], func=AF.Square), reads=["kk32"], writes=["sqk"])
                        for tg in range(NTG):
                            tsl = slice(tg * TG, (tg + 1) * TG)
                            S.pe(I("matmul", pkk[:], lhsT=blk1[:], rhs=sqk[:, tsl], start=True, stop=True), reads=["sqk"], writes=["pkk"])
                            S.act(I("activation", out=nrm[:], in_=pkk[:], func=AF.Sqrt), reads=["pkk"], writes=["nrmk"])
                            S.dve(I("tensor_scalar", out=nrm[:], in0=nrm[:], scalar1=1e-12, scalar2=None, op0=ALU.max), reads=["nrmk"], writes=["nrmk"])
                            S.dve(I("reciprocal", out=nrm[:], in_=nrm[:]), reads=["nrmk"], writes=["nrmk"])
                            S.dve(I("tensor_tensor", out=kk32[:, tsl], in0=kk32[:, tsl], in1=nrm[:], op=ALU.mult), reads=["kk32", "nrmk"], writes=["kk32"])
                        S.dma(I("dma_start", out=kknD[hp], in_=kk32[:]), reads=["kk32"], writes=["kknD"])
                zb = hp % 2

                def conszr(tg, p_, pk, zb=zb):
                    S.act(I("activation", out=szo[zb][:, tg * TG:(tg + 1) * TG], in_=p_, func=AF.Silu), reads=[pk], writes=[("szo", zb)])
                inproj(c.c_zr + hp * 128, 128, conszr, nxt=((c.c_r + (hp + 1) * 128, 128) if hp + 1 < c.NP else None))
                S.dma(I("dma_start", out=szrD[hp], in_=szo[zb][:]), reads=[("szo", zb)], writes=["szrD"])
            S.flush()
            if stop is not None and S.nphase >= stop:
                return nc

        RTG = min(256, S_); RNTG = S_ // RTG; RCK = RTG // 128
        stg["stack"].close()
        hstack.close()
        for hp0 in range(0, c.NP, 2):
            hps = [hp_ for hp_ in (hp0, hp0 + 1) if hp_ < c.NP]
            pp = contextlib.ExitStack()
            PP = []
            for pi, hp in enumerate(hps):
                PP.append(dict(w2st=sb("w2st", [96, 4, 128], F32, pp), w2bf=sb("w2bf", [96, 4, 128], BF16, pp),
                               Ytm=sb("Ytm", [128, TT, 128], F32, pp), bon=sb("bon", [128, S_], F32, pp)))
            with contextlib.ExitStack() as ph:
                S.alias = {"pHr": "pW", "pYr": "pW", "pz": "pA", "pTr": "pA", "pB": "pA"}
                S.shared = {"Ytm", "bon", "w2bf"}
                for pi, hp in enumerate(hps):
                    for i, nm in enumerate(["rwkv_w2_f", "rwkv_w2_b", "rwkv_a2_f", "rwkv_a2_b"]):
                        S.dma(I("dma_start", out=PP[pi]["w2st"][:, i, :], in_=dr[nm][:, hp * 128:(hp + 1) * 128]), writes=[("w2st", pi)])
                    S.dve(I("tensor_copy", out=PP[pi]["w2bf"][:], in_=PP[pi]["w2st"][:]), reads=[("w2st", pi)], writes=[("w2bf", pi)])
                    S.pool(I("memset", PP[pi]["Ytm"][:], 0.0), writes=[("Ytm", pi, t_) for t_ in range(TT)])
                    S.pool(I("memset", PP[pi]["bon"][:], 0.0), writes=[("bon", pi, t_) for t_ in range(RNTG)])

                def stream(d, pi, hp, sidx):
                    w2bf = PP[pi]["w2bf"]; Ytm = PP[pi]["Ytm"]; bon = PP[pi]["bon"]
                    f32n = ["r", "k", "v", "kkn", "sg", "a", "cum", "tmp", "ex", "ep", "en", "eh", "ka", "kf", "pre"]
                    T = {n: sb("R_" + n, [128, RTG], F32, ph) for n in f32n}
                    lwd = sb("lwd", [96, RTG], BF16, ph); lad = sb("lad", [96, RTG], BF16, ph)
                    AR = sb("AR", [128, RCK, 256], BF16, ph)
                    ZA = sb("ZA", [128, 128], BF16, ph); ZV = sb("ZV", [128, 128], BF16, ph)
                    BtZ = sb("BtZ", [128, 2, RTG], BF16, ph); KtZ = sb("KtZ", [128, 2, RTG], BF16, ph)
                    S.pool(I("memset", BtZ[:], 0.0), writes=["Bt"])
                    S.pool(I("memset", KtZ[:], 0.0), writes=["Kt"])
                    Bh = sb("Bh", [128, RTG], BF16, ph); Kh = sb("Kh", [128, RTG], BF16, ph)
                    vb = sb("vb", [128, RTG], BF16, ph); prb = sb("prb", [128, RTG], BF16, ph)
                    bA = BK[2 * sidx]; bB = bA; bC = BK[2 * sidx + 1]
                    pz = bA[:, 0:RTG]
                    pA = bA[:].rearrange("p (h t) -> p h t", t=256)
                    pB = bB[:].rearrange("p (h t) -> p h t", t=256)
                    pW = bC[:, 0:256].rearrange("p (h t) -> p h t", t=128)
                    pTr = bB[:].bitcast(BF16)[:, 0:512].rearrange("p (j t) -> p j t", t=128)
                    pY = bC[:, 256:384]
                    pH = bC[:, 384:512]
                    TM = sb("TM", [128, 4, 128], BF16, ph)
                    NA = sb("NA", [128, 2, 256], BF16, ph)
                    KA = sb("KA", [128, 2, 256], BF16, ph)
                    XX = [sb(f"XX{i}", [128, 2, 2, 128], BF16, ph) for i in range(2)]
                    W = [sb(f"W{i}", [128, 2, 128], BF16, ph) for i in range(2)]
                    RhT = sb("RhT", [128, 128], BF16, ph)
                    MT = sb("MTbd", [128, 128], BF16, ph)
                    H32 = sb("H32", [128, 128], F32, ph); Hb = sb("Hb", [128, 128], BF16, ph)
                    Htmp = sb("Htmp", [128, 128], F32, ph)
                    ptot = sb("ptot", [128, RCK], F32, ph)
                    Mm = Mf if d == 0 else Mb
                    Nm = Nf if d == 0 else Nb
                    w0c = pcol("w0_f" if d == 0 else "w0_b", hp)
                    a0c = pcol("a0_f" if d == 0 else "a0_b", hp)
                    S.dve(I("memset", H32[:], 0.0), writes=["H32"])
                    S.dve(I("memset", Hb[:], 0.0), writes=["Hb"])
                    tgs = range(RNTG) if d == 0 else range(RNTG - 1, -1, -1)
                    for tg in tgs:
                        tsl = slice(tg * RTG, (tg + 1) * RTG)
                        S.dma(I("dma_start", out=T["r"][:], in_=rinT[0 * c.NP + hp, :, tsl]), writes=["R_r"])
                        S.dma(I("dma_start", out=T["k"][:], in_=rinT[1 * c.NP + hp, :, tsl]), writes=["R_k"])
                        S.dma(I("dma_start", out=T["v"][:], in_=rinT[2 * c.NP + hp, :, tsl]), writes=["R_v"])
                        S.dma(I("dma_start", out=T["kkn"][:], in_=kknD[hp, :, tsl]), writes=["R_kkn"])
                        S.dma(I("dma_start", out=lwd[:], in_=lwT[d, :, tsl]), writes=["lwd"])
                        S.dma(I("dma_start", out=lad[:], in_=lwT[2 + d, :, tsl]), writes=["lad"])
                        S.pe(I("matmul", pz[:], lhsT=w2bf[:, d, :], rhs=lwd[:], start=True, stop=True), reads=[("w2bf", pi), "lwd"], writes=["pz"])
                        S.act(I("activation", out=T["sg"][:], in_=pz[:], func=AF.Sigmoid, bias=w0c), reads=["pz", "pv"], writes=["R_sg"])
                        S.pe(I("matmul", pz[:], lhsT=w2bf[:, 2 + d, :], rhs=lad[:], start=True, stop=True), reads=[("w2bf", pi), "lad"], writes=["pz"])
                        S.act(I("activation", out=T["a"][:], in_=pz[:], func=AF.Sigmoid, bias=a0c), reads=["pz", "pv"], writes=["R_a"])
                        S.dve(I("tensor_tensor_scan", out=T["pre"][:], data0=scanmask[:, 0:RTG], data1=T["sg"][:], initial=0.0, op0=ALU.mult, op1=ALU.add),
                              reads=["R_sg", "scanmask"], writes=["R_pre"])
                        pre3 = T["pre"][:].rearrange("p (c t) -> p c t", t=128)
                        cum3 = T["cum"][:].rearrange("p (c t) -> p c t", t=128)
                        tot_bc = bc_last(pre3[:, :, 127:128], 128)
                        S.dve(I("tensor_copy", out=ptot[:].rearrange("p (c o) -> p c o", o=1), in_=pre3[:, :, 127:128]), reads=["R_pre"], writes=["ptot"])
                        if d == 0:
                            S.pool(I("tensor_copy", out=T["cum"][:], in_=T["pre"][:]), reads=["R_pre"], writes=["R_cum"])
                        else:
                            S.dve(I("tensor_tensor", out=cum3, in0=tot_bc, in1=pre3, op=ALU.subtract), reads=["R_pre"], writes=["R_cum"])
                            S.dve(I("tensor_tensor", out=T["cum"][:], in0=T["cum"][:], in1=T["sg"][:], op=ALU.add), reads=["R_cum", "R_sg"], writes=["R_cum"])
                        S.pool(I("tensor_tensor", out=T["tmp"][:], in0=T["cum"][:], in1=T["sg"][:], op=ALU.subtract), reads=["R_cum", "R_sg"], writes=["R_tmp"])
                        S.act(I("activation", out=T["ex"][:], in_=T["tmp"][:], func=AF.Exp, scale=-C0), reads=["R_tmp"], writes=["R_ex"])
                        S.act(I("activation", out=T["ep"][:], in_=T["cum"][:], func=AF.Exp, scale=-C0), reads=["R_cum"], writes=["R_ep"])
                        S.act(I("activation", out=T["en"][:], in_=T["cum"][:], func=AF.Exp, scale=C0), reads=["R_cum"], writes=["R_en"])
                        S.dve(I("tensor_tensor", out=T["tmp"][:].rearrange("p (c t) -> p c t", t=128), in0=tot_bc, in1=cum3, op=ALU.subtract),
                              reads=["R_pre", "R_cum", "R_ex"], writes=["R_tmp"])
                        S.act(I("activation", out=T["eh"][:], in_=T["tmp"][:], func=AF.Exp, scale=-C0), reads=["R_tmp"], writes=["R_eh"])
                        S.pool(I("tensor_tensor", out=T["ka"][:], in0=T["kkn"][:], in1=T["a"][:], op=ALU.mult), reads=["R_kkn", "R_a"], writes=["R_ka"])
                        S.dve(I("tensor_scalar", out=T["kf"][:], in0=T["a"][:], scalar1=pcol("k_a", hp), scalar2=pvo[:, col["k_a"] + hp:col["k_a"] + hp + 1],
                                op0=ALU.mult, op1=ALU.add), reads=["R_a", "pv", "pvo"], writes=["R_kf"])
                        S.dve(I("tensor_tensor", out=T["kf"][:], in0=T["kf"][:], in1=T["k"][:], op=ALU.mult), reads=["R_kf", "R_k"], writes=["R_kf"])
                        S.dve(I("scalar_tensor_tensor", out=prb[:], in0=T["kf"][:], scalar=pcol("r_k", hp), in1=T["r"][:], op0=ALU.mult, op1=ALU.mult),
                              reads=["R_kf", "R_r"], writes=["prb"])
                        S.pe(I("matmul", pz[:], lhsT=blk1[:], rhs=prb[:], start=True, stop=True), reads=["prb"], writes=["pz"])
                        S.dve(I("tensor_tensor", out=T["pre"][:], in0=pz[:], in1=T["v"][:], op=ALU.mult), reads=["pz", "R_v"], writes=["R_pre"])
                        S.pool(I("tensor_tensor", out=bon[:, tsl], in0=bon[:, tsl], in1=T["pre"][:], op=ALU.add), reads=["R_pre", ("bon", pi, tg)], writes=[("bon", pi, tg)])
                        S.dve(I("scalar_tensor_tensor", out=AR[:, :, 0:128], in0=T["kkn"][:].rearrange("p (c t) -> p c t", t=128), scalar=-1.0,
                                in1=T["ex"][:].rearrange("p (c t) -> p c t", t=128), op0=ALU.mult, op1=ALU.mult),
                              reads=["R_kkn", "R_ex"], writes=["AR"])
                        S.pool(I("tensor_tensor", out=AR[:, :, 128:256], in0=T["r"][:].rearrange("p (c t) -> p c t", t=128),
                                 in1=T["ep"][:].rearrange("p (c t) -> p c t", t=128), op=ALU.mult), reads=["R_r", "R_ep"], writes=["AR"])
                        for h in range(2):
                            hs = slice(h * 64, (h + 1) * 64)
                            S.dve(I("tensor_tensor", out=BtZ[hs, h, :], in0=T["ka"][hs, :], in1=T["en"][hs, :], op=ALU.mult), reads=["R_ka", "R_en"], writes=["Bt"])
                            S.pool(I("tensor_tensor", out=KtZ[hs, h, :], in0=T["kf"][hs, :], in1=T["en"][hs, :], op=ALU.mult), reads=["R_kf", "R_en"], writes=["Kt"])
                        S.dve(I("tensor_tensor", out=Bh[:], in0=T["ka"][:], in1=T["eh"][:], op=ALU.mult), reads=["R_ka", "R_eh"], writes=["Bh"])
                        S.pool(I("tensor_tensor", out=Kh[:], in0=T["kf"][:], in1=T["eh"][:], op=ALU.mult), reads=["R_kf", "R_eh"], writes=["Kh"])
                        S.act(I("activation", out=vb[:], in_=T["v"][:], func=AF.Copy), reads=["R_v"], writes=["vb"])
                        S.act(I("activation", out=ptot[:], in_=ptot[:], func=AF.Exp, scale=-C0), reads=["ptot"], writes=["ptot"])
                        cks = range(RCK) if d == 0 else range(RCK - 1, -1, -1)
                        for ck in cks:
                            csl = slice(ck * 128, (ck + 1) * 128)
                            tt = tg * RCK + ck
                            for j, src in enumerate([AR[:, ck, 0:128], Bh[:, csl], Kh[:, csl], vb[:, csl]]):
                                S.pe(I("transpose", out=pTr[:, j, :], in_=src, identity=ident[:]), reads=["AR", "Bh", "Kh", "vb", "ident"], writes=["pTr"])
                            S.act(I("activation", out=TM[:], in_=pTr[:], func=AF.Copy), reads=["pTr"], writes=["TM"])
                            for h in range(2):
                                S.pe(I("matmul", pA[:, h, :], lhsT=BtZ[:, h, csl], rhs=AR[:, ck, :], start=True, stop=True), reads=["Bt", "AR"], writes=["pA"])
                            S.dve(I("tensor_tensor", out=NA[:], in0=pA[:], in1=Mm[:], op=ALU.mult), reads=["pA"], writes=["NA"])
                            for h in range(2):
                                S.pe(I("matmul", pB[:, h, :], lhsT=KtZ[:, h, csl], rhs=AR[:, ck, :], start=True, stop=True), reads=["Kt", "AR"], writes=["pB"])
                            S.dve(I("tensor_tensor", out=KA[:], in0=pB[:], in1=Mm[:], op=ALU.mult), reads=["pB"], writes=["KA"])
                            for h in range(2):
                                hs = slice(h * 64, (h + 1) * 64)
                                S.pe(I("matmul", pW[:, h, :], lhsT=AR[:, ck, 0:128], rhs=BtZ[:, h, csl], start=True, stop=True), reads=["AR", "Bt"], writes=["pW"])
                            S.dve(I("tensor_tensor", out=XX[0][:, :, 0, :], in0=pW[:], in1=Nm[:], op=ALU.mult), reads=["pW"], writes=[("XX", 0)])
                            S.act(I("activation", out=XX[0][:, :, 1, :], in_=NA[:, :, 0:128], func=AF.Copy), reads=["NA"], writes=[("XX", 0)])
                            for h in range(2):
                                S.pe(I("matmul", pW[:, h, 64:128], lhsT=KA[:, h, 0:128], rhs=TM[:, 3, h * 64:(h + 1) * 64], start=True, stop=True),
                                     reads=["KA", "TM", ("XX", 0)], writes=["pW"])
                            S.act(I("activation", out=W[0][:, :, 64:128], in_=pW[:, :, 64:128], func=AF.Copy), reads=["pW"], writes=[("W", 0)])
                            S.dve(I("tensor_copy", out=W[0][:, :, 0:64], in_=TM[:, 0, :].rearrange("p (h j) -> p h j", j=64)), reads=["TM"], writes=[("W", 0)])
                            nlev = 7
                            for lv in range(nlev):
                                a_ = lv % 2; b_ = 1 - a_
                                for h in range(2):
                                    S.pe(I("matmul", pW[:, h, :], lhsT=ident[:], rhs=W[a_][:, h, :], start=True, stop=False), reads=[("W", a_), "ident"], writes=["pW"])
                                    S.pe(I("matmul", pW[:, h, :], lhsT=XX[a_][:, h, 1, :], rhs=W[a_][:, h, :], start=False, stop=True),
                                         reads=[("W", a_), ("XX", a_)], writes=["pW"])
                                S.act(I("activation", out=W[b_][:], in_=pW[:], func=AF.Copy), reads=["pW"], writes=[("W", b_)])
                                if lv < nlev - 1:
                                    pX = pA if lv % 2 == 0 else pB
                                    pXk = "pA" if lv % 2 == 0 else "pB"
                                    for h in range(2):
                                        S.pe(I("matmul", pX[:, h, 0:128], lhsT=XX[a_][:, h, 1, :], rhs=XX[a_][:, h, 0, :], start=True, stop=True),
                                             reads=[("XX", a_)], writes=[pXk])
                                        S.pe(I("matmul", pX[:, h, 128:256], lhsT=XX[a_][:, h, 0, :], rhs=XX[a_][:, h, 1, :], start=True, stop=True),
                                             reads=[("XX", a_)], writes=[pXk])
                                    S.dve(I("tensor_copy", out=XX[b_][:].rearrange("p h x t -> p h (x t)"), in_=pX[:]), reads=[pXk], writes=[("XX", b_)])
                            Z = W[nlev % 2]
                            zk = ("W", nlev % 2)
                            S.act(I("activation", out=ZA[:].rearrange("p (h j) -> p h j", j=64), in_=Z[:, :, 0:64], func=AF.Copy), reads=[zk], writes=["ZA"])
                            S.dve(I("tensor_copy", out=ZV[:].rearrange("p (h j) -> p h j", j=64), in_=Z[:, :, 64:128]), reads=[zk], writes=["ZV"])
                            for h in range(2):
                                S.pe(I("matmul", bA[:, h * 128:(h + 1) * 128], lhsT=ZA[:], rhs=NA[:, h, 128:256], start=True, stop=True), reads=["ZA", "NA"], writes=["pA"])
                            S.dve(I("tensor_tensor", out=RhT[0:64, :], in0=bA[0:64, 0:128], in1=AR[0:64, ck, 128:256], op=ALU.add), reads=["pA", "AR"], writes=["RhT"])
                            S.dve(I("tensor_tensor", out=RhT[64:128, :], in0=bA[64:128, 128:256], in1=AR[64:128, ck, 128:256], op=ALU.add), reads=["pA", "AR"], writes=["RhT"])
                            S.pe(I("matmul", pB[:, 0, 0:128], lhsT=ZA[:], rhs=TM[:, 1, :], start=True, stop=True), reads=["ZA", "TM"], writes=["pB"])
                            S.dve(I("tensor_tensor", out=MT[:], in0=pB[:, 0, 0:128], in1=bdmask[:], op=ALU.mult), reads=["pB", "bdmask"], writes=["MTbd"])
                            for h in range(2):
                                hs = slice(h * 64, (h + 1) * 64)
                                S.pe(I("matmul", pY[:, hs], lhsT=NA[:, h, 128:256], rhs=Z[:, h, 64:128], start=True, stop=False), reads=["NA", zk], writes=["pYr"])
                                S.pe(I("matmul", pY[:, hs], lhsT=KA[:, h, 128:256], rhs=TM[:, 3, hs], start=False, stop=False), reads=["KA", "TM"], writes=["pYr"])
                                S.pe(I("matmul", pY[:, hs], lhsT=RhT[:], rhs=Hb[:, hs], start=False, stop=True), reads=["RhT", "Hb"], writes=["pYr"])
                            S.dve(I("tensor_tensor", out=Ytm[:, tt, :], in0=pY[:], in1=Ytm[:, tt, :], op=ALU.add), reads=["pYr", ("Ytm", pi, tt)], writes=[("Ytm", pi, tt)])
                            S.pe(I("matmul", pH[:], lhsT=MT[:], rhs=Hb[:], start=True, stop=False), reads=["MTbd", "Hb"], writes=["pHr"])
                            S.pe(I("matmul", pH[:], lhsT=TM[:, 1, :], rhs=ZV[:], start=False, stop=False), reads=["TM", "ZV"], writes=["pHr"])
                            S.pe(I("matmul", pH[:], lhsT=TM[:, 2, :], rhs=TM[:, 3, :], start=False, stop=True), reads=["TM"], writes=["pHr"])
                            S.dve(I("scalar_tensor_tensor", out=Htmp[:], in0=H32[:], scalar=ptot[:, ck:ck + 1], in1=pH[:], op0=ALU.mult, op1=ALU.add),
                                  reads=["H32", "ptot", "pHr"], writes=["Htmp"])
                            S.dve(I("tensor_tensor", out=H32[:], in0=Htmp[:], in1=bdmask[:], op=ALU.mult), reads=["Htmp", "bdmask"], writes=["H32"])
                            S.act(I("activation", out=Hb[:], in_=H32[:], func=AF.Copy), reads=["H32"], writes=["Hb"])

                lists = []
                for pi, hp in enumerate(hps):
                    for d in range(2):
                        base_ops = S.ops
                        S.ops = []
                        S.suffix = (pi, d)
                        stream(d, pi, hp, 2 * pi + d)
                        S.suffix = None
                        lists.append(S.ops)
                        S.ops = base_ops
                for i_ in range(max(len(l_) for l_ in lists)):
                    for l_ in lists:
                        if i_ < len(l_):
                            o_ = l_[i_]
                            o_.idx = len(S.ops)
                            S.ops.append(o_)
                S.flush()
                if stop is not None and S.nphase >= stop:
                    pp.close()
                    return nc
            for pi, hp in enumerate(hps):
              Ytm = PP[pi]["Ytm"]; bon = PP[pi]["bon"]
              with contextlib.ExitStack() as ph:
                szr = sb("szrS", [128, S_], BF16, ph)
                S.dma(I("dma_start", out=szr[:], in_=szrD[hp]), writes=["szrS"])
                pTr = BK[6][:].bitcast(BF16)[:, 0:512].rearrange("p (j t) -> p j t", t=128)
                Y4 = Ytm[:].rearrange("p t (h i) -> p (t h) i", i=64)
                NG = 2 * TT
                s1 = sb("gn_s1", [128, NG], F32, ph); s2 = sb("gn_s2", [128, NG], F32, ph)
                ysq = sb("gn_sq", [128, TT, 128], F32, ph)
                yn = sb("gn_yn", [128, TT, 128], BF16, ph)
                S.dve(I("tensor_reduce", out=s1[:], in_=Y4, axis=AX.X, op=ALU.add), reads=[("Ytm", t_) for t_ in range(TT)], writes=["gn_s1"])
                S.act(I("activation", out=ysq[:], in_=Ytm[:], func=AF.Square), reads=[("Ytm", t_) for t_ in range(TT)], writes=["gn_sq"])
                S.dve(I("tensor_reduce", out=s2[:], in_=ysq[:].rearrange("p t (h i) -> p (t h) i", i=64), axis=AX.X, op=ALU.add), reads=["gn_sq"], writes=["gn_s2"])
                S.dve(I("tensor_scalar", out=s1[:], in0=s1[:], scalar1=1.0 / 64, scalar2=None, op0=ALU.mult), reads=["gn_s1"], writes=["gn_s1"])
                S.dve(I("tensor_scalar", out=s2[:], in0=s2[:], scalar1=1.0 / 64, scalar2=GN_EPS, op0=ALU.mult, op1=ALU.add), reads=["gn_s2"], writes=["gn_s2"])
                msq = sb("gn_msq", [128, NG], F32, ph)
                S.dve(I("tensor_tensor", out=msq[:], in0=s1[:], in1=s1[:], op=ALU.mult), reads=["gn_s1"], writes=["gn_msq"])
                S.dve(I("tensor_tensor", out=s2[:], in0=s2[:], in1=msq[:], op=ALU.subtract), reads=["gn_s2", "gn_msq"], writes=["gn_s2"])
                S.act(I("activation", out=s2[:], in_=s2[:], func=AF.Sqrt), reads=["gn_s2"], writes=["gn_s2"])
                S.dve(I("reciprocal", out=s2[:], in_=s2[:]), reads=["gn_s2"], writes=["gn_s2"])
                s1b = bc_last(s1[:].rearrange("p (g o) -> p g o", o=1), 64)
                s2b = bc_last(s2[:].rearrange("p (g o) -> p g o", o=1), 64)
                ysq4 = ysq[:].rearrange("p t (h i) -> p (t h) i", i=64)
                S.dve(I("tensor_tensor", out=ysq4, in0=Y4, in1=s1b, op=ALU.subtract), reads=[("Ytm", t_) for t_ in range(TT)] + ["gn_s1", "gn_sq"], writes=["gn_sq"])
                S.dve(I("tensor_tensor", out=yn[:].rearrange("p t (h i) -> p (t h) i", i=64), in0=ysq4, in1=s2b, op=ALU.mult), reads=["gn_sq", "gn_s2"], writes=["gn_yn"])
                yfm = sb("gn_yfm", [128, 4, 128], F32, ph)
                yob = [sb(f"gn_yo{i}", [128, 4, 128], BF16, ph) for i in range(2)]
                gi = 0
                for g in range(0, TT, 4):
                    n = min(4, TT - g)
                    for j in range(n):
                        S.pe(I("transpose", out=pTr[:, j, :], in_=yn[:, g + j, :], identity=ident[:]), reads=["gn_yn", "ident"], writes=["pTr"])
                    fsl = slice(g * 128, (g + n) * 128)
                    S.act(I("activation", out=yfm[:, 0:n, :], in_=pTr[:, 0:n, :], func=AF.Identity, scale=pcol("gn_g", hp), bias=pcol("gn_b", hp)),
                          reads=["pTr", "pv"], writes=["gn_yfm"])
                    S.dve(I("tensor_tensor", out=yfm[:, 0:n, :], in0=yfm[:, 0:n, :], in1=bon[:, fsl].rearrange("p (j t) -> p j t", t=128), op=ALU.add),
                          reads=["gn_yfm"] + [("bon", t_) for t_ in range(NTG)], writes=["gn_yfm"])
                    ob = gi % 2; gi += 1
                    S.dve(I("tensor_tensor", out=yob[ob][:, 0:n, :], in0=yfm[:, 0:n, :], in1=szr[:, fsl].rearrange("p (j t) -> p j t", t=128), op=ALU.mult),
                          reads=["gn_yfm", "szrS"], writes=[("gn_yo", ob)])
                    S.dma(I("dma_start", out=yrz[hp, :, fsl].rearrange("p (j t) -> p j t", t=128), in_=yob[ob][:, 0:n, :]), reads=[("gn_yo", ob)], writes=["yrz"])
                S.flush()
                if stop is not None and S.nphase >= stop:
                            return nc
            pp.close()
        hstack2 = contextlib.ExitStack()
        es.enter_context(hstack2)
        hT2 = sb("hT2", [128, DC, S_], BF16, hstack2)
        hts["hT"] = hT2
        alloc_stg()
        with contextlib.ExitStack() as ph:
            S.dma(I("dma_start", out=hT2[:], in_=hTd.rearrange("c p s -> p c s")), writes=["hT"])
            ymzS = sb("ymzS", [128, c.MWC, S_], BF16, ph); yrzS = sb("yrzS", [128, c.NP, S_], BF16, ph)
            S.dma(I("dma_start", out=ymzS[:], in_=ymz.rearrange("c p s -> p c s")), writes=["ymzS"])
            S.dma(I("dma_start", out=yrzS[:], in_=yrz.rearrange("c p s -> p c s")), writes=["yrzS"])
            wmst = sb("wmst", [128, c.MWC, 128], F32, ph); wmbf = sb("wmbf", [128, c.MWC, 128], BF16, ph)
            wrst = sb("wrst", [128, c.NP, 128], F32, ph); wrbf = sb("wrbf", [128, c.NP, 128], BF16, ph)
            sgm = sb("sgm", [128, S_], F32, ph); sgr = sb("sgr", [128, S_], F32, ph)
            pu = [BK[2 + i][:, 0:TG] for i in range(2)]
            t1f = sb("t1f", [128, TG], F32, ph)
            mo = [sb(f"mo{i}", [128, TG], BF16, ph) for i in range(2)]
            mi = 0
            for fc in range(DC):
                fsl = slice(fc * 128, (fc + 1) * 128)
                S.dma(I("dma_start", out=wmst[:], in_=dr["w_br_mla"][:, fsl].rearrange("(c p) n -> p c n", p=128)), writes=["wmst"])
                S.dve(I("tensor_copy", out=wmbf[:], in_=wmst[:]), reads=["wmst"], writes=["wmbf"])
                S.dma(I("dma_start", out=wrst[:], in_=dr["w_br_rwkv"][:, fsl].rearrange("(c p) n -> p c n", p=128)), writes=["wrst"])
                S.dve(I("tensor_copy", out=wrbf[:], in_=wrst[:]), reads=["wrst"], writes=["wrbf"])

                def consgm(tg, p_, pk):
                    S.act(I("activation", out=sgm[:, tg * TG:(tg + 1) * TG], in_=p_, func=AF.Sigmoid), reads=[pk], writes=[("sgm", tg)])

                def consgr(tg, p_, pk):
                    S.act(I("activation", out=sgr[:, tg * TG:(tg + 1) * TG], in_=p_, func=AF.Sigmoid), reads=[pk], writes=[("sgr", tg)])
                inproj(c.c_gm + fc * 128, 128, consgm, nxt=(c.c_gr + fc * 128, 128))
                inproj(c.c_gr + fc * 128, 128, consgr, nxt=((c.c_gm + (fc + 1) * 128, 128) if fc + 1 < DC else None))
                for tg in range(NTG):
                    tsl = slice(tg * TG, (tg + 1) * TG)
                    for wc in range(c.MWC):
                        S.pe(I("matmul", pu[0][:], lhsT=wmbf[:, wc, :], rhs=ymzS[:, wc, tsl], start=(wc == 0), stop=(wc == c.MWC - 1)),
                             reads=["wmbf", "ymzS"], writes=["pu0"])
                    for wc in range(c.NP):
                        S.pe(I("matmul", pu[1][:], lhsT=wrbf[:, wc, :], rhs=yrzS[:, wc, tsl], start=(wc == 0), stop=(wc == c.NP - 1)),
                             reads=["wrbf", "yrzS"], writes=["pu1"])
                    S.dve(I("tensor_tensor", out=t1f[:], in0=pu[0][:], in1=sgm[:, tsl], op=ALU.mult), reads=["pu0", ("sgm", tg)], writes=["t1f"])
                    mb = mi % 2; mi += 1
                    S.dve(I("tensor_tensor", out=mo[mb][:], in0=pu[1][:], in1=sgr[:, tsl], op=ALU.mult), reads=["pu1", ("sgr", tg)], writes=[("mo", mb)])
                    S.pool(I("tensor_tensor", out=mo[mb][:], in0=mo[mb][:], in1=t1f[:], op=ALU.add), reads=[("mo", mb), "t1f"], writes=[("mo", mb)])
                    S.dma(I("dma_start", out=mrg[fc, :, tsl], in_=mo[mb][:]), reads=[("mo", mb)], writes=["mrg"])
            S.flush()
            if stop is not None and S.nphase >= stop:
                return nc
        stg["stack"].close()
        hstack2.close()

        with contextlib.ExitStack() as ph:
            CB = min(512, D); NCB = D // CB
            wob = sb("wob", [128, DC, D], BF16, ph)
            wos = [sb(f"wos{i}", [128, D], F32, ph) for i in range(2)]
            for fc in range(DC):
                S.dma(I("dma_start", out=wos[fc % 2][:], in_=dr["w_out"][fc * 128:(fc + 1) * 128, :]), writes=[("wos", fc % 2)])
                (S.dve if fc % 2 == 0 else S.pool)(I("tensor_copy", out=wob[:, fc, :], in_=wos[fc % 2][:]), reads=[("wos", fc % 2)], writes=["wob"])
            gpb = sb("gpb", [128, D], F32, ph)
            S.dma(I("dma_start", out=gpb[:], in_=bass.AP(dr["g_post"].tensor, 0, [[0, 128], [1, D]])), writes=["gpb"])
            mT = [sb(f"mT{i}", [128, DC, 128], BF16, ph) for i in range(2)]
            xt = [sb(f"xtF{i}", [128, D], F32, ph) for i in range(2)]
            o32 = sb("o32F", [128, D], F32, ph)
            junk = sb("junkF", [128, CB], BF16, ph)
            ssp = sb("ssp", [128, TT, NCB + 4], F32, ph)
            po = [BK[i][:, 0:CB] for i in range(2)]
            ot = [sb(f"otF{i}", [128, D], F32, ph) for i in range(2)]
            pi_ = 0
            for tt in range(TT):
                sl = tt % 2
                tsl = slice(tt * 128, (tt + 1) * 128)
                S.dma(I("dma_start", out=mT[sl][:], in_=mrg[:, :, tsl].rearrange("c p t -> p c t")), writes=[("mT", sl)])
                S.dma(I("dma_start", out=xt[sl][:], in_=dr["x"][tsl, :]), writes=[("xtF", sl)])
                for cb in range(NCB):
                    b = pi_ % 2; pi_ += 1
                    for fc in range(DC):
                        S.pe(I("matmul", po[b][:], lhsT=mT[sl][:, fc, :], rhs=wob[:, fc, cb * CB:(cb + 1) * CB], start=(fc == 0), stop=(fc == DC - 1)),
                             reads=[("mT", sl), "wob"], writes=[("po", b)])
                    S.act(I("activation", out=o32[:, cb * CB:(cb + 1) * CB], in_=po[b][:], func=AF.Copy), reads=[("po", b)], writes=[("o32F", cb)])
                    S.act(I("activation", out=junk[:], in_=po[b][:], func=AF.Square, accum_out=ssp[:, tt, cb:cb + 1]), reads=[("po", b)], writes=["junkF", ("ssp", tt)])
                sc = NCB
                S.dve(I("tensor_reduce", out=ssp[:, tt, sc:sc + 1], in_=ssp[:, tt, 0:NCB], axis=AX.X, op=ALU.add), reads=[("ssp", tt)], writes=[("ssp", tt)])
                S.dve(I("tensor_scalar", out=ssp[:, tt, sc + 1:sc + 2], in0=ssp[:, tt, sc:sc + 1], scalar1=1.0 / D, scalar2=NORM_EPS, op0=ALU.mult, op1=ALU.add),
                      reads=[("ssp", tt)], writes=[("ssp", tt)])
                S.act(I("activation", out=ssp[:, tt, sc + 2:sc + 3], in_=ssp[:, tt, sc + 1:sc + 2], func=AF.Sqrt), reads=[("ssp", tt)], writes=[("ssp", tt)])
                S.dve(I("reciprocal", out=ssp[:, tt, sc + 3:sc + 4], in_=ssp[:, tt, sc + 2:sc + 3]), reads=[("ssp", tt)], writes=[("ssp", tt)])
                S.dve(I("scalar_tensor_tensor", out=ot[sl][:], in0=o32[:], scalar=ssp[:, tt, sc + 3:sc + 4], in1=gpb[:], op0=ALU.mult, op1=ALU.mult),
                      reads=[("o32F", cb_) for cb_ in range(NCB)] + [("ssp", tt), "gpb"], writes=[("otF", sl)])
                S.pool(I("tensor_tensor", out=ot[sl][:], in0=ot[sl][:], in1=xt[sl][:], op=ALU.add), reads=[("otF", sl), ("xtF", sl)], writes=[("otF", sl)])
                S.dma(I("dma_start", out=out[tsl, :], in_=ot[sl][:]), reads=[("otF", sl)], writes=["out"], final=True)
            S.flush(last=True)
    return nc


def rope_tables(S):
    pos = np.arange(S, dtype=np.float32)
    inv = np.power(np.float32(10000.0), -np.arange(0, 64, 2, dtype=np.float32) / np.float32(64)).astype(np.float32)
    ang = (pos[:, None] * inv[None, :]).astype(np.float32)
    cosT = np.concatenate([np.cos(ang), np.cos(ang)], axis=1).T.astype(np.float32)
    sinT = np.concatenate([np.sin(ang), np.sin(ang)], axis=1).T.astype(np.float32)
    return np.ascontiguousarray(cosT), np.ascontiguousarray(sinT)


def make_in_maps(c, inputs):
    B = inputs["x"].shape[0]
    cosT, sinT = rope_tables(c.S)
    shared = {}
    for n in list(MAT_SHAPES(c).keys()) + list(VEC_LENS(c).keys()):
        a = np.ascontiguousarray(np.asarray(inputs[n], dtype=np.float32))
        if n == "rwkv_r_k":
            a = a.reshape(-1)
        shared[n] = a
    shared["cosT"] = cosT; shared["sinT"] = sinT
    maps = []
    for b in range(B):
        m = dict(shared)
        m["x"] = np.ascontiguousarray(np.asarray(inputs["x"][b], dtype=np.float32))
        maps.append(m)
    return maps


def kernel(**inputs):
    c = Cfg()
    nc = build(c)
    maps = make_in_maps(c, inputs)
    res = run_bass_kernel_spmd(nc, maps, core_ids=list(range(8)))
    return np.stack([np.asarray(r["out"], dtype=np.float32) for r in res.results], axis=0)
```
